# Optimizing a Trainium2 kernel written in Bass

```python
import math
import jax, jax.numpy as jnp
from jax import lax
import numpy as np

D_MODEL = 1024
BATCH = 8
SEQ = 4096
DEPTH = 1

HEAD_DIM = 64
POOL_WIDTH = D_MODEL // 4
POOL_WINDOWS = (2, 4, 8, 16)
POOL_GROUPS = len(POOL_WINDOWS)
POOL_GROUP_DIM = POOL_WIDTH // POOL_GROUPS
ATTN_WIDTH = D_MODEL - POOL_WIDTH
ATTN_HEADS = ATTN_WIDTH // HEAD_DIM
DILATION_CFG = ((128, 1), (512, 4), (2048, 16))
N_DIL = len(DILATION_CFG)
HEADS_PER_DIL = ATTN_HEADS // N_DIL
ATTN_OUT_WIDTH = HEADS_PER_DIL * HEAD_DIM
IN_PROJ_WIDTH = POOL_WIDTH + 3 * ATTN_WIDTH
OUT_PROJ_WIDTH = POOL_WIDTH + ATTN_OUT_WIDTH
ROT_DIM = HEAD_DIM // 4
ROPE_THETA = 500000.0
D_FF = 11 * D_MODEL // 4
CONV_WIDTH = 3
BLOCK = 128
NORM_EPS = 1e-6
N_MOD = 6

kernel_name = "hybrid_pool_dilated_attn_convffn_block"


def rms_norm(x, g):
    x32 = x.astype(jnp.float32)
    y = x32 * lax.rsqrt(jnp.mean(x32 * x32, axis=-1, keepdims=True) + NORM_EPS)
    return (y * g.astype(jnp.float32)).astype(x.dtype)


def partial_rope(t, cos, sin):
    half = ROT_DIM // 2
    t1 = t[..., :half]
    t2 = t[..., half:ROT_DIM]
    return jnp.concatenate(
        [t1 * cos - t2 * sin, t2 * cos + t1 * sin, t[..., ROT_DIM:]], axis=-1)


def causal_multiscale_pool(u, w_pool, b_pool, pool_scale):
    B, S, _ = u.shape
    u32 = u.astype(jnp.float32).reshape(B, S, POOL_GROUPS, POOL_GROUP_DIM)
    cs = jnp.cumsum(u32, axis=1)
    t = jnp.arange(S, dtype=jnp.float32)
    means = []
    for gi, w in enumerate(POOL_WINDOWS):
        csg = cs[:, :, gi]
        lagged = jnp.pad(csg[:, :S - w], ((0, 0), (w, 0), (0, 0)))
        count = jnp.minimum(t + 1.0, float(w))
        means.append((csg - lagged) / count[None, :, None])
    mixed = jnp.stack(means, axis=2) - u32
    y = jnp.einsum('bsgc,gcd->bsgd', mixed, w_pool.astype(jnp.float32)) + b_pool.astype(jnp.float32)
    return (y.reshape(B, S, POOL_WIDTH) * pool_scale.astype(jnp.float32)).astype(u.dtype)


def dilated_window_attention(q, k, v, window, dilation):
    B, S, H, hd = q.shape
    span = window // dilation
    assert span <= BLOCK
    chunk = dilation * BLOCK
    s_pad = -(-S // chunk) * chunk
    padw = ((0, 0), (0, s_pad - S), (0, 0), (0, 0))
    q, k, v = (jnp.pad(a.astype(jnp.float32), padw) for a in (q, k, v))
    nb = s_pad // chunk
    qb, kb, vb = (a.reshape(B, nb, BLOCK, dilation, H, hd) for a in (q, k, v))

    def with_prev(a):
        prev = jnp.pad(a[:, :-1], ((0, 0), (1, 0), (0, 0), (0, 0), (0, 0), (0, 0)))
        return jnp.concatenate([prev, a], axis=2)

    kk, vv = with_prev(kb), with_prev(vb)
    scores = jnp.einsum('bnirhd,bnjrhd->bnrhij', qb, kk)
    i = jnp.arange(BLOCK)[:, None]
    j = jnp.arange(2 * BLOCK)[None, :]
    n = jnp.arange(nb)[:, None, None]
    dist = BLOCK + i - j
    valid = (dist >= 0) & (dist <= span) & (n * BLOCK + j - BLOCK >= 0)
    scores = jnp.where(valid[None, :, None, None], scores, -jnp.inf)
    m = jnp.max(scores, axis=-1, keepdims=True)
    p = jnp.exp(scores - m)
    den = jnp.sum(p, axis=-1)
    o = jnp.einsum('bnrhij,bnjrhd->bnirhd', p, vv)
    den_t = jnp.moveaxis(den, -1, 2)
    lse_t = jnp.moveaxis(m[..., 0] + jnp.log(den), -1, 2)
    o = (o / den_t[..., None]).reshape(B, s_pad, H, hd)[:, :S]
    lse = lse_t.reshape(B, s_pad, H)[:, :S]
    return o, lse


def mixing_sublayer(h, cos, sin, w_in, w_pool, b_pool, pool_scale, w_out):
    B, S, _ = h.shape
    proj = h @ w_in
    u_pool = proj[..., :POOL_WIDTH]
    qkv = proj[..., POOL_WIDTH:].reshape(B, S, 3, ATTN_HEADS, HEAD_DIM)
    q = partial_rope(qkv[:, :, 0], cos, sin) * (HEAD_DIM ** -0.5)
    k = partial_rope(qkv[:, :, 1], cos, sin)
    v = qkv[:, :, 2]
    outs, lses = [], []
    for g, (window, dilation) in enumerate(DILATION_CFG):
        sl = slice(g * HEADS_PER_DIL, (g + 1) * HEADS_PER_DIL)
        o, l = dilated_window_attention(q[:, :, sl], k[:, :, sl], v[:, :, sl], window, dilation)
        outs.append(o)
        lses.append(l)
    alpha = jax.nn.softmax(jnp.stack(lses, axis=0), axis=0)
    attn = jnp.sum(alpha[..., None] * jnp.stack(outs, axis=0), axis=0)
    attn = attn.reshape(B, S, ATTN_OUT_WIDTH).astype(h.dtype)
    pool = causal_multiscale_pool(u_pool, w_pool, b_pool, pool_scale)
    return jnp.concatenate([pool, attn], axis=-1) @ w_out


def conv_ffn(h, w_up, conv_w, conv_b, w_down):
    S = h.shape[1]
    up = h @ w_up
    gate, val = up[..., :D_FF], up[..., D_FF:]
    gp = jnp.pad(gate, ((0, 0), (CONV_WIDTH - 1, 0), (0, 0)))
    gate = gp[:, 0:S] * conv_w[0] + gp[:, 1:S + 1] * conv_w[1] + gp[:, 2:S + 2] * conv_w[2] + conv_b
    return (jax.nn.gelu(gate, approximate=True) * val) @ w_down


def setup_inputs(seed: int = 0) -> dict:
    key = jax.random.key(seed)
    ks = jax.random.split(key, 20)
    f32 = jnp.float32
    nrm = lambda k, shape, s: jax.random.normal(k, shape, f32) * s
    L = DEPTH
    return {
        "x": nrm(ks[0], (BATCH, SEQ, D_MODEL), 1.0),
        "c": nrm(ks[1], (BATCH, D_MODEL), 1.0),
        "positions": jnp.broadcast_to(jnp.arange(SEQ, dtype=jnp.int32), (BATCH, SEQ)),
        "w_ada": nrm(ks[2], (L, D_MODEL, N_MOD * D_MODEL), D_MODEL ** -0.5),
        "b_ada": nrm(ks[3], (L, N_MOD * D_MODEL), 0.02),
        "g_pre_mix": 1.0 + nrm(ks[4], (L, D_MODEL), 0.05),
        "g_post_mix": 1.0 + nrm(ks[5], (L, D_MODEL), 0.05),
        "g_pre_ffn": 1.0 + nrm(ks[6], (L, D_MODEL), 0.05),
        "g_post_ffn": 1.0 + nrm(ks[7], (L, D_MODEL), 0.05),
        "w_in": nrm(ks[8], (L, D_MODEL, IN_PROJ_WIDTH), D_MODEL ** -0.5),
        "w_pool": nrm(ks[9], (L, POOL_GROUPS, POOL_GROUP_DIM, POOL_GROUP_DIM), POOL_GROUP_DIM ** -0.5),
        "b_pool": nrm(ks[10], (L, POOL_GROUPS, POOL_GROUP_DIM), 0.02),
        "pool_scale": 1.0 + nrm(ks[11], (L, POOL_WIDTH), 0.05),
        "w_out": nrm(ks[12], (L, OUT_PROJ_WIDTH, D_MODEL), OUT_PROJ_WIDTH ** -0.5),
        "w_up": nrm(ks[13], (L, D_MODEL, 2 * D_FF), D_MODEL ** -0.5),
        "conv_w": nrm(ks[14], (L, CONV_WIDTH, D_FF), CONV_WIDTH ** -0.5),
        "conv_b": nrm(ks[15], (L, D_FF), 0.02),
        "w_down": nrm(ks[16], (L, D_FF, D_MODEL), D_FF ** -0.5),
    }


def reference(x, c, positions, w_ada, b_ada, g_pre_mix, g_post_mix, g_pre_ffn, g_post_ffn,
              w_in, w_pool, b_pool, pool_scale, w_out, w_up, conv_w, conv_b, w_down):
    inv_freq = ROPE_THETA ** (-jnp.arange(0, ROT_DIM, 2, dtype=jnp.float32) / ROT_DIM)
    ang = positions.astype(jnp.float32)[..., None] * inv_freq
    cos = jnp.cos(ang)[:, :, None, :].astype(x.dtype)
    sin = jnp.sin(ang)[:, :, None, :].astype(x.dtype)
    c_act = jax.nn.silu(c)
    for l in range(DEPTH):
        mod = c_act @ w_ada[l] + b_ada[l]
        sh_m, sc_m, gt_m, sh_f, sc_f, gt_f = (t[:, None, :] for t in jnp.split(mod, N_MOD, axis=-1))
        h = rms_norm(x, g_pre_mix[l]) * (1.0 + sc_m) + sh_m
        y = mixing_sublayer(h, cos, sin, w_in[l], w_pool[l], b_pool[l], pool_scale[l], w_out[l])
        x = x + gt_m * rms_norm(y, g_post_mix[l])
        h = rms_norm(x, g_pre_ffn[l]) * (1.0 + sc_f) + sh_f
        y = conv_ffn(h, w_up[l], conv_w[l], conv_b[l], w_down[l])
        x = x + gt_f * rms_norm(y, g_post_ffn[l])
    return x
```

```python
import contextlib
import numpy as np
import ml_dtypes
import concourse.bass as bass
import concourse.mybir as mybir
from concourse.bass_utils import run_bass_kernel_spmd

F32 = mybir.dt.float32
BF16 = mybir.dt.bfloat16
I32 = mybir.dt.int32
AF = mybir.ActivationFunctionType
ALU = mybir.AluOpType
bf = ml_dtypes.bfloat16

S = 4096
D = 1024
NCH = 8
CH = 512
DFF = 2816
NJ = 22
EPS = 1e-6
ENG = ["pe", "act", "dve", "pool", "sp"]
C1 = 6.28125
C2 = float(2 * np.pi - 6.28125)
PI_LO = float(np.nextafter(np.float32(np.pi), np.float32(0)))
TWO_PI = float(2 * np.pi)
ATT_LIMIT = None
ATT_SUB = 9
ATT_HH = (0, 1)


class Buf:
    def __init__(self, name, init=()):
        self.name = name
        self.w = None
        self.r = list(init)


class Sched:
    def __init__(self):
        self.ops = {e: [] for e in ENG}
        self.cnt = {e: 0 for e in ENG}
        self.dma_i = {"sp": 0, "pool": 0}
        self.ndma = {"sp": 24, "pool": 12}
        self.dma_uses = {}

    def op(self, eng, fn, reads=(), writes=(), dma=False):
        if getattr(self, 'stopped', False):
            return None
        waits = set()
        for b in reads:
            if b.w is not None:
                waits.add(b.w)
        for b in writes:
            if b.w is not None:
                waits.add(b.w)
            waits.update(b.r)
        if dma:
            idx = self.dma_i[eng] % self.ndma[eng]
            self.dma_i[eng] += 1
            key = (eng, idx)
            prev = self.dma_uses.get(key, 0)
            self.dma_uses[key] = prev + 1
            tok = ("dma", eng, idx, 16 * (prev + 1))
            if prev > 0:
                waits.add(("dma", eng, idx, 16 * prev))
        else:
            self.cnt[eng] += 1
            tok = ("eng", eng, 0, self.cnt[eng])
        if eng == "pe":
            waits = {w for w in waits if not (w[0] == "eng" and w[1] == "pe")}
        self.ops[eng].append((fn, waits, tok, dma))
        for b in reads:
            b.r.append(tok)
        for b in writes:
            b.w = tok
            b.r = []
        return tok

    def all_tokens(self):
        toks = []
        for e in ENG:
            if self.cnt[e] > 0:
                toks.append(("eng", e, 0, self.cnt[e]))
        for (q, idx), n in self.dma_uses.items():
            toks.append(("dma", q, idx, 16 * n))
        return toks


class _Stop(Exception):
    pass


def build(stage=99):
    nc = bass.Bass("TRN2", target_bir_lowering=False)
    sc = Sched()
    dbg_bufs = []

    def dump(ap, bufs, row0, ncols, nparts=128):
        b = Buf("dbg")
        dbg_bufs.append(b)
        sc.op("pool", lambda e: e.dma_start(out=out_d[row0:row0 + nparts, 0:ncols], in_=ap), reads=list(bufs), writes=[b], dma=True)

    def checkpoint(k, dumps=()):
        if stage == k:
            for i, (ap, bufs, ncols, nparts) in enumerate(dumps):
                dump(ap, bufs, i * 128, ncols, nparts)
            sc.op("sp", None, reads=dbg_bufs)
            sc.stopped = True

    def din(name, shape, dt):
        return nc.dram_tensor(name, list(shape), dt, kind="ExternalInput").ap()

    x_d = din("x", [S, D], F32)
    cT_d = din("cT", [128, 8], F32)
    pos_d = din("pos", [128, S], I32)
    wada_d = din("w_ada", [D, 6 * D], F32)
    bA_d = din("bA", [128, 4, 8], F32)
    bG_d = din("bG", [128, 2, D], F32)
    gpre_d = din("gpre", [128, 2, 8], F32)
    gpost_d = din("gpost", [128, 2, D], F32)
    win_d = din("w_in", [D, 2560], F32)
    wout_d = din("w_out", [512, D], F32)
    wup_d = din("w_up", [D, 2 * DFF], F32)
    wdn_d = din("w_down", [DFF, D], F32)
    wpool_d = din("w_pool", [4, 64, 64], F32)
    bpool_d = din("bpool", [128, 2], F32)
    pscale_d = din("pscale", [128, 2], F32)
    convw_d = din("convw", [128, NJ, 3], F32)
    convb_d = din("convb", [128, NJ], F32)
    ident_d = din("ident", [128, 128], BF16)
    swap_d = din("swapm", [128, 128], BF16)
    invf_d = din("invf", [128, 1], F32)
    mask_d = din("mask", [128, 2, 2, 128], BF16)
    poolc_d = din("poolc", [128, 2 + 32], F32)
    out_d = nc.dram_tensor("out", [S, D], F32, kind="ExternalOutput").ap()
    x1_d = nc.dram_tensor("x1s", [S, D], F32, kind="Internal").ap()
    tab_d = nc.dram_tensor("tabs", [2, 128, S], F32, kind="Internal").ap()

    stack = contextlib.ExitStack()

    def sb(name, shape, dt):
        return stack.enter_context(nc.sbuf_tensor("sb_" + name, list(shape), dt))

    def ps(name, shape, dt):
        return stack.enter_context(nc.psum_tensor(name, list(shape), dt))

    with stack:
        sem_eng = {e: stack.enter_context(nc.semaphore("s_" + e)) for e in ["pe", "act", "dve", "pool"]}
        sem_dma = {}
        for q in ["sp", "pool"]:
            for i in range(sc.ndma[q]):
                sem_dma[(q, i)] = stack.enter_context(nc.semaphore(f"d_{q}{i}"))

        psT = ps("psT", [128, 8, 128], BF16)
        psBig = ps("psBig", [128, 5, 512], F32)
        accbig = ps("accbig", [128, 2, 512], F32)
        psR = [psBig[:, i, :] for i in range(5)]
        acc = [accbig[:, 0, :], accbig[:, 1, :]]
        b_psT = [Buf("psT0"), Buf("psT1")]
        b_psR = [Buf(f"psR{i}") for i in range(5)]
        b_acc = [Buf("acc0"), Buf("acc1")]
        psY_a = (psBig[:, 3:5, :].rearrange("p a b -> p (a b)"), [b_psR[3], b_psR[4]])
        psY_b = (accbig[:, :, :].rearrange("p a b -> p (a b)"), [b_acc[0], b_acc[1]])
        psY_c = (psBig[:, 1:3, :].rearrange("p a b -> p (a b)"), [b_psR[1], b_psR[2]])
        ring = {"i": 0, "banks": [0, 1, 2, 3, 4]}

        def next_ring():
            bk = ring["banks"]
            i = bk[ring["i"] % len(bk)]
            ring["i"] += 1
            return psR[i], b_psR[i]

        ident = sb("ident", [128, 128], BF16)
        modv = sb("modv", [128, 4, 8], F32)
        Gt = sb("Gt", [128, 2, D], F32)
        ss_t = sb("ss_t", [128, 8], F32)
        rs_t = sb("rs_t", [128, 8], F32)
        ln_t = sb("ln_t", [128, 8], F32)
        b_ident, b_modv, b_Gt = Buf("ident"), Buf("modv"), Buf("Gt")
        sink = {}
        b_ss = [Buf(f"ss{i}") for i in range(8)]
        b_rs = [Buf(f"rs{i}") for i in range(8)]
        b_ln = [Buf(f"ln{i}") for i in range(8)]
        sc.op("sp", lambda e: e.dma_start(out=ident[:], in_=ident_d), writes=[b_ident], dma=True)

        epsb = sb("epsb", [128, 1], F32)
        b_eps = Buf("eps")
        sc.op("dve", lambda e: e.memset(epsb[:], EPS), writes=[b_eps])

        def rstd_ops(i):
            sc.op("act", lambda e: e.activation(out=ln_t[:, i:i + 1], in_=ss_t[:, i:i + 1], func=AF.Ln,
                                                bias=epsb[:, 0:1], scale=1.0 / D),
                  reads=[b_ss[i], b_eps], writes=[b_ln[i]])
            sc.op("act", lambda e: e.activation(out=rs_t[:, i:i + 1], in_=ln_t[:, i:i + 1], func=AF.Exp, scale=-0.5),
                  reads=[b_ln[i]], writes=[b_rs[i]])

        def sumsq(i, src_ap, src_bufs):
            sc.op("dve", lambda e: e.memset(ss_t[:, i:i + 1], 0.0), writes=[b_ss[i]])
            jk, bjk = sink["ap"], sink["buf"]
            sc.op("act", lambda e: e.activation(out=jk, in_=src_ap, func=AF.Square, accum_out=ss_t[:, i:i + 1]),
                  reads=list(src_bufs), writes=list(bjk) + [b_ss[i]])

        st2 = contextlib.ExitStack()

        def sb2(name, shape, dt):
            return st2.enter_context(nc.sbuf_tensor("s2_" + name, list(shape), dt))

        with st2:
            cTt = sb2("cTt", [128, 8], F32)
            cactf = sb2("cactf", [128, 8], F32)
            cact = sb2("cact", [128, 8], BF16)
            ones_t = sb2("ones_t", [128, 128], BF16)
            crep = sb2("crep", [128, 8, 128], BF16)
            bA = sb2("bA", [128, 4, 8], F32)
            gpre = sb2("gpre", [128, 2, 8], F32)
            bG = sb2("bG", [128, 2, D], F32)
            gpost = sb2("gpost", [128, 2, D], F32)
            wst = [sb2(f"wst{i}", [128, 8, D], BF16) for i in range(2)]
            invf = sb2("invf", [128, 1], F32)
            posi = sb2("posi", [128, 2048], I32)
            ang = sb2("ang", [128, 2048], F32)
            a2 = sb2("a2", [128, 2048], F32)
            ki = sb2("ki", [128, 2048], I32)
            kf = sb2("kf", [128, 2048], F32)
            rr = sb2("rr", [128, 2048], F32)
            tb = sb2("tb", [128, 2048], F32)
            b_cT, b_cactf, b_cact, b_ones, b_crep = Buf("cT"), Buf("cactf"), Buf("cact"), Buf("ones"), Buf("crep")
            b_bA, b_gpre, b_bG, b_gpost, b_invf = Buf("bA"), Buf("gpre"), Buf("bG"), Buf("gpost"), Buf("invf")
            b_wst = [Buf("wst0"), Buf("wst1")]
            b_posi, b_ang, b_a2, b_ki, b_kf, b_rr, b_tb = (Buf(n) for n in ["posi", "ang", "a2", "ki", "kf", "rr", "tb"])
            b_tabd = [Buf("tabd0"), Buf("tabd1")]

            sc.op("sp", lambda e: e.dma_start(out=cTt[:], in_=cT_d), writes=[b_cT], dma=True)
            sc.op("sp", lambda e: e.dma_start(out=bA[:], in_=bA_d), writes=[b_bA], dma=True)
            sc.op("sp", lambda e: e.dma_start(out=gpre[:], in_=gpre_d), writes=[b_gpre], dma=True)
            sc.op("sp", lambda e: e.dma_start(out=invf[:], in_=invf_d), writes=[b_invf], dma=True)
            wada_v = wada_d.rearrange("(k p) n -> p k n", p=128)

            def load_seg(seg):
                t, bt = wst[seg % 2], b_wst[seg % 2]
                sc.op("pool", lambda e: e.dma_start(out=t[:], in_=wada_v[:, :, seg * D:(seg + 1) * D]),
                      writes=[bt], dma=True)

            load_seg(0)
            load_seg(1)
            sc.op("sp", lambda e: e.dma_start(out=bG[:], in_=bG_d), writes=[b_bG], dma=True)
            sc.op("sp", lambda e: e.dma_start(out=gpost[:], in_=gpost_d), writes=[b_gpost], dma=True)

            sc.op("act", lambda e: e.activation(out=cactf[:], in_=cTt[:], func=AF.Silu), reads=[b_cT], writes=[b_cactf])
            sc.op("dve", lambda e: e.tensor_copy(out=cact[:], in_=cactf[:]), reads=[b_cactf], writes=[b_cact])
            sc.op("dve", lambda e: e.memset(ones_t[:], 1.0), writes=[b_ones])
            for kc in range(8):
                sc.op("dve", lambda e, kc=kc: e.tensor_scalar(out=crep[:, kc, :], in0=ones_t[:], scalar1=cactf[:, kc:kc + 1],
                                                              scalar2=None, op0=ALU.mult),
                      reads=[b_ones, b_cactf], writes=[b_crep])

            psm = psR[0]
            segs_pp = {0: 0, 1: 1, 3: 2, 4: 3}

            def seg_pp(seg):
                t, bt = wst[seg % 2], b_wst[seg % 2]
                g = segs_pp[seg]

                def fn(e):
                    ins = None
                    for mg in range(8):
                        for kc in range(8):
                            ins = e.matmul(psm[:, g * 8 + mg:g * 8 + mg + 1], lhsT=t[:, kc, mg * 128:(mg + 1) * 128],
                                           rhs=cact[:, kc:kc + 1], start=(kc == 0), stop=(kc == 7))
                    return ins
                sc.op("pe", fn, reads=[bt, b_cact], writes=[b_psR[0]])

            def seg_gate(seg):
                t, bt = wst[seg % 2], b_wst[seg % 2]
                gi = 0 if seg == 2 else 1

                def fn(e):
                    ins = None
                    for nh in range(2):
                        for kc in range(8):
                            ins = e.matmul(psY_a[0][:, nh * 512:(nh + 1) * 512], lhsT=crep[:, kc, :],
                                           rhs=t[:, kc, nh * 512:(nh + 1) * 512], start=(kc == 0), stop=(kc == 7))
                    return ins
                sc.op("pe", fn, reads=[bt, b_crep], writes=psY_a[1])
                sc.op("dve", lambda e: e.tensor_tensor(out=Gt[:, gi, :], in0=psY_a[0], in1=bG[:, gi, :], op=ALU.add),
                      reads=psY_a[1] + [b_bG], writes=[b_Gt])
                sc.op("dve", lambda e: e.tensor_tensor(out=Gt[:, gi, :], in0=Gt[:, gi, :], in1=gpost[:, gi, :], op=ALU.mult),
                      reads=[b_gpost, b_Gt], writes=[b_Gt])

            seg_pp(0)
            load_seg(2)
            seg_pp(1)
            load_seg(3)
            seg_gate(2)
            load_seg(4)
            seg_pp(3)
            load_seg(5)
            seg_pp(4)
            seg_gate(5)
            sc.op("dve", lambda e: e.tensor_tensor(out=modv[:], in0=psm[:, 0:32].rearrange("p (a b) -> p a b", a=4),
                                                   in1=bA[:], op=ALU.add),
                  reads=[b_psR[0], b_bA], writes=[b_modv])
            for (a, gi) in ((1, 0), (3, 1)):
                sc.op("dve", lambda e, a=a, gi=gi: e.scalar_tensor_tensor(out=modv[:, a, :], in0=modv[:, a, :], scalar=1.0,
                                                                          in1=gpre[:, gi, :], op0=ALU.add, op1=ALU.mult),
                      reads=[b_modv, b_gpre], writes=[b_modv])

            for half in range(2):
                cs = slice(half * 2048, (half + 1) * 2048)
                sc.op("sp", lambda e, cs=cs: e.dma_start(out=posi[:], in_=pos_d[:, cs]), writes=[b_posi], dma=True)
                sc.op("dve", lambda e: e.tensor_copy(out=ang[:], in_=posi[:]), reads=[b_posi], writes=[b_ang])
                sc.op("dve", lambda e: e.tensor_scalar(out=ang[:], in0=ang[:], scalar1=invf[:, 0:1], scalar2=None, op0=ALU.mult),
                      reads=[b_ang, b_invf], writes=[b_ang])
                for v in range(2):
                    off = float(np.pi / 2) if v == 0 else 0.0
                    sc.op("dve", lambda e, off=off: e.tensor_scalar(out=a2[:], in0=ang[:], scalar1=off, scalar2=None, op0=ALU.add),
                          reads=[b_ang], writes=[b_a2])
                    sc.op("dve", lambda e: e.tensor_scalar(out=ki[:], in0=a2[:], scalar1=float(1.0 / (2 * np.pi)), scalar2=None,
                                                           op0=ALU.mult), reads=[b_a2], writes=[b_ki])
                    sc.op("dve", lambda e: e.tensor_copy(out=kf[:], in_=ki[:]), reads=[b_ki], writes=[b_kf])
                    sc.op("dve", lambda e: e.scalar_tensor_tensor(out=rr[:], in0=kf[:], scalar=-C1, in1=a2[:], op0=ALU.mult,
                                                                  op1=ALU.add), reads=[b_kf, b_a2], writes=[b_rr])
                    sc.op("dve", lambda e: e.scalar_tensor_tensor(out=rr[:], in0=kf[:], scalar=-C2, in1=rr[:], op0=ALU.mult,
                                                                  op1=ALU.add), reads=[b_kf, b_rr], writes=[b_rr])
                    for (cmpop, thr, corr) in ((ALU.is_gt, PI_LO, -TWO_PI), (ALU.is_lt, -PI_LO, TWO_PI)):
                        sc.op("dve", lambda e, cmpop=cmpop, thr=thr, corr=corr: e.tensor_scalar(out=kf[:], in0=rr[:], scalar1=thr, scalar2=corr,
                                                                                              op0=cmpop, op1=ALU.mult),
                              reads=[b_rr], writes=[b_kf])
                        sc.op("dve", lambda e: e.tensor_tensor(out=rr[:], in0=rr[:], in1=kf[:], op=ALU.add), reads=[b_rr, b_kf], writes=[b_rr])
                    sc.op("dve", lambda e: e.tensor_scalar(out=rr[:], in0=rr[:], scalar1=-PI_LO, scalar2=PI_LO, op0=ALU.max, op1=ALU.min),
                          reads=[b_rr], writes=[b_rr])
                    sc.op("act", lambda e: e.activation(out=tb[:], in_=rr[:], func=AF.Sin), reads=[b_rr], writes=[b_tb])
                    sc.op("sp", lambda e, v=v, cs=cs: e.dma_start(out=tab_d[v, :, cs], in_=tb[:]), reads=[b_tb],
                          writes=[b_tabd[v]], dma=True)
        checkpoint(1, [(modv[:].rearrange('p a b -> p (a b)'), [b_modv], 32, 128), (Gt[:, 0, :], [b_Gt], 1024, 128), (Gt[:, 1, :], [b_Gt], 1024, 128)])
        fence0 = sc.all_tokens()

        stM = contextlib.ExitStack()

        def sbM(name, shape, dt):
            return stM.enter_context(nc.sbuf_tensor("sm_" + name, list(shape), dt))

        with stM:
            def mb(name):
                return Buf(name, init=fence0)
            win = sbM("win", [128, 8, 2560], BF16)
            junk = sbM("junk", [128, D], BF16)
            sink["ap"], sink["buf"] = junk[:], [mb("junk")]
            wout_p = sbM("wout_p", [128, 2, D], BF16)
            wout_a = sbM("wout_a", [128, 4, D], BF16)
            wpbd = sbM("wpbd", [128, 2, 128], BF16)
            swapm = sbM("swapm", [128, 128], BF16)
            maskt = sbM("maskt", [128, 2, 2, 128], BF16)
            poolc = sbM("poolc", [128, 34], F32)
            bpool = sbM("bpool", [128, 2], F32)
            pscale = sbM("pscale", [128, 2], F32)
            KT01 = sbM("KT01", [128, 4, 1024], BF16)
            KT2 = sbM("KT2", [128, 2, S], BF16)
            NV = 48 * 256
            Vall = sbM("Vall", [128, 192 * 65], BF16)
            V3 = Vall[:, :].rearrange("p (s d) -> p s d", d=65)
            onesf = sbM("onesf", [65, 64], F32)
            QT = sbM("QT", [128, 2, 6, CH], BF16)
            xt = sbM("xt", [128, 2, D], F32)
            xn = sbM("xn", [128, 2, D], BF16)
            hT = sbM("hT", [128, 8, CH], BF16)
            tabC = sbM("tabC", [128, CH], F32)
            tabS = sbM("tabS", [128, CH], F32)
            ubuf = sbM("ubuf", [128, 2, 528], F32)
            s2b = sbM("s2b", [128, 2, 528], F32)
            s4b = sbM("s4b", [128, 2, 528], F32)
            s8b = sbM("s8b", [128, 528], F32)
            s16b = sbM("s16b", [128, 528], F32)
            mixed = sbM("mixed", [128, 2, CH], BF16)
            catP = sbM("catP", [128, 2, CH], BF16)
            qb = sbM("qb", [128, 2, CH], BF16)
            t1 = sbM("t1", [128, 2, CH], F32)
            t2 = sbM("t2", [128, 2, CH], F32)
            PT = sbM("PT", [128, 3, 1024], BF16)
            rden = sbM("rden", [65, 2, CH], F32)
            attT = sbM("attT", [128, 4, CH], BF16)
            xo = sbM("xo", [128, 2, D], F32)
            tmpy = sbM("tmpy", [128, D], F32)

            b_win = [mb(f"win{k}") for k in range(8)]
            b_woutp, b_wouta, b_wpbd, b_swap, b_mask = mb("woutp"), mb("wouta"), mb("wpbd"), mb("swap"), mb("mask")
            b_poolc, b_bpool, b_pscale = mb("poolc"), mb("bpool"), mb("pscale")
            b_KT01 = [mb("KT01_0"), mb("KT01_1")]
            b_KT2 = [mb(f"KT2_{c}") for c in range(8)]
            b_V01 = [mb("V01_0"), mb("V01_1")]
            b_V2 = [mb(f"V2_{c}") for c in range(8)]
            b_Vones = mb("Vones")
            b_QT = mb("QT")
            b_xt = [mb("xt0"), mb("xt1")]
            b_xn = [mb("xn0"), mb("xn1")]
            b_hT = mb("hT")
            b_tabC, b_tabS = mb("tabC"), mb("tabS")
            b_ubuf, b_s2, b_s4, b_s8, b_s16, b_mixed, b_catP = (mb(n) for n in ["ubuf", "s2", "s4", "s8", "s16", "mixed", "catP"])
            b_qb = [mb("qb0"), mb("qb1")]
            b_t1 = [mb("t1_0"), mb("t1_1")]
            b_t2 = [mb("t2_0"), mb("t2_1")]
            b_PT = [mb(f"PT{i}") for i in range(3)]
            b_rden = [mb("rden0"), mb("rden1")]
            b_rdrow = [mb("rdrow0"), mb("rdrow1")]
            b_attT = mb("attT")
            b_xo = [mb("xo0"), mb("xo1")]
            b_tmpy = mb("tmpy")
            b_x1d = [Buf(f"x1d{t}") for t in range(32)]

            win_v = win_d.rearrange("(k p) n -> p k n", p=128)
            for kc in range(8):
                sc.op("pool", lambda e, kc=kc: e.dma_start(out=win[:, kc, :], in_=win_v[:, kc, :]), writes=[b_win[kc]], dma=True)
            sc.op("sp", lambda e: e.dma_start(out=swapm[:], in_=swap_d), writes=[b_swap], dma=True)
            sc.op("sp", lambda e: e.dma_start(out=maskt[:], in_=mask_d), writes=[b_mask], dma=True)
            sc.op("sp", lambda e: e.dma_start(out=poolc[:], in_=poolc_d), writes=[b_poolc], dma=True)
            sc.op("sp", lambda e: e.dma_start(out=bpool[:], in_=bpool_d), writes=[b_bpool], dma=True)
            sc.op("sp", lambda e: e.dma_start(out=pscale[:], in_=pscale_d), writes=[b_pscale], dma=True)
            sc.op("dve", lambda e: e.memset(wpbd[:], 0.0), writes=[b_wpbd])
            for g in range(4):
                sc.op("pool", lambda e, g=g: e.dma_start(out=wpbd[(g % 2) * 64:(g % 2) * 64 + 64, g // 2, (g % 2) * 64:(g % 2) * 64 + 64],
                                                          in_=wpool_d[g]), writes=[b_wpbd], dma=True)
            sc.op("pool", lambda e: e.dma_start(out=wout_p[:], in_=wout_d[0:256, :].rearrange("(k p) n -> p k n", p=128)),
                  writes=[b_woutp], dma=True)
            sc.op("dve", lambda e: e.memset(wout_a[64:128, :, :], 0.0), writes=[b_wouta])
            sc.op("pool", lambda e: e.memset(attT[64:128, :, :], 0.0), writes=[b_attT])
            sc.op("pool", lambda e: e.dma_start(out=wout_a[0:64, :, :], in_=wout_d[256:512, :].rearrange("(k p) n -> p k n", p=64)),
                  writes=[b_wouta], dma=True)
            sc.op("dve", lambda e: e.memset(V3[:, :, 64:65], 1.0), writes=[b_Vones])
            b_onesf = mb("onesf")
            sc.op("dve", lambda e: e.memset(onesf[:], 1.0), writes=[b_onesf])
            sc.op("dve", lambda e: e.memset(ubuf[:, :, 0:16], 0.0), writes=[b_ubuf])
            sc.op("pool", lambda e: e.memset(QT[64:128, 0, :, :], 0.0), writes=[b_QT])
            sc.op("pool", lambda e: e.memset(QT[0:64, 1, :, :], 0.0), writes=[b_QT])

            def vslot_ap(slot, head, kparts):
                o = (slot * 4 + head) * 64
                return bass.AP(Vall.tensor if hasattr(Vall, "tensor") else Vall, o, [[NV + 64, kparts], [NV - o, 2], [1, 64]])

            def front(c):
                front_load(c, 0)
                front_load(c, 1)
                for tt in range(4):
                    front_norm(c, tt)
                    if tt + 2 < 4:
                        front_load(c, tt + 2)
                    front_trans(c, tt)

            def front_load(c, tt):
                t = c * 4 + tt
                r = t % 2
                sc.op("sp", lambda e, t=t, r=r: e.dma_start(out=xt[:, r, :], in_=x_d[t * 128:(t + 1) * 128, :]),
                      writes=[b_xt[r]], dma=True)

            def front_norm(c, tt):
                if True:
                    t = c * 4 + tt
                    r = t % 2
                    sumsq(tt, xt[:, r, :], [b_xt[r]])
                    rstd_ops(tt)
                    sc.op("dve", lambda e, r=r, tt=tt: e.tensor_scalar(out=xn[:, r, :], in0=xt[:, r, :], scalar1=rs_t[:, tt:tt + 1],
                                                                        scalar2=None, op0=ALU.mult),
                          reads=[b_xt[r], b_rs[tt]], writes=[b_xn[r]])

            def front_trans(c, tt):
                if True:
                    t = c * 4 + tt
                    r = t % 2

                    def fnT(e, r=r):
                        ins = None
                        for kc in range(8):
                            ins = e.transpose(psT[:, kc, :], xn[:, r, kc * 128:(kc + 1) * 128], ident[:])
                        return ins
                    sc.op("pe", fnT, reads=[b_xn[r], b_ident], writes=[b_psT[0], b_psT[1]])
                    for kc in range(8):
                        eng = "dve" if kc % 2 == 0 else "act"
                        if eng == "dve":
                            sc.op("dve", lambda e, kc=kc, tt=tt: e.tensor_scalar(
                                out=hT[:, kc, tt * 128:(tt + 1) * 128], in0=psT[:, kc, :], scalar1=modv[:, 1, kc:kc + 1],
                                scalar2=modv[:, 0, kc:kc + 1], op0=ALU.mult, op1=ALU.add),
                                reads=[b_psT[kc // 4], b_modv], writes=[b_hT])
                        else:
                            sc.op("act", lambda e, kc=kc, tt=tt: e.activation(
                                out=hT[:, kc, tt * 128:(tt + 1) * 128], in_=psT[:, kc, :], func=AF.Identity,
                                bias=modv[:, 0, kc:kc + 1], scale=modv[:, 1, kc:kc + 1]),
                                reads=[b_psT[kc // 4], b_modv], writes=[b_hT])

            def proj(c):
                par = c % 2
                n16, cc = c // 4, c % 4
                sc.op("sp", lambda e: e.dma_start(out=tabC[:], in_=tab_d[0, :, c * CH:(c + 1) * CH]),
                      reads=[b_tabd[0]], writes=[b_tabC], dma=True)
                sc.op("sp", lambda e: e.dma_start(out=tabS[:], in_=tab_d[1, :, c * CH:(c + 1) * CH]),
                      reads=[b_tabd[1]], writes=[b_tabS], dma=True)

                def mm_fm(pt, col0):
                    def fn(e):
                        ins = None
                        for kc in range(8):
                            ins = e.matmul(pt[:], lhsT=win[:, kc, col0:col0 + 128], rhs=hT[:, kc, :], start=(kc == 0), stop=(kc == 7))
                        return ins
                    return fn
                for ch in range(2):
                    pt, bp = next_ring()
                    sc.op("pe", mm_fm(pt, ch * 128), reads=b_win + [b_hT], writes=[bp])
                    sc.op("act", lambda e, pt=pt, ch=ch: e.activation(out=ubuf[:, ch, 16:528], in_=pt[:], func=AF.Copy),
                          reads=[bp], writes=[b_ubuf])
                qk_list = [(kind, p) for kind in range(2) for p in range(6)]
                pend = None

                def qk_stage2(kind, p, ri):
                    pt2, bp2 = next_ring()
                    sc.op("pe", lambda e, pt2=pt2, ri=ri: e.matmul(pt2[:], lhsT=swapm[:], rhs=qb[:, ri, :], start=True, stop=True),
                          reads=[b_swap, b_qb[ri]], writes=[bp2])
                    sc.op("dve", lambda e, pt2=pt2, ri=ri: e.tensor_tensor(out=t1[:, ri, :], in0=pt2[:], in1=tabS[:], op=ALU.mult),
                          reads=[bp2, b_tabS], writes=[b_t1[ri]])
                    sc.op("pool", lambda e, ri=ri: e.tensor_tensor(out=t2[:, ri, :], in0=qb[:, ri, :], in1=tabC[:], op=ALU.mult),
                          reads=[b_qb[ri], b_tabC], writes=[b_t2[ri]])
                    if kind == 0:
                        for hq in range(2):
                            prt = slice(hq * 64, hq * 64 + 64)
                            sc.op("dve", lambda e, ri=ri, prt=prt, hq=hq: e.tensor_tensor(out=QT[prt, hq, p, :], in0=t1[prt, ri, :],
                                                                                          in1=t2[prt, ri, :], op=ALU.add),
                                  reads=[b_t1[ri], b_t2[ri]], writes=[b_QT])
                        return
                    elif p < 4:
                        dst, bd = KT01[:, p, par * CH:(par + 1) * CH], [b_KT01[par]]
                    else:
                        dst, bd = KT2[:, p - 4, c * CH:(c + 1) * CH], [b_KT2[c]]
                    sc.op("dve", lambda e, ri=ri, dst=dst: e.tensor_tensor(out=dst, in0=t1[:, ri, :], in1=t2[:, ri, :], op=ALU.add),
                          reads=[b_t1[ri], b_t2[ri]], writes=bd)

                for qi, (kind, p) in enumerate(qk_list):
                    col0 = 256 + kind * 768 + p * 128
                    pt, bp = next_ring()
                    sc.op("pe", mm_fm(pt, col0), reads=b_win + [b_hT], writes=[bp])
                    ri = qi % 2
                    scl = 0.125 if kind == 0 else 1.0
                    sc.op("act", lambda e, pt=pt, ri=ri, scl=scl: e.activation(out=qb[:, ri, :], in_=pt[:], func=AF.Identity, scale=scl),
                          reads=[bp], writes=[b_qb[ri]])
                    if pend is not None:
                        qk_stage2(*pend)
                    pend = (kind, p, ri)
                qk_stage2(*pend)
                for g in range(2):
                    for bi in range(4):
                        pt, bp = next_ring()
                        tok = slice(bi * 128, (bi + 1) * 128) if g == 0 else slice(bi, CH, 4)
                        col0 = 1792 + g * 256
                        slot = g * 8 + par * 4 + bi

                        def fn(e, pt=pt, tok=tok, col0=col0):
                            ins = None
                            for kc in range(8):
                                ins = e.matmul(pt[:, 0:256], lhsT=hT[:, kc, tok], rhs=win[:, kc, col0:col0 + 256],
                                               start=(kc == 0), stop=(kc == 7))
                            return ins
                        sc.op("pe", fn, reads=b_win + [b_hT], writes=[bp])
                        eng = "act" if bi % 2 == 0 else "dve"
                        if eng == "act":
                            sc.op("act", lambda e, pt=pt, slot=slot: e.activation(out=V3[:, slot * 4:(slot + 1) * 4, 0:64],
                                                                                   in_=pt[:, 0:256].rearrange("p (h d) -> p h d", h=4),
                                                                                   func=AF.Copy), reads=[bp], writes=[b_V01[par]])
                        else:
                            sc.op("dve", lambda e, pt=pt, slot=slot: e.tensor_copy(out=V3[:, slot * 4:(slot + 1) * 4, 0:64],
                                                                                    in_=pt[:, 0:256].rearrange("p (h d) -> p h d", h=4)),
                                  reads=[bp], writes=[b_V01[par]])
                for r0 in range(0, 16, 2):
                    pt, bp = next_ring()

                    def fn(e, pt=pt, r0=r0):
                        ins = None
                        for dr in range(2):
                            for kc in range(8):
                                ins = e.matmul(pt[cc * 32:(cc + 1) * 32, dr * 256:(dr + 1) * 256], lhsT=hT[:, kc, r0 + dr:CH:16],
                                               rhs=win[:, kc, 2304:2560], start=(kc == 0), stop=(kc == 7),
                                               tile_position=(0, cc * 32))
                        return ins
                    sc.op("pe", fn, reads=b_win + [b_hT], writes=[bp])
                    slot = 16 + n16 * 16 + r0
                    eng = "act" if (r0 // 2) % 2 == 0 else "dve"
                    if eng == "act":
                        sc.op("act", lambda e, pt=pt, slot=slot: e.activation(out=V3[cc * 32:(cc + 1) * 32, slot * 4:(slot + 2) * 4, 0:64],
                                                                               in_=pt[cc * 32:(cc + 1) * 32, :].rearrange("p (h d) -> p h d", h=8), func=AF.Copy),
                              reads=[bp], writes=[b_V2[c]])
                    else:
                        sc.op("dve", lambda e, pt=pt, slot=slot: e.tensor_copy(out=V3[cc * 32:(cc + 1) * 32, slot * 4:(slot + 2) * 4, 0:64],
                                                                                in_=pt[cc * 32:(cc + 1) * 32, :].rearrange("p (h d) -> p h d", h=8)),
                              reads=[bp], writes=[b_V2[c]])
                sc.op("pool", lambda e: e.tensor_tensor(out=s2b[:, :, 1:528], in0=ubuf[:, :, 1:528], in1=ubuf[:, :, 0:527], op=ALU.add),
                      reads=[b_ubuf], writes=[b_s2])
                sc.op("pool", lambda e: e.tensor_tensor(out=s4b[:, :, 3:528], in0=s2b[:, :, 3:528], in1=s2b[:, :, 1:526], op=ALU.add),
                      reads=[b_s2], writes=[b_s4])
                sc.op("pool", lambda e: e.tensor_tensor(out=s8b[:, 7:528], in0=s4b[:, 1, 7:528], in1=s4b[:, 1, 3:524], op=ALU.add),
                      reads=[b_s4], writes=[b_s8])
                sc.op("pool", lambda e: e.tensor_tensor(out=s16b[64:128, 15:528], in0=s8b[64:128, 15:528], in1=s8b[64:128, 7:520], op=ALU.add),
                      reads=[b_s8], writes=[b_s16])
                srcs = [(s2b, 0, slice(0, 64), b_s2, 0), (s4b, 0, slice(64, 128), b_s4, 0),
                        (s8b, None, slice(0, 64), b_s8, 1), (s16b, None, slice(64, 128), b_s16, 1)]
                for (stile, sidx, prt, bs, ch) in srcs:
                    sap = (stile[prt, sidx, 16:528] if sidx is not None else stile[prt, 16:528])
                    if c == 0:
                        sap16 = (stile[prt, sidx, 16:32] if sidx is not None else stile[prt, 16:32])
                        sc.op("pool", lambda e, sap16=sap16, prt=prt, ch=ch: e.tensor_tensor(out=sap16, in0=sap16,
                                                                                               in1=poolc[prt, 2 + ch * 16:2 + ch * 16 + 16], op=ALU.mult),
                              reads=[bs, b_poolc], writes=[bs])
                    sc.op("dve", lambda e, sap=sap, prt=prt, ch=ch: e.scalar_tensor_tensor(
                        out=mixed[prt, ch, :], in0=sap, scalar=poolc[prt, ch:ch + 1], in1=ubuf[prt, ch, 16:528],
                        op0=ALU.mult, op1=ALU.subtract), reads=[bs, b_poolc, b_ubuf], writes=[b_mixed])
                sc.op("pool", lambda e: e.tensor_copy(out=ubuf[:, :, 0:16], in_=ubuf[:, :, 512:528]), reads=[b_ubuf, b_s2, b_mixed], writes=[b_ubuf])
            def pool_mm(c):
                for ch in range(2):
                    pt, bp = next_ring()
                    sc.op("pe", lambda e, pt=pt, ch=ch: e.matmul(pt[:], lhsT=wpbd[:, ch, :], rhs=mixed[:, ch, :], start=True, stop=True),
                          reads=[b_wpbd, b_mixed], writes=[bp])
                    sc.op("dve", lambda e, pt=pt, ch=ch: e.tensor_scalar(out=catP[:, ch, :], in0=pt[:], scalar1=bpool[:, ch:ch + 1],
                                                                          scalar2=pscale[:, ch:ch + 1], op0=ALU.add, op1=ALU.mult),
                          reads=[bp, b_bpool, b_pscale], writes=[b_catP])

            pt_i = {"i": 0}

            def attn(c):
                par = c % 2
                n16, cc = c // 4, c % 4
                norm_pending = []
                for sp_ in range(2):
                    first = [True, True]
                    items = []
                    for tt in range(4):
                        n = c * 4 + tt
                        tiles = []
                        if n > 0:
                            pn = n - 1
                            tiles.append((0, KT01[:, sp_, (pn % 8) * 128:(pn % 8) * 128 + 128], (pn % 8), 128, [b_KT01[(pn // 4) % 2]], [b_V01[(pn // 4) % 2]]))
                        tiles.append((1, KT01[:, sp_, (n % 8) * 128:(n % 8) * 128 + 128], (n % 8), 128, [b_KT01[par]], [b_V01[par]]))
                        items.append(dict(q=(sp_, slice(tt * 128, (tt + 1) * 128)), nq=128, tiles=tiles, mcol=0,
                                          outsl=slice(tt * 128, (tt + 1) * 128)))
                    for r in range(4):
                        tiles = []
                        if c > 0:
                            pp = 1 - par
                            tiles.append((0, KT01[:, 2 + sp_, pp * CH + r:(pp + 1) * CH:4], 8 + pp * 4 + r, 128, [b_KT01[pp]], [b_V01[pp]]))
                        tiles.append((1, KT01[:, 2 + sp_, par * CH + r:(par + 1) * CH:4], 8 + par * 4 + r, 128, [b_KT01[par]], [b_V01[par]]))
                        items.append(dict(q=(2 + sp_, slice(r, CH, 4)), nq=128, tiles=tiles, mcol=0, outsl=slice(r, CH, 4)))
                    for r in range(16):
                        tiles = []
                        if n16 > 0:
                            tiles.append((0, KT2[:, sp_, r:2048:16], 16 + r, 128, b_KT2[0:4], b_V2[0:4]))
                        kp = (cc + 1) * 32
                        tiles.append((1, KT2[:, sp_, n16 * 2048 + r:n16 * 2048 + kp * 16:16], 16 + n16 * 16 + r, kp,
                                      b_KT2[n16 * 4:c + 1], b_V2[n16 * 4:c + 1]))
                        items.append(dict(q=(4 + sp_, slice(r, CH, 16)), nq=32, tiles=tiles, mcol=cc * 32, outsl=slice(r, CH, 16)))
                    if ATT_LIMIT is not None:
                        items = items[:ATT_LIMIT]
                    batches, cur, sig = [], [], None
                    for it in items:
                        isig = (it["nq"], tuple((tl[0], tl[3]) for tl in it["tiles"]), it["mcol"])
                        cap = 512 // (2 * it["nq"])
                        if cur and (isig != sig or len(cur) >= cap):
                            batches.append(cur)
                            cur = []
                        sig = isig
                        cur.append(it)
                    if cur:
                        batches.append(cur)

                    def stage_pv(ctx):
                        batch, pi, colof, nq = ctx
                        fl = [first[0], first[1]]
                        first[0] = first[1] = False
                        vb = set()
                        for it in batch:
                            for tl in it["tiles"]:
                                vb.update(tl[5])

                        def fnV(e, batch=batch, pi=pi, fl=fl, colof=colof, nq=nq, sp_=sp_):
                            ins = None
                            fl = list(fl)
                            for it in batch:
                                for tl in it["tiles"]:
                                    kidx, _, slot, kp, _, _ = tl
                                    for hh in range(2):
                                        col = hh * 512 + colof(kidx, it["ii"])
                                        ins = e.matmul(acc[hh][0:65, it["outsl"]], lhsT=V3[0:kp, slot * 4 + sp_ * 2 + hh, :],
                                                       rhs=PT[0:kp, pi, col:col + nq], start=fl[hh], stop=False, skip_group_check=True)
                                        fl[hh] = False
                            return ins
                        sc.op("pe", fnV, reads=[b_PT[pi], b_Vones] + list(vb), writes=[b_acc[0], b_acc[1]])

                    pend_pv = []
                    for batch in batches:
                        pts = [next_ring(), next_ring()]
                        pi = pt_i["i"] % 3
                        pt_i["i"] += 1
                        ni = len(batch)
                        nq = batch[0]["nq"]
                        mc = batch[0]["mcol"]
                        tl0 = batch[0]["tiles"]
                        kidxs = [tl[0] for tl in tl0]
                        kps = {tl[0]: tl[3] for tl in tl0}
                        kb = set()
                        for ii, it in enumerate(batch):
                            it["ii"] = ii
                            for tl in it["tiles"]:
                                kb.update(tl[4])

                        def colof(kidx, ii, ni=ni, nq=nq):
                            return kidx * (ni * nq) + ii * nq

                        def fnS(e, batch=batch, pts=pts, colof=colof, nq=nq):
                            ins = None
                            for hh in range(2):
                                pt = pts[hh][0]
                                for it in batch:
                                    for tl in it["tiles"]:
                                        kidx, kap, _, kp, _, _ = tl
                                        col = colof(kidx, it["ii"])
                                        ins = e.matmul(pt[0:kp, col:col + nq], lhsT=kap, rhs=QT[:, hh, it["q"][0], it["q"][1]],
                                                       start=True, stop=True)
                            return ins
                        sc.op("pe", fnS, reads=[b_QT] + list(kb), writes=[pts[0][1], pts[1][1]])
                        for hh in range(2):
                            pt, bp = pts[hh]
                            hb = hh * 512
                            if ni == 1 and len(kidxs) == 2:
                                sc.op("act", lambda e, pt=pt, pi=pi, hb=hb: e.activation(out=PT[:, pi, hb:hb + 256], in_=pt[:, 0:256], func=AF.Exp),
                                      reads=[bp], writes=[b_PT[pi]])
                                sc.op("dve", lambda e, pi=pi, hb=hb: e.tensor_tensor(
                                    out=PT[:, pi, hb:hb + 256].rearrange("p (k q) -> p k q", k=2),
                                    in0=PT[:, pi, hb:hb + 256].rearrange("p (k q) -> p k q", k=2),
                                    in1=maskt[:, :, 0, :], op=ALU.mult), reads=[b_PT[pi], b_mask], writes=[b_PT[pi]])
                            else:
                                for kidx in kidxs:
                                    kp = kps[kidx]
                                    base = kidx * ni * nq
                                    wdt = ni * nq
                                    sc.op("act", lambda e, pt=pt, pi=pi, kp=kp, base=base, wdt=wdt, hb=hb: e.activation(
                                        out=PT[0:kp, pi, hb + base:hb + base + wdt], in_=pt[0:kp, base:base + wdt], func=AF.Exp),
                                        reads=[bp], writes=[b_PT[pi]])
                                    if ni == 1:
                                        sc.op("dve", lambda e, pi=pi, kp=kp, base=base, wdt=wdt, kidx=kidx, mc=mc, nq=nq, hb=hb: e.tensor_tensor(
                                            out=PT[0:kp, pi, hb + base:hb + base + wdt], in0=PT[0:kp, pi, hb + base:hb + base + wdt],
                                            in1=maskt[0:kp, kidx, 0, mc:mc + nq], op=ALU.mult),
                                            reads=[b_PT[pi], b_mask], writes=[b_PT[pi]])
                                    else:
                                        mk = bass.AP(maskt, kidx * 256 + mc, [[512, kp], [0, ni], [1, nq]])
                                        sc.op("dve", lambda e, pi=pi, kp=kp, base=base, wdt=wdt, mk=mk, ni=ni, hb=hb: e.tensor_tensor(
                                            out=PT[0:kp, pi, hb + base:hb + base + wdt].rearrange("p (i q) -> p i q", i=ni),
                                            in0=PT[0:kp, pi, hb + base:hb + base + wdt].rearrange("p (i q) -> p i q", i=ni),
                                            in1=mk, op=ALU.mult), reads=[b_PT[pi], b_mask], writes=[b_PT[pi]])
                        pend_pv.append((batch, pi, colof, nq))
                        if norm_pending and len(pend_pv) >= 2:
                            norm_pending.pop(0)()
                        if len(pend_pv) >= 2:
                            stage_pv(pend_pv.pop(0))
                    while pend_pv:
                        stage_pv(pend_pv.pop(0))
                    for hh in range(2):
                        sc.op("act", lambda e, hh=hh: e.activation(out=rden[64:65, hh, :], in_=acc[hh][64:65, :], func=AF.Ln),
                              reads=[b_acc[hh]], writes=[b_rdrow[hh]])
                        sc.op("act", lambda e, hh=hh: e.activation(out=rden[64:65, hh, :], in_=rden[64:65, hh, :], func=AF.Exp, scale=-1.0),
                              reads=[b_rdrow[hh]], writes=[b_rdrow[hh]])

                    def norm_b(sp_=sp_):
                        for hh in range(2):
                            pt, bp = next_ring()
                            sc.op("pe", lambda e, hh=hh, pt=pt: e.matmul(pt[0:64, :], lhsT=onesf[64:65, 0:64], rhs=rden[64:65, hh, :],
                                                                          start=True, stop=True),
                                  reads=[b_rdrow[hh], b_onesf], writes=[bp])
                            sc.op("act", lambda e, hh=hh, pt=pt: e.activation(out=rden[0:64, hh, :], in_=pt[0:64, :], func=AF.Copy),
                                  reads=[bp], writes=[b_rden[hh]])
                            sc.op("dve", lambda e, hh=hh, sp_=sp_: e.tensor_tensor(out=attT[0:64, sp_ * 2 + hh, :], in0=acc[hh][0:64, :],
                                                                                    in1=rden[0:64, hh, :], op=ALU.mult),
                                  reads=[b_acc[hh], b_rden[hh]], writes=[b_attT])
                    norm_pending.append(norm_b)
                return norm_pending

            def _rows(ap, hh):
                return ap[hh * 64:(hh + 1) * 64]

            def epilogue(t, r, gi, x_src_d, x_src_bufs, dst_d, dst_buf, b_xo_, xo_, b_tmp, tmp_, ssi, psYp):
                psY, psYb = psYp
                sc.op("sp", lambda e: e.dma_start(out=xo_[:, r, :], in_=x_src_d[t * 128:(t + 1) * 128, :]),
                      reads=list(x_src_bufs), writes=[b_xo_[r]], dma=True)
                sumsq(ssi, psY, psYb)
                rstd_ops(ssi)
                sc.op("dve", lambda e: e.scalar_tensor_tensor(out=tmp_[:], in0=psY, scalar=rs_t[:, ssi:ssi + 1], in1=Gt[:, gi, :],
                                                              op0=ALU.mult, op1=ALU.mult),
                      reads=psYb + [b_rs[ssi], b_Gt], writes=[b_tmp])
                sc.op("pool", lambda e: e.tensor_tensor(out=xo_[:, r, :], in0=xo_[:, r, :], in1=tmp_[:], op=ALU.add),
                      reads=[b_tmp, b_xo_[r]], writes=[b_xo_[r]])
                sc.op("sp", lambda e: e.dma_start(out=dst_d[t * 128:(t + 1) * 128, :], in_=xo_[:, r, :]),
                      reads=[b_xo_[r]], writes=[dst_buf], dma=True)

            def outproj_pool(c, tt):
                tok = slice(tt * 128, (tt + 1) * 128)
                psYp = psY_a if tt % 2 == 0 else psY_c
                psY = psYp[0]

                def fn(e, tok=tok, psY=psY):
                    ins = None
                    for nh in range(2):
                        ns = slice(nh * 512, (nh + 1) * 512)
                        for ch in range(2):
                            ins = e.matmul(psY[:, ns], lhsT=catP[:, ch, tok], rhs=wout_p[:, ch, ns], start=(ch == 0), stop=False)
                    return ins
                sc.op("pe", fn, reads=[b_catP, b_woutp], writes=psYp[1])

            def outproj_attn(c, tt):
                t = c * 4 + tt
                tok = slice(tt * 128, (tt + 1) * 128)
                psYp = psY_a if tt % 2 == 0 else psY_c
                psY = psYp[0]

                def fn(e, tok=tok, psY=psY):
                    ins = None
                    for nh in range(2):
                        ns = slice(nh * 512, (nh + 1) * 512)
                        for s_ in range(4):
                            ins = e.matmul(psY[:, ns], lhsT=attT[:, s_, tok], rhs=wout_a[:, s_, ns], start=False, stop=(s_ == 3))
                    return ins
                sc.op("pe", fn, reads=[b_attT, b_wouta], writes=psYp[1])
                epilogue(t, t % 2, 0, x_d, [], x1_d, b_x1d[t], b_xo, xo, b_tmpy, tmpy, 4 + (tt % 4), psYp)

            front(0)
            checkpoint(2, [(hT[:, 0, :], [b_hT], 512, 128), (hT[:, 7, :], [b_hT], 512, 128)])
            for c in range(NCH):
                proj(c)
                if c == 0:
                    checkpoint(3, [(QT[:, 0, 0, :], [b_QT], 512, 128), (KT01[:, 0, 0:512], [b_KT01[0]], 512, 128), (Vall[:, 0:1024], [b_V01[0]], 1024, 128),
                                   (catP[:, 0, :], [b_catP], 512, 128), (QT[:, 1, 5, :], [b_QT], 512, 128), (KT2[:, 1, 0:512], [b_KT2[0]], 512, 128),
                                   (Vall[0:32, 16 * 256:20 * 256], [b_V2[0]], 1024, 32), (catP[:, 1, :], [b_catP], 512, 128), (tabC[:], [b_tabC], 512, 128), (tabS[:], [b_tabS], 512, 128)])
                normp = attn(c)
                if c == 0:
                    checkpoint(4, [(attT[0:64, 0, :], [b_attT], 512, 64), (attT[0:64, 1, :], [b_attT], 512, 64), (attT[0:64, 2, :], [b_attT], 512, 64), (attT[0:64, 3, :], [b_attT], 512, 64)])
                pool_mm(c)
                outproj_pool(c, 0)
                ring["banks"] = [0]
                while normp:
                    normp.pop(0)()
                outproj_pool(c, 1)
                if c + 1 < NCH:
                    front_load(c + 1, 0)
                    front_load(c + 1, 1)
                    front_norm(c + 1, 0)
                    front_load(c + 1, 2)
                for tt in range(4):
                    if c + 1 < NCH and tt + 1 < 4:
                        front_norm(c + 1, tt + 1)
                        if tt + 3 < 4:
                            front_load(c + 1, tt + 3)
                    if tt >= 2:
                        outproj_pool(c, tt)
                    outproj_attn(c, tt)
                    if c + 1 < NCH:
                        front_trans(c + 1, tt)
                ring["banks"] = [0, 1, 2, 3, 4]
        fence1 = sc.all_tokens()

        stF = contextlib.ExitStack()

        def sbF(name, shape, dt):
            return stF.enter_context(nc.sbuf_tensor("sf_" + name, list(shape), dt))

        with stF:
            def fb(name):
                return Buf(name, init=fence1)
            wup = sbF("wup", [128, 8, 2 * DFF], BF16)
            wdn = sbF("wdn", [128, NJ, D], BF16)
            convw = sbF("convw", [128, NJ, 3], F32)
            convb = sbF("convb", [128, NJ], F32)
            xf = sbF("xf", [128, 2, D], F32)
            xnf = sbF("xnf", [128, 2, D], BF16)
            h2T = sbF("h2T", [128, 8, CH], BF16)
            aT = sbF("aT", [128, NJ, CH], BF16)
            gS = sbF("gS", [128, 2, 514], F32)
            tcv = sbF("tcv", [128, 2, CH], F32)
            ge = sbF("ge", [128, 2, CH], BF16)
            vS = sbF("vS", [128, 2, CH], BF16)
            halo = sbF("halo", [128, NJ, 2], F32)
            xo2 = sbF("xo2", [128, 2, D], F32)
            tmp2 = sbF("tmp2", [128, D], F32)
            b_wup = [fb(f"wup{i}") for i in range(11)]
            b_wdn = [fb(f"wdn{i}") for i in range(11)]
            b_convw, b_convb = fb("convw"), fb("convb")
            b_xf = [fb("xf0"), fb("xf1")]
            b_xnf, b_h2T, b_aT = [fb("xnf0"), fb("xnf1")], fb("h2T"), fb("aT")
            b_gS = [fb("gS0"), fb("gS1")]
            b_tcv = [fb("tcv0"), fb("tcv1")]
            b_ge = [fb("ge0"), fb("ge1")]
            sink["ap"], sink["buf"] = ge[:, :, :].rearrange("p a b -> p (a b)"), b_ge
            b_vS = [fb("vS0"), fb("vS1")]
            b_halo = [fb(f"halo{j}") for j in range(NJ)]
            b_xo2 = [fb("xo2_0"), fb("xo2_1")]
            b_tmp2 = fb("tmp2")
            b_outd = [Buf(f"outd{t}") for t in range(32)]

            wup_v = wup_d.rearrange("(k p) n -> p k n", p=128)
            wdn_v = wdn_d.rearrange("(j p) n -> p j n", p=128)
            sc.op("sp", lambda e: e.dma_start(out=convw[:], in_=convw_d), writes=[b_convw], dma=True)
            sc.op("sp", lambda e: e.dma_start(out=convb[:], in_=convb_d), writes=[b_convb], dma=True)
            sc.op("dve", lambda e: e.memset(halo[:], 0.0), writes=b_halo)
            for i in range(11):
                sc.op("pool", lambda e, i=i: e.dma_start(out=wup[:, :, i * 256:(i + 1) * 256], in_=wup_v[:, :, i * 256:(i + 1) * 256]),
                      writes=[b_wup[i]], dma=True)
                sc.op("pool", lambda e, i=i: e.dma_start(out=wup[:, :, DFF + i * 256:DFF + (i + 1) * 256],
                                                         in_=wup_v[:, :, DFF + i * 256:DFF + (i + 1) * 256]),
                      writes=[b_wup[i]], dma=True)
            for i in range(11):
                sc.op("pool", lambda e, i=i: e.dma_start(out=wdn[:, 2 * i:2 * i + 2, :], in_=wdn_v[:, 2 * i:2 * i + 2, :]),
                      writes=[b_wdn[i]], dma=True)

            def frontF(c):
                frontF_load(c, 0)
                frontF_load(c, 1)
                for tt in range(4):
                    frontF_norm(c, tt)
                    if tt + 2 < 4:
                        frontF_load(c, tt + 2)
                    frontF_trans(c, tt)

            def frontF_load(c, tt):
                t = c * 4 + tt
                r = t % 2
                sc.op("sp", lambda e, t=t, r=r: e.dma_start(out=xf[:, r, :], in_=x1_d[t * 128:(t + 1) * 128, :]),
                      reads=[b_x1d[t]], writes=[b_xf[r]], dma=True)

            def frontF_norm(c, tt):
                if True:
                    t = c * 4 + tt
                    r = t % 2
                    sumsq(tt, xf[:, r, :], [b_xf[r]])
                    rstd_ops(tt)
                    sc.op("dve", lambda e, r=r, tt=tt: e.tensor_scalar(out=xnf[:, r, :], in0=xf[:, r, :], scalar1=rs_t[:, tt:tt + 1],
                                                                        scalar2=None, op0=ALU.mult),
                          reads=[b_xf[r], b_rs[tt]], writes=[b_xnf[r]])

            def frontF_trans(c, tt):
                if True:
                    r = (c * 4 + tt) % 2

                    def fnT(e, r=r):
                        ins = None
                        for kc in range(8):
                            ins = e.transpose(psT[:, kc, :], xnf[:, r, kc * 128:(kc + 1) * 128], ident[:])
                        return ins
                    sc.op("pe", fnT, reads=[b_xnf[r], b_ident], writes=[b_psT[0], b_psT[1]])
                    for kc in range(8):
                        if kc % 2 == 0:
                            sc.op("dve", lambda e, kc=kc, tt=tt: e.tensor_scalar(
                                out=h2T[:, kc, tt * 128:(tt + 1) * 128], in0=psT[:, kc, :], scalar1=modv[:, 3, kc:kc + 1],
                                scalar2=modv[:, 2, kc:kc + 1], op0=ALU.mult, op1=ALU.add),
                                reads=[b_psT[kc // 4], b_modv], writes=[b_h2T])
                        else:
                            sc.op("act", lambda e, kc=kc, tt=tt: e.activation(
                                out=h2T[:, kc, tt * 128:(tt + 1) * 128], in_=psT[:, kc, :], func=AF.Identity,
                                bias=modv[:, 2, kc:kc + 1], scale=modv[:, 3, kc:kc + 1]),
                                reads=[b_psT[kc // 4], b_modv], writes=[b_h2T])

            def up(c):
                for j in range(NJ):
                    up_s1(j)
                    if j > 0:
                        up_s2(j - 1)
                up_s2(NJ - 1)

            def up_s1(j):
                if True:
                    ri = j % 2
                    ptg, bpg = next_ring()

                    def fng(e, ptg=ptg, j=j):
                        ins = None
                        for kc in range(8):
                            ins = e.matmul(ptg[:], lhsT=wup[:, kc, j * 128:(j + 1) * 128], rhs=h2T[:, kc, :], start=(kc == 0), stop=(kc == 7))
                        return ins
                    sc.op("pe", fng, reads=[b_wup[j // 2], b_h2T], writes=[bpg])
                    ptv, bpv = next_ring()

                    def fnv(e, ptv=ptv, j=j):
                        ins = None
                        for kc in range(8):
                            ins = e.matmul(ptv[:], lhsT=wup[:, kc, DFF + j * 128:DFF + (j + 1) * 128], rhs=h2T[:, kc, :],
                                           start=(kc == 0), stop=(kc == 7))
                        return ins
                    sc.op("pe", fnv, reads=[b_wup[j // 2], b_h2T], writes=[bpv])
                    sc.op("pool", lambda e, ri=ri, j=j: e.tensor_copy(out=gS[:, ri, 0:2], in_=halo[:, j, :]),
                          reads=[b_halo[j]], writes=[b_gS[ri]])
                    sc.op("act", lambda e, ri=ri, ptg=ptg: e.activation(out=gS[:, ri, 2:514], in_=ptg[:], func=AF.Copy),
                          reads=[bpg], writes=[b_gS[ri]])
                    sc.op("act", lambda e, ri=ri, ptv=ptv: e.activation(out=vS[:, ri, :], in_=ptv[:], func=AF.Copy),
                          reads=[bpv], writes=[b_vS[ri]])
                    sc.op("pool", lambda e, ri=ri, j=j: e.tensor_copy(out=halo[:, j, :], in_=gS[:, ri, 512:514]),
                          reads=[b_gS[ri]], writes=[b_halo[j]])
                    sc.op("dve", lambda e, ri=ri, j=j: e.tensor_scalar(out=tcv[:, ri, :], in0=gS[:, ri, 2:514], scalar1=convw[:, j, 2:3],
                                                                        scalar2=convb[:, j:j + 1], op0=ALU.mult, op1=ALU.add),
                          reads=[b_gS[ri], b_convw, b_convb], writes=[b_tcv[ri]])
                    sc.op("dve", lambda e, ri=ri, j=j: e.scalar_tensor_tensor(out=tcv[:, ri, :], in0=gS[:, ri, 1:513], scalar=convw[:, j, 1:2],
                                                                                in1=tcv[:, ri, :], op0=ALU.mult, op1=ALU.add),
                          reads=[b_gS[ri], b_convw, b_tcv[ri]], writes=[b_tcv[ri]])
                    sc.op("dve", lambda e, ri=ri, j=j: e.scalar_tensor_tensor(out=tcv[:, ri, :], in0=gS[:, ri, 0:512], scalar=convw[:, j, 0:1],
                                                                               in1=tcv[:, ri, :], op0=ALU.mult, op1=ALU.add),
                          reads=[b_gS[ri], b_convw, b_tcv[ri]], writes=[b_tcv[ri]])
            def up_s2(j):
                if True:
                    ri = j % 2
                    sc.op("act", lambda e, ri=ri: e.activation(out=ge[:, ri, :], in_=tcv[:, ri, :], func=AF.Gelu_apprx_tanh),
                          reads=[b_tcv[ri]], writes=[b_ge[ri]])
                    sc.op("dve", lambda e, ri=ri, j=j: e.tensor_tensor(out=aT[:, j, :], in0=vS[:, ri, :], in1=ge[:, ri, :], op=ALU.mult),
                          reads=[b_vS[ri], b_ge[ri]], writes=[b_aT])

            def down_tile(c, tt):
                if True:
                    t = c * 4 + tt
                    tok = slice(tt * 128, (tt + 1) * 128)

                    psYp = psY_a if t % 2 == 0 else psY_b
                    psY = psYp[0]

                    def fn(e, tok=tok, psY=psY):
                        ins = None
                        for nh in range(2):
                            ns = slice(nh * 512, (nh + 1) * 512)
                            for j in range(NJ):
                                ins = e.matmul(psY[:, ns], lhsT=aT[:, j, tok], rhs=wdn[:, j, ns], start=(j == 0), stop=(j == NJ - 1))
                        return ins
                    sc.op("pe", fn, reads=[b_aT] + b_wdn, writes=psYp[1])
                    epilogue(t, t % 2, 1, x1_d, [b_x1d[t]], out_d, b_outd[t], b_xo2, xo2, b_tmp2, tmp2, 4 + (tt % 4), psYp)

            ring["banks"] = [0, 1, 2]
            frontF(0)
            for c in range(NCH):
                up(c)
                if c + 1 < NCH:
                    frontF_load(c + 1, 0)
                    frontF_load(c + 1, 1)
                    frontF_norm(c + 1, 0)
                    frontF_load(c + 1, 2)
                for tt in range(4):
                    if c + 1 < NCH and tt + 1 < 4:
                        frontF_norm(c + 1, tt + 1)
                        if tt + 3 < 4:
                            frontF_load(c + 1, tt + 3)
                    down_tile(c, tt)
                    if c + 1 < NCH:
                        frontF_trans(c + 1, tt)
            sc.op("sp", None, reads=b_outd)

            emit(nc, sc, sem_eng, sem_dma)
    return nc


def emit(nc, sc, sem_eng, sem_dma):
    def run_engine(ename, eng):
        seen = {}
        for (fn, waits, tok, dma) in sc.ops[ename]:
            for w in sorted(waits):
                key = (w[0], w[1], w[2])
                if seen.get(key, 0) >= w[3]:
                    continue
                seen[key] = w[3]
                sem = sem_eng[w[1]] if w[0] == "eng" else sem_dma[(w[1], w[2])]
                eng.wait_ge(sem, w[3])
            if fn is None:
                continue
            ins = fn(eng)
            if dma:
                ins.then_inc(sem_dma[(tok[1], tok[2])], 16)
            else:
                ins.then_inc(sem_eng[ename], 1)

    with nc.Block() as block:
        @block.sync
        def _(e):
            run_engine("sp", e)

        @block.tensor
        def _(e):
            run_engine("pe", e)

        @block.scalar
        def _(e):
            run_engine("act", e)

        @block.vector
        def _(e):
            run_engine("dve", e)

        @block.gpsimd
        def _(e):
            run_engine("pool", e)


def _consts():
    ident = np.eye(128, dtype=np.float32).astype(bf)
    sw = np.zeros((128, 128), np.float32)
    for hb in (0, 64):
        for d in range(8):
            sw[hb + d + 8, hb + d] = -1.0
            sw[hb + d, hb + d + 8] = 1.0
    inv_freq = (500000.0 ** (-np.arange(0, 16, 2, dtype=np.float32) / 16.0)).astype(np.float32)
    invf = np.zeros((128, 1), np.float32)
    for p in range(128):
        d = p % 64
        if d < 16:
            invf[p, 0] = inv_freq[d % 8]
    jj = np.arange(128)[:, None]
    ii = np.arange(128)[None, :]
    m_prev = (jj >= ii).astype(np.float32)
    m_cur = (jj <= ii).astype(np.float32)
    mask = np.zeros((128, 2, 2, 128), np.float32)
    for h in range(2):
        mask[:, 0, h, :] = m_prev
        mask[:, 1, h, :] = m_cur
    poolc = np.zeros((128, 34), np.float32)
    wins = {(0, 0): 2, (0, 1): 4, (1, 0): 8, (1, 1): 16}
    for ch in range(2):
        for half in range(2):
            w = wins[(ch, half)]
            prt = slice(half * 64, half * 64 + 64)
            poolc[prt, ch] = 1.0 / w
            for t in range(16):
                poolc[prt, 2 + ch * 16 + t] = w / min(t + 1, w)
    return dict(ident=ident, swapm=sw.astype(bf), invf=invf, mask=mask.astype(bf), poolc=poolc)


_NC_CACHE = {}


def _prep(x, c, positions, w_ada, b_ada, g_pre_mix, g_post_mix, g_pre_ffn, g_post_ffn,
          w_in, w_pool, b_pool, pool_scale, w_out, w_up, conv_w, conv_b, w_down):
    f32 = np.float32
    x = np.asarray(x, f32)
    c = np.asarray(c, f32)
    positions = np.asarray(positions, np.int32)
    w_ada = np.ascontiguousarray(np.asarray(w_ada, f32)[0])
    b_ada = np.asarray(b_ada, f32)[0]
    B = x.shape[0]
    consts = _consts()

    def pp(v):
        return np.ascontiguousarray(v.reshape(8, 128).T)
    bA = np.stack([pp(b_ada[0:D]), pp(b_ada[D:2 * D]), pp(b_ada[3 * D:4 * D]), pp(b_ada[4 * D:5 * D])], axis=1)
    bG = np.stack([np.broadcast_to(b_ada[2 * D:3 * D], (128, D)), np.broadcast_to(b_ada[5 * D:6 * D], (128, D))], axis=1)
    gpre = np.stack([pp(np.asarray(g_pre_mix, f32)[0]), pp(np.asarray(g_pre_ffn, f32)[0])], axis=1)
    gpost = np.stack([np.broadcast_to(np.asarray(g_post_mix, f32)[0], (128, D)),
                      np.broadcast_to(np.asarray(g_post_ffn, f32)[0], (128, D))], axis=1)
    bpool = np.ascontiguousarray(np.asarray(b_pool, f32)[0].reshape(2, 128).T)
    pscale = np.ascontiguousarray(np.asarray(pool_scale, f32)[0].reshape(2, 128).T)
    convw = np.ascontiguousarray(np.asarray(conv_w, f32)[0].T.reshape(NJ, 128, 3).transpose(1, 0, 2))
    convb = np.ascontiguousarray(np.asarray(conv_b, f32)[0].reshape(NJ, 128).T)
    shared = {
        "w_ada": w_ada, "bA": np.ascontiguousarray(bA), "bG": np.ascontiguousarray(bG), "gpre": np.ascontiguousarray(gpre),
        "gpost": np.ascontiguousarray(gpost), "w_in": np.ascontiguousarray(np.asarray(w_in, f32)[0]),
        "w_out": np.ascontiguousarray(np.asarray(w_out, f32)[0]), "w_up": np.ascontiguousarray(np.asarray(w_up, f32)[0]),
        "w_down": np.ascontiguousarray(np.asarray(w_down, f32)[0]), "w_pool": np.ascontiguousarray(np.asarray(w_pool, f32)[0]),
        "bpool": bpool, "pscale": pscale, "convw": convw, "convb": convb,
    }
    shared.update(consts)
    in_maps = []
    for b in range(B):
        m = dict(shared)
        m["x"] = np.ascontiguousarray(x[b])
        m["cT"] = np.ascontiguousarray(c[b].reshape(8, 128).T)
        m["pos"] = np.ascontiguousarray(np.broadcast_to(positions[b][None, :], (128, S)))
        in_maps.append(m)
    return in_maps


def kernel(x, c, positions, w_ada, b_ada, g_pre_mix, g_post_mix, g_pre_ffn, g_post_ffn,
           w_in, w_pool, b_pool, pool_scale, w_out, w_up, conv_w, conv_b, w_down):
    f32 = np.float32
    in_maps = _prep(x, c, positions, w_ada, b_ada, g_pre_mix, g_post_mix, g_pre_ffn, g_post_ffn,
                    w_in, w_pool, b_pool, pool_scale, w_out, w_up, conv_w, conv_b, w_down)
    B = len(in_maps)
    if "nc" not in _NC_CACHE:
        _NC_CACHE["nc"] = build()
    nc = _NC_CACHE["nc"]
    res = run_bass_kernel_spmd(nc, in_maps, core_ids=list(range(B)))
    return np.stack([np.asarray(r["out"], f32) for r in res.results], axis=0)
```

```python
import contextlib
import numpy as np
import ml_dtypes
import concourse.bass as bass
import concourse.mybir as mybir
from concourse.bass_utils import run_bass_kernel_spmd

F32 = mybir.dt.float32
BF16 = mybir.dt.bfloat16
I32 = mybir.dt.int32
AF = mybir.ActivationFunctionType
ALU = mybir.AluOpType
bf = ml_dtypes.bfloat16

S = 4096
D = 1024
NCH = 8
CH = 512
DFF = 2816
NJ = 22
EPS = 1e-6
ENG = ["pe", "act", "dve", "pool", "sp"]
C1 = 6.28125
C2 = float(2 * np.pi - 6.28125)
PI_LO = float(np.nextafter(np.float32(np.pi), np.float32(0)))
TWO_PI = float(2 * np.pi)
ATT_LIMIT = None
ATT_SUB = 9
ATT_HH = (0, 1)


class Buf:
    def __init__(self, name, init=()):
        self.name = name
        self.w = None
        self.r = list(init)


class Sched:
    def __init__(self):
        self.ops = {e: [] for e in ENG}
        self.cnt = {e: 0 for e in ENG}
        self.dma_i = {"sp": 0, "pool": 0}
        self.ndma = {"sp": 24, "pool": 12}
        self.dma_uses = {}

    def op(self, eng, fn, reads=(), writes=(), dma=False):
        if getattr(self, 'stopped', False):
            return None
        waits = set()
        for b in reads:
            if b.w is not None:
                waits.add(b.w)
        for b in writes:
            if b.w is not None:
                waits.add(b.w)
            waits.update(b.r)
        if dma:
            idx = self.dma_i[eng] % self.ndma[eng]
            self.dma_i[eng] += 1
            key = (eng, idx)
            prev = self.dma_uses.get(key, 0)
            self.dma_uses[key] = prev + 1
            tok = ("dma", eng, idx, 16 * (prev + 1))
            if prev > 0:
                waits.add(("dma", eng, idx, 16 * prev))
        else:
            self.cnt[eng] += 1
            tok = ("eng", eng, 0, self.cnt[eng])
        if eng == "pe":
            waits = {w for w in waits if not (w[0] == "eng" and w[1] == "pe")}
        self.ops[eng].append((fn, waits, tok, dma))
        for b in reads:
            b.r.append(tok)
        for b in writes:
            b.w = tok
            b.r = []
        return tok

    def all_tokens(self):
        toks = []
        for e in ENG:
            if self.cnt[e] > 0:
                toks.append(("eng", e, 0, self.cnt[e]))
        for (q, idx), n in self.dma_uses.items():
            toks.append(("dma", q, idx, 16 * n))
        return toks


class _Stop(Exception):
    pass


def build(stage=99):
    nc = bass.Bass("TRN2", target_bir_lowering=False)
    sc = Sched()
    dbg_bufs = []

    def dump(ap, bufs, row0, ncols, nparts=128):
        b = Buf("dbg")
        dbg_bufs.append(b)
        sc.op("pool", lambda e: e.dma_start(out=out_d[row0:row0 + nparts, 0:ncols], in_=ap), reads=list(bufs), writes=[b], dma=True)

    def checkpoint(k, dumps=()):
        if stage == k:
            for i, (ap, bufs, ncols, nparts) in enumerate(dumps):
                dump(ap, bufs, i * 128, ncols, nparts)
            sc.op("sp", None, reads=dbg_bufs)
            sc.stopped = True

    def din(name, shape, dt):
        return nc.dram_tensor(name, list(shape), dt, kind="ExternalInput").ap()

    x_d = din("x", [S, D], F32)
    cT_d = din("cT", [128, 8], F32)
    pos_d = din("pos", [128, 256], I32)
    wada_d = din("w_ada", [D, 6 * D], F32)
    bA_d = din("bA", [128, 4, 8], F32)
    bG_d = din("bG", [128, 2, D], F32)
    gpre_d = din("gpre", [128, 2, 8], F32)
    gpost_d = din("gpost", [128, 2, D], F32)
    win_d = din("w_in", [D, 2560], F32)
    wout_d = din("w_out", [512, D], F32)
    wup_d = din("w_up", [D, 2 * DFF], F32)
    wdn_d = din("w_down", [DFF, D], F32)
    wpool_d = din("w_pool", [4, 64, 64], F32)
    bpool_d = din("bpool", [128, 2], F32)
    pscale_d = din("pscale", [128, 2], F32)
    convw_d = din("convw", [128, NJ, 3], F32)
    convb_d = din("convb", [128, NJ], F32)
    ident_d = din("ident", [128, 128], BF16)
    swap_d = din("swapm", [128, 128], BF16)
    invf_d = din("invf", [128, 1], F32)
    mask_d = din("mask", [128, 2, 2, 128], BF16)
    poolc_d = din("poolc", [128, 2 + 32], F32)
    out_d = nc.dram_tensor("out", [S, D], F32, kind="ExternalOutput").ap()
    x1_d = nc.dram_tensor("x1s", [S, D], F32, kind="Internal").ap()
    tab_d = nc.dram_tensor("tabs", [2, 8, S], F32, kind="Internal").ap()

    stack = contextlib.ExitStack()

    def sb(name, shape, dt):
        return stack.enter_context(nc.sbuf_tensor("sb_" + name, list(shape), dt))

    def ps(name, shape, dt):
        return stack.enter_context(nc.psum_tensor(name, list(shape), dt))

    with stack:
        sem_eng = {e: stack.enter_context(nc.semaphore("s_" + e)) for e in ["pe", "act", "dve", "pool"]}
        sem_dma = {}
        for q in ["sp", "pool"]:
            for i in range(sc.ndma[q]):
                sem_dma[(q, i)] = stack.enter_context(nc.semaphore(f"d_{q}{i}"))

        psT = ps("psT", [128, 8, 128], BF16)
        psBig = ps("psBig", [128, 5, 512], F32)
        accbig = ps("accbig", [128, 2, 512], F32)
        psR = [psBig[:, i, :] for i in range(5)]
        acc = [accbig[:, 0, :], accbig[:, 1, :]]
        b_psT = [Buf("psT0"), Buf("psT1")]
        b_psR = [Buf(f"psR{i}") for i in range(5)]
        b_acc = [Buf("acc0"), Buf("acc1")]
        psY_a = (psBig[:, 3:5, :].rearrange("p a b -> p (a b)"), [b_psR[3], b_psR[4]])
        psY_b = (accbig[:, :, :].rearrange("p a b -> p (a b)"), [b_acc[0], b_acc[1]])
        psY_c = (psBig[:, 1:3, :].rearrange("p a b -> p (a b)"), [b_psR[1], b_psR[2]])
        ring = {"i": 0, "banks": [0, 1, 2, 3, 4]}

        def next_ring():
            bk = ring["banks"]
            i = bk[ring["i"] % len(bk)]
            ring["i"] += 1
            return psR[i], b_psR[i]

        ident = sb("ident", [128, 128], BF16)
        modv = sb("modv", [128, 4, 8], F32)
        Gt = sb("Gt", [128, 2, D], F32)
        ss_t = sb("ss_t", [128, 8], F32)
        rs_t = sb("rs_t", [128, 8], F32)
        ln_t = sb("ln_t", [128, 8], F32)
        b_ident, b_modv, b_Gt = Buf("ident"), Buf("modv"), Buf("Gt")
        sink = {}
        b_ss = [Buf(f"ss{i}") for i in range(8)]
        b_rs = [Buf(f"rs{i}") for i in range(8)]
        b_ln = [Buf(f"ln{i}") for i in range(8)]
        sc.op("sp", lambda e: e.dma_start(out=ident[:], in_=ident_d), writes=[b_ident], dma=True)

        epsb = sb("epsb", [128, 1], F32)
        b_eps = Buf("eps")
        sc.op("dve", lambda e: e.memset(epsb[:], EPS), writes=[b_eps])

        def rstd_ops(i):
            sc.op("act", lambda e: e.activation(out=ln_t[:, i:i + 1], in_=ss_t[:, i:i + 1], func=AF.Ln,
                                                bias=epsb[:, 0:1], scale=1.0 / D),
                  reads=[b_ss[i], b_eps], writes=[b_ln[i]])
            sc.op("act", lambda e: e.activation(out=rs_t[:, i:i + 1], in_=ln_t[:, i:i + 1], func=AF.Exp, scale=-0.5),
                  reads=[b_ln[i]], writes=[b_rs[i]])

        def sumsq(i, src_ap, src_bufs):
            sc.op("dve", lambda e: e.memset(ss_t[:, i:i + 1], 0.0), writes=[b_ss[i]])
            jk, bjk = sink["ap"], sink["buf"]
            sc.op("act", lambda e: e.activation(out=jk, in_=src_ap, func=AF.Square, accum_out=ss_t[:, i:i + 1]),
                  reads=list(src_bufs), writes=list(bjk) + [b_ss[i]])

        st2 = contextlib.ExitStack()

        def sb2(name, shape, dt):
            return st2.enter_context(nc.sbuf_tensor("s2_" + name, list(shape), dt))

        with st2:
            cTt = sb2("cTt", [128, 8], F32)
            cactf = sb2("cactf", [128, 8], F32)
            cact = sb2("cact", [128, 8], BF16)
            ones_t = sb2("ones_t", [128, 128], BF16)
            crep = sb2("crep", [128, 8, 128], BF16)
            bA = sb2("bA", [128, 4, 8], F32)
            gpre = sb2("gpre", [128, 2, 8], F32)
            bG = sb2("bG", [128, 2, D], F32)
            gpost = sb2("gpost", [128, 2, D], F32)
            wst = [sb2(f"wst{i}", [128, 8, D], BF16) for i in range(2)]
            invf = sb2("invf", [128, 1], F32)
            posi = sb2("posi", [128, 256], I32)
            ang = sb2("ang", [128, 256], F32)
            a2 = sb2("a2", [128, 256], F32)
            ki = sb2("ki", [128, 256], I32)
            kf = sb2("kf", [128, 256], F32)
            rr = sb2("rr", [128, 256], F32)
            tb = sb2("tb", [128, 256], F32)
            b_cT, b_cactf, b_cact, b_ones, b_crep = Buf("cT"), Buf("cactf"), Buf("cact"), Buf("ones"), Buf("crep")
            b_bA, b_gpre, b_bG, b_gpost, b_invf = Buf("bA"), Buf("gpre"), Buf("bG"), Buf("gpost"), Buf("invf")
            b_wst = [Buf("wst0"), Buf("wst1")]
            b_posi, b_ang, b_a2, b_ki, b_kf, b_rr, b_tb = (Buf(n) for n in ["posi", "ang", "a2", "ki", "kf", "rr", "tb"])
            b_tabd = [Buf("tabd0"), Buf("tabd1")]

            sc.op("sp", lambda e: e.dma_start(out=cTt[:], in_=cT_d), writes=[b_cT], dma=True)
            sc.op("sp", lambda e: e.dma_start(out=bA[:], in_=bA_d), writes=[b_bA], dma=True)
            sc.op("sp", lambda e: e.dma_start(out=gpre[:], in_=gpre_d), writes=[b_gpre], dma=True)
            sc.op("sp", lambda e: e.dma_start(out=invf[:], in_=invf_d), writes=[b_invf], dma=True)
            wada_v = wada_d.rearrange("(k p) n -> p k n", p=128)

            def load_seg(seg):
                t, bt = wst[seg % 2], b_wst[seg % 2]
                sc.op("pool", lambda e: e.dma_start(out=t[:], in_=wada_v[:, :, seg * D:(seg + 1) * D]),
                      writes=[bt], dma=True)

            load_seg(0)
            load_seg(1)
            sc.op("sp", lambda e: e.dma_start(out=bG[:], in_=bG_d), writes=[b_bG], dma=True)
            sc.op("sp", lambda e: e.dma_start(out=gpost[:], in_=gpost_d), writes=[b_gpost], dma=True)

            sc.op("act", lambda e: e.activation(out=cactf[:], in_=cTt[:], func=AF.Silu), reads=[b_cT], writes=[b_cactf])
            sc.op("dve", lambda e: e.tensor_copy(out=cact[:], in_=cactf[:]), reads=[b_cactf], writes=[b_cact])
            sc.op("dve", lambda e: e.memset(ones_t[:], 1.0), writes=[b_ones])
            for kc in range(8):
                sc.op("dve", lambda e, kc=kc: e.tensor_scalar(out=crep[:, kc, :], in0=ones_t[:], scalar1=cactf[:, kc:kc + 1],
                                                              scalar2=None, op0=ALU.mult),
                      reads=[b_ones, b_cactf], writes=[b_crep])

            psm = psR[0]
            segs_pp = {0: 0, 1: 1, 3: 2, 4: 3}

            def seg_pp(seg):
                t, bt = wst[seg % 2], b_wst[seg % 2]
                g = segs_pp[seg]

                def fn(e):
                    ins = None
                    for mg in range(8):
                        for kc in range(8):
                            ins = e.matmul(psm[:, g * 8 + mg:g * 8 + mg + 1], lhsT=t[:, kc, mg * 128:(mg + 1) * 128],
                                           rhs=cact[:, kc:kc + 1], start=(kc == 0), stop=(kc == 7))
                    return ins
                sc.op("pe", fn, reads=[bt, b_cact], writes=[b_psR[0]])

            def seg_gate(seg):
                t, bt = wst[seg % 2], b_wst[seg % 2]
                gi = 0 if seg == 2 else 1

                def fn(e):
                    ins = None
                    for nh in range(2):
                        for kc in range(8):
                            ins = e.matmul(psY_a[0][:, nh * 512:(nh + 1) * 512], lhsT=crep[:, kc, :],
                                           rhs=t[:, kc, nh * 512:(nh + 1) * 512], start=(kc == 0), stop=(kc == 7))
                    return ins
                sc.op("pe", fn, reads=[bt, b_crep], writes=psY_a[1])
                sc.op("dve", lambda e: e.tensor_tensor(out=Gt[:, gi, :], in0=psY_a[0], in1=bG[:, gi, :], op=ALU.add),
                      reads=psY_a[1] + [b_bG], writes=[b_Gt])
                sc.op("dve", lambda e: e.tensor_tensor(out=Gt[:, gi, :], in0=Gt[:, gi, :], in1=gpost[:, gi, :], op=ALU.mult),
                      reads=[b_gpost, b_Gt], writes=[b_Gt])

            seg_pp(0)
            load_seg(2)
            seg_pp(1)
            load_seg(3)
            seg_gate(2)
            load_seg(4)
            seg_pp(3)
            load_seg(5)
            seg_pp(4)
            seg_gate(5)
            sc.op("dve", lambda e: e.tensor_tensor(out=modv[:], in0=psm[:, 0:32].rearrange("p (a b) -> p a b", a=4),
                                                   in1=bA[:], op=ALU.add),
                  reads=[b_psR[0], b_bA], writes=[b_modv])
            for (a, gi) in ((1, 0), (3, 1)):
                sc.op("dve", lambda e, a=a, gi=gi: e.scalar_tensor_tensor(out=modv[:, a, :], in0=modv[:, a, :], scalar=1.0,
                                                                          in1=gpre[:, gi, :], op0=ALU.add, op1=ALU.mult),
                      reads=[b_modv, b_gpre], writes=[b_modv])

            for half in range(1):
                sc.op("sp", lambda e: e.dma_start(out=posi[:], in_=pos_d), writes=[b_posi], dma=True)
                sc.op("dve", lambda e: e.tensor_copy(out=ang[:], in_=posi[:]), reads=[b_posi], writes=[b_ang])
                sc.op("dve", lambda e: e.tensor_scalar(out=ang[:], in0=ang[:], scalar1=invf[:, 0:1], scalar2=None, op0=ALU.mult),
                      reads=[b_ang, b_invf], writes=[b_ang])
                for v in range(2):
                    off = float(np.pi / 2) if v == 0 else 0.0
                    sc.op("dve", lambda e, off=off: e.tensor_scalar(out=a2[:], in0=ang[:], scalar1=off, scalar2=None, op0=ALU.add),
                          reads=[b_ang], writes=[b_a2])
                    sc.op("dve", lambda e: e.tensor_scalar(out=ki[:], in0=a2[:], scalar1=float(1.0 / (2 * np.pi)), scalar2=None,
                                                           op0=ALU.mult), reads=[b_a2], writes=[b_ki])
                    sc.op("dve", lambda e: e.tensor_copy(out=kf[:], in_=ki[:]), reads=[b_ki], writes=[b_kf])
                    sc.op("dve", lambda e: e.scalar_tensor_tensor(out=rr[:], in0=kf[:], scalar=-C1, in1=a2[:], op0=ALU.mult,
                                                                  op1=ALU.add), reads=[b_kf, b_a2], writes=[b_rr])
                    sc.op("dve", lambda e: e.scalar_tensor_tensor(out=rr[:], in0=kf[:], scalar=-C2, in1=rr[:], op0=ALU.mult,
                                                                  op1=ALU.add), reads=[b_kf, b_rr], writes=[b_rr])
                    for (cmpop, thr, corr) in ((ALU.is_gt, PI_LO, -TWO_PI), (ALU.is_lt, -PI_LO, TWO_PI)):
                        sc.op("dve", lambda e, cmpop=cmpop, thr=thr, corr=corr: e.tensor_scalar(out=kf[:], in0=rr[:], scalar1=thr, scalar2=corr,
                                                                                              op0=cmpop, op1=ALU.mult),
                              reads=[b_rr], writes=[b_kf])
                        sc.op("dve", lambda e: e.tensor_tensor(out=rr[:], in0=rr[:], in1=kf[:], op=ALU.add), reads=[b_rr, b_kf], writes=[b_rr])
                    sc.op("dve", lambda e: e.tensor_scalar(out=rr[:], in0=rr[:], scalar1=-PI_LO, scalar2=PI_LO, op0=ALU.max, op1=ALU.min),
                          reads=[b_rr], writes=[b_rr])
                    sc.op("act", lambda e: e.activation(out=tb[:], in_=rr[:], func=AF.Sin), reads=[b_rr], writes=[b_tb])
                    sc.op("sp", lambda e, v=v: e.dma_start(out=tab_d[v].rearrange("f (t i) -> (f t) i", i=256), in_=tb[:]), reads=[b_tb],
                          writes=[b_tabd[v]], dma=True)
        checkpoint(1, [(modv[:].rearrange('p a b -> p (a b)'), [b_modv], 32, 128), (Gt[:, 0, :], [b_Gt], 1024, 128), (Gt[:, 1, :], [b_Gt], 1024, 128)])
        fence0 = sc.all_tokens()

        stM = contextlib.ExitStack()

        def sbM(name, shape, dt):
            return stM.enter_context(nc.sbuf_tensor("sm_" + name, list(shape), dt))

        with stM:
            def mb(name):
                return Buf(name, init=fence0)
            win = sbM("win", [128, 8, 2560], BF16)
            junk = sbM("junk", [128, D], BF16)
            sink["ap"], sink["buf"] = junk[:], [mb("junk")]
            wout_p = sbM("wout_p", [128, 2, D], BF16)
            wout_a = sbM("wout_a", [128, 4, D], BF16)
            wpbd = sbM("wpbd", [128, 2, 128], BF16)
            swapm = sbM("swapm", [128, 128], BF16)
            maskt = sbM("maskt", [128, 2, 2, 128], BF16)
            poolc = sbM("poolc", [128, 34], F32)
            bpool = sbM("bpool", [128, 2], F32)
            pscale = sbM("pscale", [128, 2], F32)
            KT01 = sbM("KT01", [128, 4, 1024], BF16)
            KT2 = sbM("KT2", [128, 2, S], BF16)
            NV = 48 * 256
            Vall = sbM("Vall", [128, 192 * 65], BF16)
            V3 = Vall[:, :].rearrange("p (s d) -> p s d", d=65)
            onesf = sbM("onesf", [65, 64], F32)
            QT = sbM("QT", [128, 2, 6, CH], BF16)
            xt = sbM("xt", [128, 2, D], F32)
            xn = sbM("xn", [128, 2, D], BF16)
            hT = sbM("hT", [128, 8, CH], BF16)
            tabC = sbM("tabC", [128, CH], F32)
            tabS = sbM("tabS", [128, CH], F32)
            ubuf = sbM("ubuf", [128, 2, 528], F32)
            s2b = sbM("s2b", [128, 2, 528], F32)
            s4b = sbM("s4b", [128, 2, 528], F32)
            s8b = sbM("s8b", [128, 528], F32)
            s16b = sbM("s16b", [128, 528], F32)
            mixed = sbM("mixed", [128, 2, CH], BF16)
            catP = sbM("catP", [128, 2, CH], BF16)
            qb = sbM("qb", [128, 2, CH], BF16)
            t1 = sbM("t1", [128, 2, CH], F32)
            t2 = sbM("t2", [128, 2, CH], F32)
            PT = sbM("PT", [128, 3, 1024], BF16)
            rden = sbM("rden", [65, 2, CH], F32)
            attT = sbM("attT", [128, 4, CH], BF16)
            xo = sbM("xo", [128, 2, D], F32)
            tmpy = sbM("tmpy", [128, D], F32)

            b_win = [mb(f"win{k}") for k in range(8)]
            b_woutp, b_wouta, b_wpbd, b_swap, b_mask = mb("woutp"), mb("wouta"), mb("wpbd"), mb("swap"), mb("mask")
            b_poolc, b_bpool, b_pscale = mb("poolc"), mb("bpool"), mb("pscale")
            b_KT01 = [mb("KT01_0"), mb("KT01_1")]
            b_KT2 = [mb(f"KT2_{c}") for c in range(8)]
            b_V01 = [mb("V01_0"), mb("V01_1")]
            b_V2 = [mb(f"V2_{c}") for c in range(8)]
            b_Vones = mb("Vones")
            b_QT = mb("QT")
            b_xt = [mb("xt0"), mb("xt1")]
            b_xn = [mb("xn0"), mb("xn1")]
            b_hT = mb("hT")
            b_tabC, b_tabS = mb("tabC"), mb("tabS")
            b_ubuf, b_s2, b_s4, b_s8, b_s16, b_mixed, b_catP = (mb(n) for n in ["ubuf", "s2", "s4", "s8", "s16", "mixed", "catP"])
            b_qb = [mb("qb0"), mb("qb1")]
            b_t1 = [mb("t1_0"), mb("t1_1")]
            b_t2 = [mb("t2_0"), mb("t2_1")]
            b_PT = [mb(f"PT{i}") for i in range(3)]
            b_rden = [mb("rden0"), mb("rden1")]
            b_rdrow = [mb("rdrow0"), mb("rdrow1")]
            b_attT = mb("attT")
            b_xo = [mb("xo0"), mb("xo1")]
            b_tmpy = mb("tmpy")
            b_x1d = [Buf(f"x1d{t}") for t in range(32)]

            win_v = win_d.rearrange("(k p) n -> p k n", p=128)
            for kc in range(8):
                sc.op("pool", lambda e, kc=kc: e.dma_start(out=win[:, kc, :], in_=win_v[:, kc, :]), writes=[b_win[kc]], dma=True)
            sc.op("sp", lambda e: e.dma_start(out=swapm[:], in_=swap_d), writes=[b_swap], dma=True)
            sc.op("sp", lambda e: e.dma_start(out=maskt[:], in_=mask_d), writes=[b_mask], dma=True)
            sc.op("sp", lambda e: e.dma_start(out=poolc[:], in_=poolc_d), writes=[b_poolc], dma=True)
            sc.op("sp", lambda e: e.dma_start(out=bpool[:], in_=bpool_d), writes=[b_bpool], dma=True)
            sc.op("sp", lambda e: e.dma_start(out=pscale[:], in_=pscale_d), writes=[b_pscale], dma=True)
            sc.op("dve", lambda e: e.memset(wpbd[:], 0.0), writes=[b_wpbd])
            for g in range(4):
                sc.op("pool", lambda e, g=g: e.dma_start(out=wpbd[(g % 2) * 64:(g % 2) * 64 + 64, g // 2, (g % 2) * 64:(g % 2) * 64 + 64],
                                                          in_=wpool_d[g]), writes=[b_wpbd], dma=True)
            sc.op("pool", lambda e: e.dma_start(out=wout_p[:], in_=wout_d[0:256, :].rearrange("(k p) n -> p k n", p=128)),
                  writes=[b_woutp], dma=True)
            sc.op("dve", lambda e: e.memset(wout_a[64:128, :, :], 0.0), writes=[b_wouta])
            sc.op("pool", lambda e: e.memset(attT[64:128, :, :], 0.0), writes=[b_attT])
            sc.op("pool", lambda e: e.dma_start(out=wout_a[0:64, :, :], in_=wout_d[256:512, :].rearrange("(k p) n -> p k n", p=64)),
                  writes=[b_wouta], dma=True)
            sc.op("dve", lambda e: e.memset(V3[:, :, 64:65], 1.0), writes=[b_Vones])
            b_onesf = mb("onesf")
            sc.op("dve", lambda e: e.memset(onesf[:], 1.0), writes=[b_onesf])
            sc.op("dve", lambda e: e.memset(ubuf[:, :, 0:16], 0.0), writes=[b_ubuf])
            sc.op("dve", lambda e: e.memset(tabC[:], 1.0), writes=[b_tabC])
            sc.op("dve", lambda e: e.memset(tabS[:], 0.0), writes=[b_tabS])
            sc.op("pool", lambda e: e.memset(QT[64:128, 0, :, :], 0.0), writes=[b_QT])
            sc.op("pool", lambda e: e.memset(QT[0:64, 1, :, :], 0.0), writes=[b_QT])

            def vslot_ap(slot, head, kparts):
                o = (slot * 4 + head) * 64
                return bass.AP(Vall.tensor if hasattr(Vall, "tensor") else Vall, o, [[NV + 64, kparts], [NV - o, 2], [1, 64]])

            def front(c):
                front_load(c, 0)
                front_load(c, 1)
                for tt in range(4):
                    front_norm(c, tt)
                    if tt + 2 < 4:
                        front_load(c, tt + 2)
                    front_trans(c, tt)

            def front_load(c, tt):
                t = c * 4 + tt
                r = t % 2
                sc.op("sp", lambda e, t=t, r=r: e.dma_start(out=xt[:, r, :], in_=x_d[t * 128:(t + 1) * 128, :]),
                      writes=[b_xt[r]], dma=True)

            def front_norm(c, tt):
                if True:
                    t = c * 4 + tt
                    r = t % 2
                    sumsq(tt, xt[:, r, :], [b_xt[r]])
                    rstd_ops(tt)
                    sc.op("dve", lambda e, r=r, tt=tt: e.tensor_scalar(out=xn[:, r, :], in0=xt[:, r, :], scalar1=rs_t[:, tt:tt + 1],
                                                                        scalar2=None, op0=ALU.mult),
                          reads=[b_xt[r], b_rs[tt]], writes=[b_xn[r]])

            def front_trans(c, tt):
                if True:
                    t = c * 4 + tt
                    r = t % 2

                    def fnT(e, r=r):
                        ins = None
                        for kc in range(8):
                            ins = e.transpose(psT[:, kc, :], xn[:, r, kc * 128:(kc + 1) * 128], ident[:])
                        return ins
                    sc.op("pe", fnT, reads=[b_xn[r], b_ident], writes=[b_psT[0], b_psT[1]])
                    for kc in range(8):
                        eng = "dve" if kc % 2 == 0 else "act"
                        if eng == "dve":
                            sc.op("dve", lambda e, kc=kc, tt=tt: e.tensor_scalar(
                                out=hT[:, kc, tt * 128:(tt + 1) * 128], in0=psT[:, kc, :], scalar1=modv[:, 1, kc:kc + 1],
                                scalar2=modv[:, 0, kc:kc + 1], op0=ALU.mult, op1=ALU.add),
                                reads=[b_psT[kc // 4], b_modv], writes=[b_hT])
                        else:
                            sc.op("act", lambda e, kc=kc, tt=tt: e.activation(
                                out=hT[:, kc, tt * 128:(tt + 1) * 128], in_=psT[:, kc, :], func=AF.Identity,
                                bias=modv[:, 0, kc:kc + 1], scale=modv[:, 1, kc:kc + 1]),
                                reads=[b_psT[kc // 4], b_modv], writes=[b_hT])

            def proj(c):
                par = c % 2
                n16, cc = c // 4, c % 4
                for (v, tt_, bt_) in ((0, tabC, b_tabC), (1, tabS, b_tabS)):
                    for p0 in (0, 8, 64, 72):
                        sc.op("sp", lambda e, v=v, tt_=tt_, p0=p0: e.dma_start(out=tt_[p0:p0 + 8, :], in_=tab_d[v, :, c * CH:(c + 1) * CH]),
                              reads=[b_tabd[v]], writes=[bt_], dma=True)

                def mm_fm(pt, col0):
                    def fn(e):
                        ins = None
                        for kc in range(8):
                            ins = e.matmul(pt[:], lhsT=win[:, kc, col0:col0 + 128], rhs=hT[:, kc, :], start=(kc == 0), stop=(kc == 7))
                        return ins
                    return fn
                for ch in range(2):
                    pt, bp = next_ring()
                    sc.op("pe", mm_fm(pt, ch * 128), reads=b_win + [b_hT], writes=[bp])
                    sc.op("act", lambda e, pt=pt, ch=ch: e.activation(out=ubuf[:, ch, 16:528], in_=pt[:], func=AF.Copy),
                          reads=[bp], writes=[b_ubuf])
                qk_list = [(kind, p) for kind in range(2) for p in range(6)]
                pend = None

                def qk_stage2(kind, p, ri):
                    pt2, bp2 = next_ring()
                    sc.op("pe", lambda e, pt2=pt2, ri=ri: e.matmul(pt2[:], lhsT=swapm[:], rhs=qb[:, ri, :], start=True, stop=True),
                          reads=[b_swap, b_qb[ri]], writes=[bp2])
                    sc.op("dve", lambda e, pt2=pt2, ri=ri: e.tensor_tensor(out=t1[:, ri, :], in0=pt2[:], in1=tabS[:], op=ALU.mult),
                          reads=[bp2, b_tabS], writes=[b_t1[ri]])
                    sc.op("pool", lambda e, ri=ri: e.tensor_tensor(out=t2[:, ri, :], in0=qb[:, ri, :], in1=tabC[:], op=ALU.mult),
                          reads=[b_qb[ri], b_tabC], writes=[b_t2[ri]])
                    if kind == 0:
                        for hq in range(2):
                            prt = slice(hq * 64, hq * 64 + 64)
                            sc.op("dve", lambda e, ri=ri, prt=prt, hq=hq: e.tensor_tensor(out=QT[prt, hq, p, :], in0=t1[prt, ri, :],
                                                                                          in1=t2[prt, ri, :], op=ALU.add),
                                  reads=[b_t1[ri], b_t2[ri]], writes=[b_QT])
                        return
                    elif p < 4:
                        dst, bd = KT01[:, p, par * CH:(par + 1) * CH], [b_KT01[par]]
                    else:
                        dst, bd = KT2[:, p - 4, c * CH:(c + 1) * CH], [b_KT2[c]]
                    sc.op("dve", lambda e, ri=ri, dst=dst: e.tensor_tensor(out=dst, in0=t1[:, ri, :], in1=t2[:, ri, :], op=ALU.add),
                          reads=[b_t1[ri], b_t2[ri]], writes=bd)

                for qi, (kind, p) in enumerate(qk_list):
                    col0 = 256 + kind * 768 + p * 128
                    pt, bp = next_ring()
                    sc.op("pe", mm_fm(pt, col0), reads=b_win + [b_hT], writes=[bp])
                    ri = qi % 2
                    scl = 0.125 if kind == 0 else 1.0
                    sc.op("act", lambda e, pt=pt, ri=ri, scl=scl: e.activation(out=qb[:, ri, :], in_=pt[:], func=AF.Identity, scale=scl),
                          reads=[bp], writes=[b_qb[ri]])
                    if pend is not None:
                        qk_stage2(*pend)
                    pend = (kind, p, ri)
                qk_stage2(*pend)
                for g in range(2):
                    for bi in range(4):
                        pt, bp = next_ring()
                        tok = slice(bi * 128, (bi + 1) * 128) if g == 0 else slice(bi, CH, 4)
                        col0 = 1792 + g * 256
                        slot = g * 8 + par * 4 + bi

                        def fn(e, pt=pt, tok=tok, col0=col0):
                            ins = None
                            for kc in range(8):
                                ins = e.matmul(pt[:, 0:256], lhsT=hT[:, kc, tok], rhs=win[:, kc, col0:col0 + 256],
                                               start=(kc == 0), stop=(kc == 7))
                            return ins
                        sc.op("pe", fn, reads=b_win + [b_hT], writes=[bp])
                        eng = "act" if bi % 2 == 0 else "dve"
                        if eng == "act":
                            sc.op("act", lambda e, pt=pt, slot=slot: e.activation(out=V3[:, slot * 4:(slot + 1) * 4, 0:64],
                                                                                   in_=pt[:, 0:256].rearrange("p (h d) -> p h d", h=4),
                                                                                   func=AF.Copy), reads=[bp], writes=[b_V01[par]])
                        else:
                            sc.op("dve", lambda e, pt=pt, slot=slot: e.tensor_copy(out=V3[:, slot * 4:(slot + 1) * 4, 0:64],
                                                                                    in_=pt[:, 0:256].rearrange("p (h d) -> p h d", h=4)),
                                  reads=[bp], writes=[b_V01[par]])
                for r0 in range(0, 16, 2):
                    pt, bp = next_ring()

                    def fn(e, pt=pt, r0=r0):
                        ins = None
                        for dr in range(2):
                            for kc in range(8):
                                ins = e.matmul(pt[cc * 32:(cc + 1) * 32, dr * 256:(dr + 1) * 256], lhsT=hT[:, kc, r0 + dr:CH:16],
                                               rhs=win[:, kc, 2304:2560], start=(kc == 0), stop=(kc == 7),
                                               tile_position=(0, cc * 32))
                        return ins
                    sc.op("pe", fn, reads=b_win + [b_hT], writes=[bp])
                    slot = 16 + n16 * 16 + r0
                    eng = "act" if (r0 // 2) % 2 == 0 else "dve"
                    if eng == "act":
                        sc.op("act", lambda e, pt=pt, slot=slot: e.activation(out=V3[cc * 32:(cc + 1) * 32, slot * 4:(slot + 2) * 4, 0:64],
                                                                               in_=pt[cc * 32:(cc + 1) * 32, :].rearrange("p (h d) -> p h d", h=8), func=AF.Copy),
                              reads=[bp], writes=[b_V2[c]])
                    else:
                        sc.op("dve", lambda e, pt=pt, slot=slot: e.tensor_copy(out=V3[cc * 32:(cc + 1) * 32, slot * 4:(slot + 2) * 4, 0:64],
                                                                                in_=pt[cc * 32:(cc + 1) * 32, :].rearrange("p (h d) -> p h d", h=8)),
                              reads=[bp], writes=[b_V2[c]])
                sc.op("pool", lambda e: e.tensor_tensor(out=s2b[:, :, 1:528], in0=ubuf[:, :, 1:528], in1=ubuf[:, :, 0:527], op=ALU.add),
                      reads=[b_ubuf], writes=[b_s2])
                sc.op("pool", lambda e: e.tensor_tensor(out=s4b[:, :, 3:528], in0=s2b[:, :, 3:528], in1=s2b[:, :, 1:526], op=ALU.add),
                      reads=[b_s2], writes=[b_s4])
                sc.op("pool", lambda e: e.tensor_tensor(out=s8b[:, 7:528], in0=s4b[:, 1, 7:528], in1=s4b[:, 1, 3:524], op=ALU.add),
                      reads=[b_s4], writes=[b_s8])
                sc.op("pool", lambda e: e.tensor_tensor(out=s16b[64:128, 15:528], in0=s8b[64:128, 15:528], in1=s8b[64:128, 7:520], op=ALU.add),
                      reads=[b_s8], writes=[b_s16])
                srcs = [(s2b, 0, slice(0, 64), b_s2, 0), (s4b, 0, slice(64, 128), b_s4, 0),
                        (s8b, None, slice(0, 64), b_s8, 1), (s16b, None, slice(64, 128), b_s16, 1)]
                for (stile, sidx, prt, bs, ch) in srcs:
                    sap = (stile[prt, sidx, 16:528] if sidx is not None else stile[prt, 16:528])
                    if c == 0:
                        sap16 = (stile[prt, sidx, 16:32] if sidx is not None else stile[prt, 16:32])
                        sc.op("pool", lambda e, sap16=sap16, prt=prt, ch=ch: e.tensor_tensor(out=sap16, in0=sap16,
                                                                                               in1=poolc[prt, 2 + ch * 16:2 + ch * 16 + 16], op=ALU.mult),
                              reads=[bs, b_poolc], writes=[bs])
                    sc.op("dve", lambda e, sap=sap, prt=prt, ch=ch: e.scalar_tensor_tensor(
                        out=mixed[prt, ch, :], in0=sap, scalar=poolc[prt, ch:ch + 1], in1=ubuf[prt, ch, 16:528],
                        op0=ALU.mult, op1=ALU.subtract), reads=[bs, b_poolc, b_ubuf], writes=[b_mixed])
                sc.op("pool", lambda e: e.tensor_copy(out=ubuf[:, :, 0:16], in_=ubuf[:, :, 512:528]), reads=[b_ubuf, b_s2, b_mixed], writes=[b_ubuf])
            def pool_mm(c):
                for ch in range(2):
                    pt, bp = next_ring()
                    sc.op("pe", lambda e, pt=pt, ch=ch: e.matmul(pt[:], lhsT=wpbd[:, ch, :], rhs=mixed[:, ch, :], start=True, stop=True),
                          reads=[b_wpbd, b_mixed], writes=[bp])
                    sc.op("dve", lambda e, pt=pt, ch=ch: e.tensor_scalar(out=catP[:, ch, :], in0=pt[:], scalar1=bpool[:, ch:ch + 1],
                                                                          scalar2=pscale[:, ch:ch + 1], op0=ALU.add, op1=ALU.mult),
                          reads=[bp, b_bpool, b_pscale], writes=[b_catP])

            pt_i = {"i": 0}

            def attn(c):
                par = c % 2
                n16, cc = c // 4, c % 4
                norm_pending = []
                for sp_ in range(2):
                    first = [True, True]
                    items = []
                    for tt in range(4):
                        n = c * 4 + tt
                        tiles = []
                        if n > 0:
                            pn = n - 1
                            tiles.append((0, KT01[:, sp_, (pn % 8) * 128:(pn % 8) * 128 + 128], (pn % 8), 128, [b_KT01[(pn // 4) % 2]], [b_V01[(pn // 4) % 2]]))
                        tiles.append((1, KT01[:, sp_, (n % 8) * 128:(n % 8) * 128 + 128], (n % 8), 128, [b_KT01[par]], [b_V01[par]]))
                        items.append(dict(q=(sp_, slice(tt * 128, (tt + 1) * 128)), nq=128, tiles=tiles, mcol=0,
                                          outsl=slice(tt * 128, (tt + 1) * 128)))
                    for r in range(4):
                        tiles = []
                        if c > 0:
                            pp = 1 - par
                            tiles.append((0, KT01[:, 2 + sp_, pp * CH + r:(pp + 1) * CH:4], 8 + pp * 4 + r, 128, [b_KT01[pp]], [b_V01[pp]]))
                        tiles.append((1, KT01[:, 2 + sp_, par * CH + r:(par + 1) * CH:4], 8 + par * 4 + r, 128, [b_KT01[par]], [b_V01[par]]))
                        items.append(dict(q=(2 + sp_, slice(r, CH, 4)), nq=128, tiles=tiles, mcol=0, outsl=slice(r, CH, 4)))
                    for r in range(16):
                        tiles = []
                        if n16 > 0:
                            tiles.append((0, KT2[:, sp_, r:2048:16], 16 + r, 128, b_KT2[0:4], b_V2[0:4]))
                        kp = (cc + 1) * 32
                        tiles.append((1, KT2[:, sp_, n16 * 2048 + r:n16 * 2048 + kp * 16:16], 16 + n16 * 16 + r, kp,
                                      b_KT2[n16 * 4:c + 1], b_V2[n16 * 4:c + 1]))
                        items.append(dict(q=(4 + sp_, slice(r, CH, 16)), nq=32, tiles=tiles, mcol=cc * 32, outsl=slice(r, CH, 16)))
                    if ATT_LIMIT is not None:
                        items = items[:ATT_LIMIT]
                    batches, cur, sig = [], [], None
                    for it in items:
                        isig = (it["nq"], tuple((tl[0], tl[3]) for tl in it["tiles"]), it["mcol"])
                        cap = 512 // (2 * it["nq"])
                        if cur and (isig != sig or len(cur) >= cap):
                            batches.append(cur)
                            cur = []
                        sig = isig
                        cur.append(it)
                    if cur:
                        batches.append(cur)

                    def stage_pv(ctx):
                        batch, pi, colof, nq = ctx
                        fl = [first[0], first[1]]
                        first[0] = first[1] = False
                        vb = set()
                        for it in batch:
                            for tl in it["tiles"]:
                                vb.update(tl[5])

                        def fnV(e, batch=batch, pi=pi, fl=fl, colof=colof, nq=nq, sp_=sp_):
                            ins = None
                            fl = list(fl)
                            for it in batch:
                                for tl in it["tiles"]:
                                    kidx, _, slot, kp, _, _ = tl
                                    for hh in range(2):
                                        col = hh * 512 + colof(kidx, it["ii"])
                                        ins = e.matmul(acc[hh][0:65, it["outsl"]], lhsT=V3[0:kp, slot * 4 + sp_ * 2 + hh, :],
                                                       rhs=PT[0:kp, pi, col:col + nq], start=fl[hh], stop=False, skip_group_check=True)
                                        fl[hh] = False
                            return ins
                        sc.op("pe", fnV, reads=[b_PT[pi], b_Vones] + list(vb), writes=[b_acc[0], b_acc[1]])

                    pend_pv = []
                    for batch in batches:
                        pts = [next_ring(), next_ring()]
                        pi = pt_i["i"] % 3
                        pt_i["i"] += 1
                        ni = len(batch)
                        nq = batch[0]["nq"]
                        mc = batch[0]["mcol"]
                        tl0 = batch[0]["tiles"]
                        kidxs = [tl[0] for tl in tl0]
                        kps = {tl[0]: tl[3] for tl in tl0}
                        kb = set()
                        for ii, it in enumerate(batch):
                            it["ii"] = ii
                            for tl in it["tiles"]:
                                kb.update(tl[4])

                        def colof(kidx, ii, ni=ni, nq=nq):
                            return kidx * (ni * nq) + ii * nq

                        def fnS(e, batch=batch, pts=pts, colof=colof, nq=nq):
                            ins = None
                            for hh in range(2):
                                pt = pts[hh][0]
                                for it in batch:
                                    for tl in it["tiles"]:
                                        kidx, kap, _, kp, _, _ = tl
                                        col = colof(kidx, it["ii"])
                                        ins = e.matmul(pt[0:kp, col:col + nq], lhsT=kap, rhs=QT[:, hh, it["q"][0], it["q"][1]],
                                                       start=True, stop=True)
                            return ins
                        sc.op("pe", fnS, reads=[b_QT] + list(kb), writes=[pts[0][1], pts[1][1]])
                        for hh in range(2):
                            pt, bp = pts[hh]
                            hb = hh * 512
                            if ni == 1 and len(kidxs) == 2:
                                sc.op("act", lambda e, pt=pt, pi=pi, hb=hb: e.activation(out=PT[:, pi, hb:hb + 256], in_=pt[:, 0:256], func=AF.Exp),
                                      reads=[bp], writes=[b_PT[pi]])
                                sc.op("dve", lambda e, pi=pi, hb=hb: e.tensor_tensor(
                                    out=PT[:, pi, hb:hb + 256].rearrange("p (k q) -> p k q", k=2),
                                    in0=PT[:, pi, hb:hb + 256].rearrange("p (k q) -> p k q", k=2),
                                    in1=maskt[:, :, 0, :], op=ALU.mult), reads=[b_PT[pi], b_mask], writes=[b_PT[pi]])
                            else:
                                for kidx in kidxs:
                                    kp = kps[kidx]
                                    base = kidx * ni * nq
                                    wdt = ni * nq
                                    sc.op("act", lambda e, pt=pt, pi=pi, kp=kp, base=base, wdt=wdt, hb=hb: e.activation(
                                        out=PT[0:kp, pi, hb + base:hb + base + wdt], in_=pt[0:kp, base:base + wdt], func=AF.Exp),
                                        reads=[bp], writes=[b_PT[pi]])
                                    if ni == 1:
                                        sc.op("dve", lambda e, pi=pi, kp=kp, base=base, wdt=wdt, kidx=kidx, mc=mc, nq=nq, hb=hb: e.tensor_tensor(
                                            out=PT[0:kp, pi, hb + base:hb + base + wdt], in0=PT[0:kp, pi, hb + base:hb + base + wdt],
                                            in1=maskt[0:kp, kidx, 0, mc:mc + nq], op=ALU.mult),
                                            reads=[b_PT[pi], b_mask], writes=[b_PT[pi]])
                                    else:
                                        mk = bass.AP(maskt, kidx * 256 + mc, [[512, kp], [0, ni], [1, nq]])
                                        sc.op("dve", lambda e, pi=pi, kp=kp, base=base, wdt=wdt, mk=mk, ni=ni, hb=hb: e.tensor_tensor(
                                            out=PT[0:kp, pi, hb + base:hb + base + wdt].rearrange("p (i q) -> p i q", i=ni),
                                            in0=PT[0:kp, pi, hb + base:hb + base + wdt].rearrange("p (i q) -> p i q", i=ni),
                                            in1=mk, op=ALU.mult), reads=[b_PT[pi], b_mask], writes=[b_PT[pi]])
                        pend_pv.append((batch, pi, colof, nq))
                        if norm_pending and len(pend_pv) >= 2:
                            norm_pending.pop(0)()
                        if len(pend_pv) >= 2:
                            stage_pv(pend_pv.pop(0))
                    while pend_pv:
                        stage_pv(pend_pv.pop(0))
                    for hh in range(2):
                        sc.op("act", lambda e, hh=hh: e.activation(out=rden[64:65, hh, :], in_=acc[hh][64:65, :], func=AF.Ln),
                              reads=[b_acc[hh]], writes=[b_rdrow[hh]])
                        sc.op("act", lambda e, hh=hh: e.activation(out=rden[64:65, hh, :], in_=rden[64:65, hh, :], func=AF.Exp, scale=-1.0),
                              reads=[b_rdrow[hh]], writes=[b_rdrow[hh]])

                    def norm_b(sp_=sp_):
                        for hh in range(2):
                            pt, bp = next_ring()
                            sc.op("pe", lambda e, hh=hh, pt=pt: e.matmul(pt[0:64, :], lhsT=onesf[64:65, 0:64], rhs=rden[64:65, hh, :],
                                                                          start=True, stop=True),
                                  reads=[b_rdrow[hh], b_onesf], writes=[bp])
                            sc.op("act", lambda e, hh=hh, pt=pt: e.activation(out=rden[0:64, hh, :], in_=pt[0:64, :], func=AF.Copy),
                                  reads=[bp], writes=[b_rden[hh]])
                            sc.op("dve", lambda e, hh=hh, sp_=sp_: e.tensor_tensor(out=attT[0:64, sp_ * 2 + hh, :], in0=acc[hh][0:64, :],
                                                                                    in1=rden[0:64, hh, :], op=ALU.mult),
                                  reads=[b_acc[hh], b_rden[hh]], writes=[b_attT])
                    norm_pending.append(norm_b)
                return norm_pending

            def _rows(ap, hh):
                return ap[hh * 64:(hh + 1) * 64]

            def epilogue(t, r, gi, x_src_d, x_src_bufs, dst_d, dst_buf, b_xo_, xo_, b_tmp, tmp_, ssi, psYp):
                psY, psYb = psYp
                sc.op("sp", lambda e: e.dma_start(out=xo_[:, r, :], in_=x_src_d[t * 128:(t + 1) * 128, :]),
                      reads=list(x_src_bufs), writes=[b_xo_[r]], dma=True)
                sumsq(ssi, psY, psYb)
                rstd_ops(ssi)
                sc.op("dve", lambda e: e.scalar_tensor_tensor(out=tmp_[:], in0=psY, scalar=rs_t[:, ssi:ssi + 1], in1=Gt[:, gi, :],
                                                              op0=ALU.mult, op1=ALU.mult),
                      reads=psYb + [b_rs[ssi], b_Gt], writes=[b_tmp])
                sc.op("pool", lambda e: e.tensor_tensor(out=xo_[:, r, :], in0=xo_[:, r, :], in1=tmp_[:], op=ALU.add),
                      reads=[b_tmp, b_xo_[r]], writes=[b_xo_[r]])
                sc.op("sp", lambda e: e.dma_start(out=dst_d[t * 128:(t + 1) * 128, :], in_=xo_[:, r, :]),
                      reads=[b_xo_[r]], writes=[dst_buf], dma=True)

            def outproj_pool(c, tt):
                tok = slice(tt * 128, (tt + 1) * 128)
                psYp = psY_a if tt % 2 == 0 else psY_c
                psY = psYp[0]

                def fn(e, tok=tok, psY=psY):
                    ins = None
                    for nh in range(2):
                        ns = slice(nh * 512, (nh + 1) * 512)
                        for ch in range(2):
                            ins = e.matmul(psY[:, ns], lhsT=catP[:, ch, tok], rhs=wout_p[:, ch, ns], start=(ch == 0), stop=False)
                    return ins
                sc.op("pe", fn, reads=[b_catP, b_woutp], writes=psYp[1])

            def outproj_attn(c, tt):
                t = c * 4 + tt
                tok = slice(tt * 128, (tt + 1) * 128)
                psYp = psY_a if tt % 2 == 0 else psY_c
                psY = psYp[0]

                def fn(e, tok=tok, psY=psY):
                    ins = None
                    for nh in range(2):
                        ns = slice(nh * 512, (nh + 1) * 512)
                        for s_ in range(4):
                            ins = e.matmul(psY[:, ns], lhsT=attT[:, s_, tok], rhs=wout_a[:, s_, ns], start=False, stop=(s_ == 3))
                    return ins
                sc.op("pe", fn, reads=[b_attT, b_wouta], writes=psYp[1])
                epilogue(t, t % 2, 0, x_d, [], x1_d, b_x1d[t], b_xo, xo, b_tmpy, tmpy, 4 + (tt % 4), psYp)

            front(0)
            checkpoint(2, [(hT[:, 0, :], [b_hT], 512, 128), (hT[:, 7, :], [b_hT], 512, 128)])
            win_addr = nc.lookup_mloc(win).addr
            b_wup_pre = [Buf(f"wupq{q}") for q in range(10)]
            wup_v = wup_d.rearrange("(k p) n -> p k n", p=128)
            winflat = win[:, :, :].rearrange("p k n -> p (k n)")

            def wup_cols(q):
                c0 = (q % 2) * DFF + (q // 2) * 256
                return slice(c0, c0 + 256)

            for c in range(NCH):
                proj(c)
                if c == NCH - 1:
                    for q in range(10):
                        dst = winflat[:, q * 2048:(q + 1) * 2048].rearrange("p (k n) -> p k n", k=8)
                        sc.op("pool", lambda e, q=q, dst=dst: e.dma_start(out=dst, in_=wup_v[:, :, wup_cols(q)]),
                              writes=b_win + [b_wup_pre[q]], dma=True)
                if c == 0:
                    checkpoint(3, [(QT[:, 0, 0, :], [b_QT], 512, 128), (KT01[:, 0, 0:512], [b_KT01[0]], 512, 128), (Vall[:, 0:1024], [b_V01[0]], 1024, 128),
                                   (catP[:, 0, :], [b_catP], 512, 128), (QT[:, 1, 5, :], [b_QT], 512, 128), (KT2[:, 1, 0:512], [b_KT2[0]], 512, 128),
                                   (Vall[0:32, 16 * 256:20 * 256], [b_V2[0]], 1024, 32), (catP[:, 1, :], [b_catP], 512, 128), (tabC[:], [b_tabC], 512, 128), (tabS[:], [b_tabS], 512, 128)])
                normp = attn(c)
                if c == 0:
                    checkpoint(4, [(attT[0:64, 0, :], [b_attT], 512, 64), (attT[0:64, 1, :], [b_attT], 512, 64), (attT[0:64, 2, :], [b_attT], 512, 64), (attT[0:64, 3, :], [b_attT], 512, 64)])
                pool_mm(c)
                outproj_pool(c, 0)
                ring["banks"] = [0]
                while normp:
                    normp.pop(0)()
                outproj_pool(c, 1)
                if c + 1 < NCH:
                    front_load(c + 1, 0)
                    front_load(c + 1, 1)
                    front_norm(c + 1, 0)
                    front_load(c + 1, 2)
                for tt in range(4):
                    if c + 1 < NCH and tt + 1 < 4:
                        front_norm(c + 1, tt + 1)
                        if tt + 3 < 4:
                            front_load(c + 1, tt + 3)
                    if c + 1 < NCH:
                        front_trans(c + 1, tt)
                    if tt >= 2:
                        outproj_pool(c, tt)
                    outproj_attn(c, tt)
                ring["banks"] = [0, 1, 2, 3, 4]
        fence1 = sc.all_tokens()

        stF = contextlib.ExitStack()

        def sbF(name, shape, dt):
            return stF.enter_context(nc.sbuf_tensor("sf_" + name, list(shape), dt))

        with stF:
            def fb(name):
                return Buf(name, init=fence1)
            wup = sbF("wup", [128, 22, 8, 256], BF16)
            assert nc.lookup_mloc(wup).addr == win_addr, "wup must alias the w_in region for the early prefetch"
            wdn = sbF("wdn", [128, NJ, D], BF16)
            convw = sbF("convw", [128, NJ, 3], F32)
            convb = sbF("convb", [128, NJ], F32)
            xf = sbF("xf", [128, 2, D], F32)
            xnf = sbF("xnf", [128, 2, D], BF16)
            h2T = sbF("h2T", [128, 8, CH], BF16)
            aT = sbF("aT", [128, NJ, CH], BF16)
            gS = sbF("gS", [128, 2, 514], F32)
            tcv = sbF("tcv", [128, 2, CH], F32)
            ge = sbF("ge", [128, 2, CH], BF16)
            vS = sbF("vS", [128, 2, CH], BF16)
            halo = sbF("halo", [128, NJ, 2], F32)
            xo2 = sbF("xo2", [128, 2, D], F32)
            tmp2 = sbF("tmp2", [128, D], F32)
            b_wup = [b_wup_pre[q] if q < 10 else fb(f"wup{q}") for q in range(22)]
            b_wdn = [fb(f"wdn{i}") for i in range(11)]
            b_convw, b_convb = fb("convw"), fb("convb")
            b_xf = [fb("xf0"), fb("xf1")]
            b_xnf, b_h2T, b_aT = [fb("xnf0"), fb("xnf1")], fb("h2T"), fb("aT")
            b_gS = [fb("gS0"), fb("gS1")]
            b_tcv = [fb("tcv0"), fb("tcv1")]
            b_ge = [fb("ge0"), fb("ge1")]
            sink["ap"], sink["buf"] = ge[:, :, :].rearrange("p a b -> p (a b)"), b_ge
            b_vS = [fb("vS0"), fb("vS1")]
            b_halo = [fb(f"halo{j}") for j in range(NJ)]
            b_xo2 = [fb("xo2_0"), fb("xo2_1")]
            b_tmp2 = fb("tmp2")
            b_outd = [Buf(f"outd{t}") for t in range(32)]

            wdn_v = wdn_d.rearrange("(j p) n -> p j n", p=128)
            sc.op("sp", lambda e: e.dma_start(out=convw[:], in_=convw_d), writes=[b_convw], dma=True)
            sc.op("sp", lambda e: e.dma_start(out=convb[:], in_=convb_d), writes=[b_convb], dma=True)
            sc.op("dve", lambda e: e.memset(halo[:], 0.0), writes=b_halo)
            for i in range(11):
                for q in (2 * i, 2 * i + 1):
                    if q >= 10:
                        sc.op("pool", lambda e, q=q: e.dma_start(out=wup[:, q, :, :], in_=wup_v[:, :, wup_cols(q)]),
                              writes=[b_wup[q]], dma=True)
            for i in range(11):
                sc.op("pool", lambda e, i=i: e.dma_start(out=wdn[:, 2 * i:2 * i + 2, :], in_=wdn_v[:, 2 * i:2 * i + 2, :]),
                      writes=[b_wdn[i]], dma=True)

            def frontF(c):
                frontF_load(c, 0)
                frontF_load(c, 1)
                for tt in range(4):
                    frontF_norm(c, tt)
                    if tt + 2 < 4:
                        frontF_load(c, tt + 2)
                    frontF_trans(c, tt)

            def frontF_load(c, tt):
                t = c * 4 + tt
                r = t % 2
                sc.op("sp", lambda e, t=t, r=r: e.dma_start(out=xf[:, r, :], in_=x1_d[t * 128:(t + 1) * 128, :]),
                      reads=[b_x1d[t]], writes=[b_xf[r]], dma=True)

            def frontF_norm(c, tt):
                if True:
                    t = c * 4 + tt
                    r = t % 2
                    sumsq(tt, xf[:, r, :], [b_xf[r]])
                    rstd_ops(tt)
                    sc.op("dve", lambda e, r=r, tt=tt: e.tensor_scalar(out=xnf[:, r, :], in0=xf[:, r, :], scalar1=rs_t[:, tt:tt + 1],
                                                                        scalar2=None, op0=ALU.mult),
                          reads=[b_xf[r], b_rs[tt]], writes=[b_xnf[r]])

            def frontF_trans(c, tt):
                if True:
                    r = (c * 4 + tt) % 2

                    def fnT(e, r=r):
                        ins = None
                        for kc in range(8):
                            ins = e.transpose(psT[:, kc, :], xnf[:, r, kc * 128:(kc + 1) * 128], ident[:])
                        return ins
                    sc.op("pe", fnT, reads=[b_xnf[r], b_ident], writes=[b_psT[0], b_psT[1]])
                    for kc in range(8):
                        if kc % 2 == 0:
                            sc.op("dve", lambda e, kc=kc, tt=tt: e.tensor_scalar(
                                out=h2T[:, kc, tt * 128:(tt + 1) * 128], in0=psT[:, kc, :], scalar1=modv[:, 3, kc:kc + 1],
                                scalar2=modv[:, 2, kc:kc + 1], op0=ALU.mult, op1=ALU.add),
                                reads=[b_psT[kc // 4], b_modv], writes=[b_h2T])
                        else:
                            sc.op("act", lambda e, kc=kc, tt=tt: e.activation(
                                out=h2T[:, kc, tt * 128:(tt + 1) * 128], in_=psT[:, kc, :], func=AF.Identity,
                                bias=modv[:, 2, kc:kc + 1], scale=modv[:, 3, kc:kc + 1]),
                                reads=[b_psT[kc // 4], b_modv], writes=[b_h2T])

            def up(c):
                for j in range(NJ):
                    up_s1(j)
                    if j > 0:
                        up_s2(j - 1)
                up_s2(NJ - 1)

            def up_s1(j):
                if True:
                    ri = j % 2
                    ptg, bpg = next_ring()

                    def fng(e, ptg=ptg, j=j):
                        ins = None
                        for kc in range(8):
                            ins = e.matmul(ptg[:], lhsT=wup[:, 2 * (j // 2), kc, (j % 2) * 128:(j % 2) * 128 + 128], rhs=h2T[:, kc, :],
                                           start=(kc == 0), stop=(kc == 7))
                        return ins
                    sc.op("pe", fng, reads=[b_wup[2 * (j // 2)], b_h2T], writes=[bpg])
                    ptv, bpv = next_ring()

                    def fnv(e, ptv=ptv, j=j):
                        ins = None
                        for kc in range(8):
                            ins = e.matmul(ptv[:], lhsT=wup[:, 2 * (j // 2) + 1, kc, (j % 2) * 128:(j % 2) * 128 + 128], rhs=h2T[:, kc, :],
                                           start=(kc == 0), stop=(kc == 7))
                        return ins
                    sc.op("pe", fnv, reads=[b_wup[2 * (j // 2) + 1], b_h2T], writes=[bpv])
                    sc.op("pool", lambda e, ri=ri, j=j: e.tensor_copy(out=gS[:, ri, 0:2], in_=halo[:, j, :]),
                          reads=[b_halo[j]], writes=[b_gS[ri]])
                    sc.op("act", lambda e, ri=ri, ptg=ptg: e.activation(out=gS[:, ri, 2:514], in_=ptg[:], func=AF.Copy),
                          reads=[bpg], writes=[b_gS[ri]])
                    sc.op("act", lambda e, ri=ri, ptv=ptv: e.activation(out=vS[:, ri, :], in_=ptv[:], func=AF.Copy),
                          reads=[bpv], writes=[b_vS[ri]])
                    sc.op("pool", lambda e, ri=ri, j=j: e.tensor_copy(out=halo[:, j, :], in_=gS[:, ri, 512:514]),
                          reads=[b_gS[ri]], writes=[b_halo[j]])
                    sc.op("dve", lambda e, ri=ri, j=j: e.tensor_scalar(out=tcv[:, ri, :], in0=gS[:, ri, 2:514], scalar1=convw[:, j, 2:3],
                                                                        scalar2=convb[:, j:j + 1], op0=ALU.mult, op1=ALU.add),
                          reads=[b_gS[ri], b_convw, b_convb], writes=[b_tcv[ri]])
                    sc.op("dve", lambda e, ri=ri, j=j: e.scalar_tensor_tensor(out=tcv[:, ri, :], in0=gS[:, ri, 1:513], scalar=convw[:, j, 1:2],
                                                                                in1=tcv[:, ri, :], op0=ALU.mult, op1=ALU.add),
                          reads=[b_gS[ri], b_convw, b_tcv[ri]], writes=[b_tcv[ri]])
                    sc.op("dve", lambda e, ri=ri, j=j: e.scalar_tensor_tensor(out=tcv[:, ri, :], in0=gS[:, ri, 0:512], scalar=convw[:, j, 0:1],
                                                                               in1=tcv[:, ri, :], op0=ALU.mult, op1=ALU.add),
                          reads=[b_gS[ri], b_convw, b_tcv[ri]], writes=[b_tcv[ri]])
            def up_s2(j):
                if True:
                    ri = j % 2
                    sc.op("act", lambda e, ri=ri: e.activation(out=ge[:, ri, :], in_=tcv[:, ri, :], func=AF.Gelu_apprx_tanh),
                          reads=[b_tcv[ri]], writes=[b_ge[ri]])
                    sc.op("dve", lambda e, ri=ri, j=j: e.tensor_tensor(out=aT[:, j, :], in0=vS[:, ri, :], in1=ge[:, ri, :], op=ALU.mult),
                          reads=[b_vS[ri], b_ge[ri]], writes=[b_aT])

            def down_tile(c, tt):
                if True:
                    t = c * 4 + tt
                    tok = slice(tt * 128, (tt + 1) * 128)

                    psYp = psY_a if t % 2 == 0 else psY_b
                    psY = psYp[0]

                    def fn(e, tok=tok, psY=psY):
                        ins = None
                        for nh in range(2):
                            ns = slice(nh * 512, (nh + 1) * 512)
                            for j in range(NJ):
                                ins = e.matmul(psY[:, ns], lhsT=aT[:, j, tok], rhs=wdn[:, j, ns], start=(j == 0), stop=(j == NJ - 1))
                        return ins
                    sc.op("pe", fn, reads=[b_aT] + b_wdn, writes=psYp[1])
                    epilogue(t, t % 2, 1, x1_d, [b_x1d[t]], out_d, b_outd[t], b_xo2, xo2, b_tmp2, tmp2, 4 + (tt % 4), psYp)

            ring["banks"] = [0, 1, 2]
            frontF(0)
            for c in range(NCH):
                up(c)
                if c + 1 < NCH:
                    frontF_load(c + 1, 0)
                    frontF_load(c + 1, 1)
                    frontF_norm(c + 1, 0)
                    frontF_load(c + 1, 2)
                for tt in range(4):
                    if c + 1 < NCH and tt + 1 < 4:
                        frontF_norm(c + 1, tt + 1)
                        if tt + 3 < 4:
                            frontF_load(c + 1, tt + 3)
                    if c + 1 < NCH:
                        frontF_trans(c + 1, tt)
                    down_tile(c, tt)
            sc.op("sp", None, reads=b_outd)

            emit(nc, sc, sem_eng, sem_dma)
    return nc


def emit(nc, sc, sem_eng, sem_dma):
    def run_engine(ename, eng):
        seen = {}
        for (fn, waits, tok, dma) in sc.ops[ename]:
            for w in sorted(waits):
                key = (w[0], w[1], w[2])
                if seen.get(key, 0) >= w[3]:
                    continue
                seen[key] = w[3]
                sem = sem_eng[w[1]] if w[0] == "eng" else sem_dma[(w[1], w[2])]
                eng.wait_ge(sem, w[3])
            if fn is None:
                continue
            ins = fn(eng)
            if dma:
                ins.then_inc(sem_dma[(tok[1], tok[2])], 16)
            else:
                ins.then_inc(sem_eng[ename], 1)

    with nc.Block() as block:
        @block.sync
        def _(e):
            run_engine("sp", e)

        @block.tensor
        def _(e):
            run_engine("pe", e)

        @block.scalar
        def _(e):
            run_engine("act", e)

        @block.vector
        def _(e):
            run_engine("dve", e)

        @block.gpsimd
        def _(e):
            run_engine("pool", e)


def _consts():
    ident = np.eye(128, dtype=np.float32).astype(bf)
    sw = np.zeros((128, 128), np.float32)
    for hb in (0, 64):
        for d in range(8):
            sw[hb + d + 8, hb + d] = -1.0
            sw[hb + d, hb + d + 8] = 1.0
    inv_freq = (500000.0 ** (-np.arange(0, 16, 2, dtype=np.float32) / 16.0)).astype(np.float32)
    invf = np.zeros((128, 1), np.float32)
    for p in range(128):
        invf[p, 0] = inv_freq[p // 16]
    jj = np.arange(128)[:, None]
    ii = np.arange(128)[None, :]
    m_prev = (jj >= ii).astype(np.float32)
    m_cur = (jj <= ii).astype(np.float32)
    mask = np.zeros((128, 2, 2, 128), np.float32)
    for h in range(2):
        mask[:, 0, h, :] = m_prev
        mask[:, 1, h, :] = m_cur
    poolc = np.zeros((128, 34), np.float32)
    wins = {(0, 0): 2, (0, 1): 4, (1, 0): 8, (1, 1): 16}
    for ch in range(2):
        for half in range(2):
            w = wins[(ch, half)]
            prt = slice(half * 64, half * 64 + 64)
            poolc[prt, ch] = 1.0 / w
            for t in range(16):
                poolc[prt, 2 + ch * 16 + t] = w / min(t + 1, w)
    return dict(ident=ident, swapm=sw.astype(bf), invf=invf, mask=mask.astype(bf), poolc=poolc)


_NC_CACHE = {}


def _prep(x, c, positions, w_ada, b_ada, g_pre_mix, g_post_mix, g_pre_ffn, g_post_ffn,
          w_in, w_pool, b_pool, pool_scale, w_out, w_up, conv_w, conv_b, w_down):
    f32 = np.float32
    x = np.asarray(x, f32)
    c = np.asarray(c, f32)
    positions = np.asarray(positions, np.int32)
    w_ada = np.ascontiguousarray(np.asarray(w_ada, f32)[0])
    b_ada = np.asarray(b_ada, f32)[0]
    B = x.shape[0]
    consts = _consts()

    def pp(v):
        return np.ascontiguousarray(v.reshape(8, 128).T)
    bA = np.stack([pp(b_ada[0:D]), pp(b_ada[D:2 * D]), pp(b_ada[3 * D:4 * D]), pp(b_ada[4 * D:5 * D])], axis=1)
    bG = np.stack([np.broadcast_to(b_ada[2 * D:3 * D], (128, D)), np.broadcast_to(b_ada[5 * D:6 * D], (128, D))], axis=1)
    gpre = np.stack([pp(np.asarray(g_pre_mix, f32)[0]), pp(np.asarray(g_pre_ffn, f32)[0])], axis=1)
    gpost = np.stack([np.broadcast_to(np.asarray(g_post_mix, f32)[0], (128, D)),
                      np.broadcast_to(np.asarray(g_post_ffn, f32)[0], (128, D))], axis=1)
    bpool = np.ascontiguousarray(np.asarray(b_pool, f32)[0].reshape(2, 128).T)
    pscale = np.ascontiguousarray(np.asarray(pool_scale, f32)[0].reshape(2, 128).T)
    convw = np.ascontiguousarray(np.asarray(conv_w, f32)[0].T.reshape(NJ, 128, 3).transpose(1, 0, 2))
    convb = np.ascontiguousarray(np.asarray(conv_b, f32)[0].reshape(NJ, 128).T)
    shared = {
        "w_ada": w_ada, "bA": np.ascontiguousarray(bA), "bG": np.ascontiguousarray(bG), "gpre": np.ascontiguousarray(gpre),
        "gpost": np.ascontiguousarray(gpost), "w_in": np.ascontiguousarray(np.asarray(w_in, f32)[0]),
        "w_out": np.ascontiguousarray(np.asarray(w_out, f32)[0]), "w_up": np.ascontiguousarray(np.asarray(w_up, f32)[0]),
        "w_down": np.ascontiguousarray(np.asarray(w_down, f32)[0]), "w_pool": np.ascontiguousarray(np.asarray(w_pool, f32)[0]),
        "bpool": bpool, "pscale": pscale, "convw": convw, "convb": convb,
    }
    shared.update(consts)
    in_maps = []
    for b in range(B):
        m = dict(shared)
        m["x"] = np.ascontiguousarray(x[b])
        m["cT"] = np.ascontiguousarray(c[b].reshape(8, 128).T)
        m["pos"] = np.ascontiguousarray(np.tile(positions[b].reshape(16, 256), (8, 1)))
        in_maps.append(m)
    return in_maps


def kernel(x, c, positions, w_ada, b_ada, g_pre_mix, g_post_mix, g_pre_ffn, g_post_ffn,
           w_in, w_pool, b_pool, pool_scale, w_out, w_up, conv_w, conv_b, w_down):
    f32 = np.float32
    in_maps = _prep(x, c, positions, w_ada, b_ada, g_pre_mix, g_post_mix, g_pre_ffn, g_post_ffn,
                    w_in, w_pool, b_pool, pool_scale, w_out, w_up, conv_w, conv_b, w_down)
    B = len(in_maps)
    if "nc" not in _NC_CACHE:
        _NC_CACHE["nc"] = build()
    nc = _NC_CACHE["nc"]
    res = run_bass_kernel_spmd(nc, in_maps, core_ids=list(range(B)))
    return np.stack([np.asarray(r["out"], f32) for r in res.results], axis=0)
```

```python
import contextlib
import numpy as np
import ml_dtypes
import concourse.bass as bass
import concourse.mybir as mybir
from concourse.bass_utils import run_bass_kernel_spmd

F32 = mybir.dt.float32
BF16 = mybir.dt.bfloat16
I32 = mybir.dt.int32
AF = mybir.ActivationFunctionType
ALU = mybir.AluOpType
bf = ml_dtypes.bfloat16

S = 4096
D = 1024
NCH = 8
CH = 512
DFF = 2816
NJ = 22
EPS = 1e-6
ENG = ["pe", "act", "dve", "pool", "sp"]
C1 = 6.28125
C2 = float(2 * np.pi - 6.28125)
PI_LO = float(np.nextafter(np.float32(np.pi), np.float32(0)))
TWO_PI = float(2 * np.pi)
ATT_LIMIT = None
EMBED_WAIT = True
ATT_SUB = 9
ATT_HH = (0, 1)


class Buf:
    def __init__(self, name, init=()):
        self.name = name
        self.w = None
        self.r = list(init)


class Sched:
    def __init__(self):
        self.ops = {e: [] for e in ENG}
        self.cnt = {e: 0 for e in ENG}
        self.dma_i = {"sp": 0, "pool": 0}
        self.ndma = {"sp": 24, "pool": 12}
        self.dma_uses = {}

    def op(self, eng, fn, reads=(), writes=(), dma=False):
        if getattr(self, 'stopped', False):
            return None
        waits = set()
        for b in reads:
            if b.w is not None:
                waits.add(b.w)
        for b in writes:
            if b.w is not None:
                waits.add(b.w)
            waits.update(b.r)
        if dma:
            idx = self.dma_i[eng] % self.ndma[eng]
            self.dma_i[eng] += 1
            key = (eng, idx)
            prev = self.dma_uses.get(key, 0)
            self.dma_uses[key] = prev + 1
            tok = ("dma", eng, idx, 16 * (prev + 1))
            if prev > 0:
                waits.add(("dma", eng, idx, 16 * prev))
        else:
            self.cnt[eng] += 1
            tok = ("eng", eng, 0, self.cnt[eng])
        if eng == "pe":
            waits = {w for w in waits if not (w[0] == "eng" and w[1] == "pe")}
        self.ops[eng].append((fn, waits, tok, dma))
        for b in reads:
            b.r.append(tok)
        for b in writes:
            b.w = tok
            b.r = []
        return tok

    def all_tokens(self):
        toks = []
        for e in ENG:
            if self.cnt[e] > 0:
                toks.append(("eng", e, 0, self.cnt[e]))
        for (q, idx), n in self.dma_uses.items():
            toks.append(("dma", q, idx, 16 * n))
        return toks


class _Stop(Exception):
    pass


def build(stage=99):
    nc = bass.Bass("TRN2", target_bir_lowering=False)
    sc = Sched()
    dbg_bufs = []

    def dump(ap, bufs, row0, ncols, nparts=128):
        b = Buf("dbg")
        dbg_bufs.append(b)
        sc.op("pool", lambda e: e.dma_start(out=out_d[row0:row0 + nparts, 0:ncols], in_=ap), reads=list(bufs), writes=[b], dma=True)

    def checkpoint(k, dumps=()):
        if stage == k:
            for i, (ap, bufs, ncols, nparts) in enumerate(dumps):
                dump(ap, bufs, i * 128, ncols, nparts)
            sc.op("sp", None, reads=dbg_bufs)
            sc.stopped = True

    def din(name, shape, dt):
        return nc.dram_tensor(name, list(shape), dt, kind="ExternalInput").ap()

    x_d = din("x", [S, D], F32)
    cT_d = din("cT", [128, 8], F32)
    pos_d = din("pos", [128, 256], I32)
    wada_d = din("w_ada", [D, 6 * D], F32)
    bA_d = din("bA", [128, 4, 8], F32)
    bG_d = din("bG", [128, 2, D], F32)
    gpre_d = din("gpre", [128, 2, 8], F32)
    gpost_d = din("gpost", [128, 2, D], F32)
    win_d = din("w_in", [D, 2560], F32)
    wout_d = din("w_out", [512, D], F32)
    wup_d = din("w_up", [D, 2 * DFF], F32)
    wdn_d = din("w_down", [DFF, D], F32)
    wpool_d = din("w_pool", [4, 64, 64], F32)
    bpool_d = din("bpool", [128, 2], F32)
    pscale_d = din("pscale", [128, 2], F32)
    convw_d = din("convw", [128, NJ, 3], F32)
    convb_d = din("convb", [128, NJ], F32)
    ident_d = din("ident", [128, 128], BF16)
    swap_d = din("swapm", [128, 128], BF16)
    invf_d = din("invf", [128, 1], F32)
    mask_d = din("mask", [128, 2, 2, 128], BF16)
    poolc_d = din("poolc", [128, 2 + 32], F32)
    out_d = nc.dram_tensor("out", [S, D], F32, kind="ExternalOutput").ap()
    x1_d = nc.dram_tensor("x1s", [S, D], F32, kind="Internal").ap()
    tab_d = nc.dram_tensor("tabs", [2, 8, S], F32, kind="Internal").ap()

    stack = contextlib.ExitStack()

    def sb(name, shape, dt):
        return stack.enter_context(nc.sbuf_tensor("sb_" + name, list(shape), dt))

    def ps(name, shape, dt):
        return stack.enter_context(nc.psum_tensor(name, list(shape), dt))

    with stack:
        sem_eng = {e: stack.enter_context(nc.semaphore("s_" + e)) for e in ["pe", "act", "dve", "pool"]}
        sem_dma = {}
        for q in ["sp", "pool"]:
            for i in range(sc.ndma[q]):
                sem_dma[(q, i)] = stack.enter_context(nc.semaphore(f"d_{q}{i}"))

        psT = ps("psT", [128, 8, 128], BF16)
        psBig = ps("psBig", [128, 5, 512], F32)
        accbig = ps("accbig", [128, 2, 512], F32)
        psR = [psBig[:, i, :] for i in range(5)]
        acc = [accbig[:, 0, :], accbig[:, 1, :]]
        b_psT = [Buf("psT0"), Buf("psT1")]
        b_psR = [Buf(f"psR{i}") for i in range(5)]
        b_acc = [Buf("acc0"), Buf("acc1")]
        psY_a = (psBig[:, 3:5, :].rearrange("p a b -> p (a b)"), [b_psR[3], b_psR[4]])
        psY_b = (accbig[:, :, :].rearrange("p a b -> p (a b)"), [b_acc[0], b_acc[1]])
        psY_c = (psBig[:, 1:3, :].rearrange("p a b -> p (a b)"), [b_psR[1], b_psR[2]])
        ring = {"i": 0, "banks": [0, 1, 2, 3, 4]}

        def next_ring():
            bk = ring["banks"]
            i = bk[ring["i"] % len(bk)]
            ring["i"] += 1
            return psR[i], b_psR[i]

        ident = sb("ident", [128, 128], BF16)
        modv = sb("modv", [128, 4, 8], F32)
        Gt = sb("Gt", [128, 2, D], F32)
        ss_t = sb("ss_t", [128, 8], F32)
        rs_t = sb("rs_t", [128, 8], F32)
        ln_t = sb("ln_t", [128, 8], F32)
        b_ident, b_modv, b_Gt = Buf("ident"), Buf("modv"), Buf("Gt")
        sink = {}
        b_ss = [Buf(f"ss{i}") for i in range(8)]
        b_rs = [Buf(f"rs{i}") for i in range(8)]
        b_ln = [Buf(f"ln{i}") for i in range(8)]
        sc.op("sp", lambda e: e.dma_start(out=ident[:], in_=ident_d), writes=[b_ident], dma=True)

        epsb = sb("epsb", [128, 1], F32)
        b_eps = Buf("eps")
        sc.op("dve", lambda e: e.memset(epsb[:], EPS), writes=[b_eps])

        def rstd_ops(i):
            sc.op("act", lambda e: e.activation(out=ln_t[:, i:i + 1], in_=ss_t[:, i:i + 1], func=AF.Ln,
                                                bias=epsb[:, 0:1], scale=1.0 / D),
                  reads=[b_ss[i], b_eps], writes=[b_ln[i]])
            sc.op("act", lambda e: e.activation(out=rs_t[:, i:i + 1], in_=ln_t[:, i:i + 1], func=AF.Exp, scale=-0.5),
                  reads=[b_ln[i]], writes=[b_rs[i]])

        def sumsq(i, src_ap, src_bufs):
            sc.op("dve", lambda e: e.memset(ss_t[:, i:i + 1], 0.0), writes=[b_ss[i]])
            jk, bjk = sink["ap"], sink["buf"]
            sc.op("act", lambda e: e.activation(out=jk, in_=src_ap, func=AF.Square, accum_out=ss_t[:, i:i + 1]),
                  reads=list(src_bufs), writes=list(bjk) + [b_ss[i]])

        st2 = contextlib.ExitStack()

        def sb2(name, shape, dt):
            return st2.enter_context(nc.sbuf_tensor("s2_" + name, list(shape), dt))

        with st2:
            cTt = sb2("cTt", [128, 8], F32)
            cactf = sb2("cactf", [128, 8], F32)
            cact = sb2("cact", [128, 8], BF16)
            ones_t = sb2("ones_t", [128, 128], BF16)
            crep = sb2("crep", [128, 8, 128], BF16)
            bA = sb2("bA", [128, 4, 8], F32)
            gpre = sb2("gpre", [128, 2, 8], F32)
            bG = sb2("bG", [128, 2, D], F32)
            gpost = sb2("gpost", [128, 2, D], F32)
            wst = [sb2(f"wst{i}", [128, 8, D], BF16) for i in range(2)]
            invf = sb2("invf", [128, 1], F32)
            posi = sb2("posi", [128, 256], I32)
            ang = sb2("ang", [128, 256], F32)
            a2 = sb2("a2", [128, 256], F32)
            ki = sb2("ki", [128, 256], I32)
            kf = sb2("kf", [128, 256], F32)
            rr = sb2("rr", [128, 256], F32)
            tb = sb2("tb", [128, 256], F32)
            b_cT, b_cactf, b_cact, b_ones, b_crep = Buf("cT"), Buf("cactf"), Buf("cact"), Buf("ones"), Buf("crep")
            b_bA, b_gpre, b_bG, b_gpost, b_invf = Buf("bA"), Buf("gpre"), Buf("bG"), Buf("gpost"), Buf("invf")
            b_wst = [Buf("wst0"), Buf("wst1")]
            b_posi, b_ang, b_a2, b_ki, b_kf, b_rr, b_tb = (Buf(n) for n in ["posi", "ang", "a2", "ki", "kf", "rr", "tb"])
            b_tabd = [Buf("tabd0"), Buf("tabd1")]

            sc.op("sp", lambda e: e.dma_start(out=cTt[:], in_=cT_d), writes=[b_cT], dma=True)
            sc.op("sp", lambda e: e.dma_start(out=bA[:], in_=bA_d), writes=[b_bA], dma=True)
            sc.op("sp", lambda e: e.dma_start(out=gpre[:], in_=gpre_d), writes=[b_gpre], dma=True)
            sc.op("sp", lambda e: e.dma_start(out=invf[:], in_=invf_d), writes=[b_invf], dma=True)
            wada_v = wada_d.rearrange("(k p) n -> p k n", p=128)

            def load_seg(seg):
                t, bt = wst[seg % 2], b_wst[seg % 2]
                sc.op("pool", lambda e: e.dma_start(out=t[:], in_=wada_v[:, :, seg * D:(seg + 1) * D]),
                      writes=[bt], dma=True)

            load_seg(0)
            load_seg(1)
            sc.op("sp", lambda e: e.dma_start(out=bG[:], in_=bG_d), writes=[b_bG], dma=True)
            sc.op("sp", lambda e: e.dma_start(out=gpost[:], in_=gpost_d), writes=[b_gpost], dma=True)

            sc.op("act", lambda e: e.activation(out=cactf[:], in_=cTt[:], func=AF.Silu), reads=[b_cT], writes=[b_cactf])
            sc.op("dve", lambda e: e.tensor_copy(out=cact[:], in_=cactf[:]), reads=[b_cactf], writes=[b_cact])
            sc.op("dve", lambda e: e.memset(ones_t[:], 1.0), writes=[b_ones])
            for kc in range(8):
                sc.op("dve", lambda e, kc=kc: e.tensor_scalar(out=crep[:, kc, :], in0=ones_t[:], scalar1=cactf[:, kc:kc + 1],
                                                              scalar2=None, op0=ALU.mult),
                      reads=[b_ones, b_cactf], writes=[b_crep])

            psm = psR[0]
            segs_pp = {0: 0, 1: 1, 3: 2, 4: 3}

            def seg_pp(seg):
                t, bt = wst[seg % 2], b_wst[seg % 2]
                g = segs_pp[seg]

                def fn(e):
                    ins = None
                    for mg in range(8):
                        for kc in range(8):
                            ins = e.matmul(psm[:, g * 8 + mg:g * 8 + mg + 1], lhsT=t[:, kc, mg * 128:(mg + 1) * 128],
                                           rhs=cact[:, kc:kc + 1], start=(kc == 0), stop=(kc == 7))
                    return ins
                sc.op("pe", fn, reads=[bt, b_cact], writes=[b_psR[0]])

            def seg_gate(seg):
                t, bt = wst[seg % 2], b_wst[seg % 2]
                gi = 0 if seg == 2 else 1

                def fn(e):
                    ins = None
                    for nh in range(2):
                        for kc in range(8):
                            ins = e.matmul(psY_a[0][:, nh * 512:(nh + 1) * 512], lhsT=crep[:, kc, :],
                                           rhs=t[:, kc, nh * 512:(nh + 1) * 512], start=(kc == 0), stop=(kc == 7))
                    return ins
                sc.op("pe", fn, reads=[bt, b_crep], writes=psY_a[1])
                sc.op("dve", lambda e: e.tensor_tensor(out=Gt[:, gi, :], in0=psY_a[0], in1=bG[:, gi, :], op=ALU.add),
                      reads=psY_a[1] + [b_bG], writes=[b_Gt])
                sc.op("dve", lambda e: e.tensor_tensor(out=Gt[:, gi, :], in0=Gt[:, gi, :], in1=gpost[:, gi, :], op=ALU.mult),
                      reads=[b_gpost, b_Gt], writes=[b_Gt])

            seg_pp(0)
            load_seg(2)
            seg_pp(1)
            load_seg(3)
            seg_gate(2)
            load_seg(4)
            seg_pp(3)
            load_seg(5)
            seg_pp(4)
            seg_gate(5)
            sc.op("dve", lambda e: e.tensor_tensor(out=modv[:], in0=psm[:, 0:32].rearrange("p (a b) -> p a b", a=4),
                                                   in1=bA[:], op=ALU.add),
                  reads=[b_psR[0], b_bA], writes=[b_modv])
            for (a, gi) in ((1, 0), (3, 1)):
                sc.op("dve", lambda e, a=a, gi=gi: e.scalar_tensor_tensor(out=modv[:, a, :], in0=modv[:, a, :], scalar=1.0,
                                                                          in1=gpre[:, gi, :], op0=ALU.add, op1=ALU.mult),
                      reads=[b_modv, b_gpre], writes=[b_modv])

            for half in range(1):
                sc.op("sp", lambda e: e.dma_start(out=posi[:], in_=pos_d), writes=[b_posi], dma=True)
                sc.op("dve", lambda e: e.tensor_copy(out=ang[:], in_=posi[:]), reads=[b_posi], writes=[b_ang])
                sc.op("dve", lambda e: e.tensor_scalar(out=ang[:], in0=ang[:], scalar1=invf[:, 0:1], scalar2=None, op0=ALU.mult),
                      reads=[b_ang, b_invf], writes=[b_ang])
                for v in range(2):
                    off = float(np.pi / 2) if v == 0 else 0.0
                    sc.op("dve", lambda e, off=off: e.tensor_scalar(out=a2[:], in0=ang[:], scalar1=off, scalar2=None, op0=ALU.add),
                          reads=[b_ang], writes=[b_a2])
                    sc.op("dve", lambda e: e.tensor_scalar(out=ki[:], in0=a2[:], scalar1=float(1.0 / (2 * np.pi)), scalar2=None,
                                                           op0=ALU.mult), reads=[b_a2], writes=[b_ki])
                    sc.op("dve", lambda e: e.tensor_copy(out=kf[:], in_=ki[:]), reads=[b_ki], writes=[b_kf])
                    sc.op("dve", lambda e: e.scalar_tensor_tensor(out=rr[:], in0=kf[:], scalar=-C1, in1=a2[:], op0=ALU.mult,
                                                                  op1=ALU.add), reads=[b_kf, b_a2], writes=[b_rr])
                    sc.op("dve", lambda e: e.scalar_tensor_tensor(out=rr[:], in0=kf[:], scalar=-C2, in1=rr[:], op0=ALU.mult,
                                                                  op1=ALU.add), reads=[b_kf, b_rr], writes=[b_rr])
                    for (cmpop, thr, corr) in ((ALU.is_gt, PI_LO, -TWO_PI), (ALU.is_lt, -PI_LO, TWO_PI)):
                        sc.op("dve", lambda e, cmpop=cmpop, thr=thr, corr=corr: e.tensor_scalar(out=kf[:], in0=rr[:], scalar1=thr, scalar2=corr,
                                                                                              op0=cmpop, op1=ALU.mult),
                              reads=[b_rr], writes=[b_kf])
                        sc.op("dve", lambda e: e.tensor_tensor(out=rr[:], in0=rr[:], in1=kf[:], op=ALU.add), reads=[b_rr, b_kf], writes=[b_rr])
                    sc.op("dve", lambda e: e.tensor_scalar(out=rr[:], in0=rr[:], scalar1=-PI_LO, scalar2=PI_LO, op0=ALU.max, op1=ALU.min),
                          reads=[b_rr], writes=[b_rr])
                    sc.op("act", lambda e: e.activation(out=tb[:], in_=rr[:], func=AF.Sin), reads=[b_rr], writes=[b_tb])
                    sc.op("sp", lambda e, v=v: e.dma_start(out=tab_d[v].rearrange("f (t i) -> (f t) i", i=256), in_=tb[:]), reads=[b_tb],
                          writes=[b_tabd[v]], dma=True)
        checkpoint(1, [(modv[:].rearrange('p a b -> p (a b)'), [b_modv], 32, 128), (Gt[:, 0, :], [b_Gt], 1024, 128), (Gt[:, 1, :], [b_Gt], 1024, 128)])
        fence0 = sc.all_tokens()

        stM = contextlib.ExitStack()

        def sbM(name, shape, dt):
            return stM.enter_context(nc.sbuf_tensor("sm_" + name, list(shape), dt))

        with stM:
            def mb(name):
                return Buf(name, init=fence0)
            win = sbM("win", [128, 8, 2560], BF16)
            junk = sbM("junk", [128, D], BF16)
            sink["ap"], sink["buf"] = junk[:], [mb("junk")]
            wout_p = sbM("wout_p", [128, 2, D], BF16)
            wout_a = sbM("wout_a", [128, 4, D], BF16)
            wpbd = sbM("wpbd", [128, 2, 128], BF16)
            swapm = sbM("swapm", [128, 128], BF16)
            maskt = sbM("maskt", [128, 2, 2, 128], BF16)
            poolc = sbM("poolc", [128, 34], F32)
            bpool = sbM("bpool", [128, 2], F32)
            pscale = sbM("pscale", [128, 2], F32)
            KT01 = sbM("KT01", [128, 4, 1024], BF16)
            KT2 = sbM("KT2", [128, 2, S], BF16)
            NV = 48 * 256
            Vall = sbM("Vall", [128, 192 * 65], BF16)
            V3 = Vall[:, :].rearrange("p (s d) -> p s d", d=65)
            onesf = sbM("onesf", [65, 64], F32)
            QT = sbM("QT", [128, 2, 6, CH], BF16)
            xt = sbM("xt", [128, 2, D], F32)
            xn = sbM("xn", [128, 2, D], BF16)
            hT = sbM("hT", [128, 8, CH], BF16)
            tabC = sbM("tabC", [128, CH], F32)
            tabS = sbM("tabS", [128, CH], F32)
            ubuf = sbM("ubuf", [128, 2, 528], F32)
            s2b = sbM("s2b", [128, 2, 528], F32)
            s4b = sbM("s4b", [128, 2, 528], F32)
            s8b = sbM("s8b", [128, 528], F32)
            s16b = sbM("s16b", [128, 528], F32)
            mixed = sbM("mixed", [128, 2, CH], BF16)
            catP = sbM("catP", [128, 2, CH], BF16)
            qb = sbM("qb", [128, 2, CH], BF16)
            t1 = sbM("t1", [128, 2, CH], F32)
            t2 = sbM("t2", [128, 2, CH], F32)
            PT = sbM("PT", [128, 3, 1024], BF16)
            rden = sbM("rden", [65, 2, CH], F32)
            attT = sbM("attT", [128, 4, CH], BF16)
            xo = sbM("xo", [128, 2, D], F32)
            tmpy = sbM("tmpy", [128, D], F32)

            b_win = [mb(f"win{k}") for k in range(8)]
            b_woutp, b_wouta, b_wpbd, b_swap, b_mask = mb("woutp"), mb("wouta"), mb("wpbd"), mb("swap"), mb("mask")
            b_poolc, b_bpool, b_pscale = mb("poolc"), mb("bpool"), mb("pscale")
            b_KT01 = [mb("KT01_0"), mb("KT01_1")]
            b_KT2 = [mb(f"KT2_{c}") for c in range(8)]
            b_V01 = [mb("V01_0"), mb("V01_1")]
            b_V2 = [mb(f"V2_{c}") for c in range(8)]
            b_Vones = mb("Vones")
            b_QT = mb("QT")
            b_xt = [mb("xt0"), mb("xt1")]
            b_xn = [mb("xn0"), mb("xn1")]
            b_hT = mb("hT")
            b_tabC, b_tabS = mb("tabC"), mb("tabS")
            b_ubuf, b_s2, b_s4, b_s8, b_s16, b_mixed, b_catP = (mb(n) for n in ["ubuf", "s2", "s4", "s8", "s16", "mixed", "catP"])
            b_qb = [mb("qb0"), mb("qb1")]
            b_t1 = [mb("t1_0"), mb("t1_1")]
            b_t2 = [mb("t2_0"), mb("t2_1")]
            b_PT = [mb(f"PT{i}") for i in range(3)]
            b_rden = [mb("rden0"), mb("rden1")]
            b_rdrow = [mb("rdrow0"), mb("rdrow1")]
            b_attT = mb("attT")
            b_xo = [mb("xo0"), mb("xo1")]
            b_tmpy = mb("tmpy")
            b_x1d = [Buf(f"x1d{t}") for t in range(32)]

            win_v = win_d.rearrange("(k p) n -> p k n", p=128)
            for kc in range(8):
                sc.op("pool", lambda e, kc=kc: e.dma_start(out=win[:, kc, :], in_=win_v[:, kc, :]), writes=[b_win[kc]], dma=True)
            sc.op("sp", lambda e: e.dma_start(out=swapm[:], in_=swap_d), writes=[b_swap], dma=True)
            sc.op("sp", lambda e: e.dma_start(out=maskt[:], in_=mask_d), writes=[b_mask], dma=True)
            sc.op("sp", lambda e: e.dma_start(out=poolc[:], in_=poolc_d), writes=[b_poolc], dma=True)
            sc.op("sp", lambda e: e.dma_start(out=bpool[:], in_=bpool_d), writes=[b_bpool], dma=True)
            sc.op("sp", lambda e: e.dma_start(out=pscale[:], in_=pscale_d), writes=[b_pscale], dma=True)
            sc.op("dve", lambda e: e.memset(wpbd[:], 0.0), writes=[b_wpbd])
            for g in range(4):
                sc.op("pool", lambda e, g=g: e.dma_start(out=wpbd[(g % 2) * 64:(g % 2) * 64 + 64, g // 2, (g % 2) * 64:(g % 2) * 64 + 64],
                                                          in_=wpool_d[g]), writes=[b_wpbd], dma=True)
            sc.op("pool", lambda e: e.dma_start(out=wout_p[:], in_=wout_d[0:256, :].rearrange("(k p) n -> p k n", p=128)),
                  writes=[b_woutp], dma=True)
            sc.op("dve", lambda e: e.memset(wout_a[64:128, :, :], 0.0), writes=[b_wouta])
            sc.op("pool", lambda e: e.memset(attT[64:128, :, :], 0.0), writes=[b_attT])
            sc.op("pool", lambda e: e.dma_start(out=wout_a[0:64, :, :], in_=wout_d[256:512, :].rearrange("(k p) n -> p k n", p=64)),
                  writes=[b_wouta], dma=True)
            sc.op("dve", lambda e: e.memset(V3[:, :, 64:65], 1.0), writes=[b_Vones])
            b_onesf = mb("onesf")
            sc.op("dve", lambda e: e.memset(onesf[:], 1.0), writes=[b_onesf])
            sc.op("dve", lambda e: e.memset(ubuf[:, :, 0:16], 0.0), writes=[b_ubuf])
            sc.op("dve", lambda e: e.memset(tabC[:], 1.0), writes=[b_tabC])
            sc.op("dve", lambda e: e.memset(tabS[:], 0.0), writes=[b_tabS])
            sc.op("pool", lambda e: e.memset(QT[64:128, 0, :, :], 0.0), writes=[b_QT])
            sc.op("pool", lambda e: e.memset(QT[0:64, 1, :, :], 0.0), writes=[b_QT])

            def vslot_ap(slot, head, kparts):
                o = (slot * 4 + head) * 64
                return bass.AP(Vall.tensor if hasattr(Vall, "tensor") else Vall, o, [[NV + 64, kparts], [NV - o, 2], [1, 64]])

            def front(c):
                front_load(c, 0)
                front_load(c, 1)
                for tt in range(4):
                    front_norm(c, tt)
                    if tt + 2 < 4:
                        front_load(c, tt + 2)
                    front_trans(c, tt)

            def front_load(c, tt):
                t = c * 4 + tt
                r = t % 2
                sc.op("sp", lambda e, t=t, r=r: e.dma_start(out=xt[:, r, :], in_=x_d[t * 128:(t + 1) * 128, :]),
                      writes=[b_xt[r]], dma=True)

            def front_norm(c, tt):
                if True:
                    t = c * 4 + tt
                    r = t % 2
                    sumsq(tt, xt[:, r, :], [b_xt[r]])
                    rstd_ops(tt)
                    sc.op("dve", lambda e, r=r, tt=tt: e.tensor_scalar(out=xn[:, r, :], in0=xt[:, r, :], scalar1=rs_t[:, tt:tt + 1],
                                                                        scalar2=None, op0=ALU.mult),
                          reads=[b_xt[r], b_rs[tt]], writes=[b_xn[r]])

            def front_trans(c, tt):
                if True:
                    t = c * 4 + tt
                    r = t % 2

                    def fnT(e, r=r):
                        ins = None
                        for kc in range(8):
                            ins = e.transpose(psT[:, kc, :], xn[:, r, kc * 128:(kc + 1) * 128], ident[:])
                        return ins
                    sc.op("pe", fnT, reads=[b_xn[r], b_ident], writes=[b_psT[0], b_psT[1]])
                    for kc in range(8):
                        eng = "dve" if kc % 2 == 0 else "act"
                        if eng == "dve":
                            sc.op("dve", lambda e, kc=kc, tt=tt: e.tensor_scalar(
                                out=hT[:, kc, tt * 128:(tt + 1) * 128], in0=psT[:, kc, :], scalar1=modv[:, 1, kc:kc + 1],
                                scalar2=modv[:, 0, kc:kc + 1], op0=ALU.mult, op1=ALU.add),
                                reads=[b_psT[kc // 4], b_modv], writes=[b_hT])
                        else:
                            sc.op("act", lambda e, kc=kc, tt=tt: e.activation(
                                out=hT[:, kc, tt * 128:(tt + 1) * 128], in_=psT[:, kc, :], func=AF.Identity,
                                bias=modv[:, 0, kc:kc + 1], scale=modv[:, 1, kc:kc + 1]),
                                reads=[b_psT[kc // 4], b_modv], writes=[b_hT])

            def proj(c):
                par = c % 2
                n16, cc = c // 4, c % 4
                for (v, tt_, bt_) in ((0, tabC, b_tabC), (1, tabS, b_tabS)):
                    for p0 in (0, 8, 64, 72):
                        sc.op("sp", lambda e, v=v, tt_=tt_, p0=p0: e.dma_start(out=tt_[p0:p0 + 8, :], in_=tab_d[v, :, c * CH:(c + 1) * CH]),
                              reads=[b_tabd[v]], writes=[bt_], dma=True)

                def mm_fm(pt, col0):
                    def fn(e):
                        ins = None
                        for kc in range(8):
                            ins = e.matmul(pt[:], lhsT=win[:, kc, col0:col0 + 128], rhs=hT[:, kc, :], start=(kc == 0), stop=(kc == 7))
                        return ins
                    return fn
                for ch in range(2):
                    pt, bp = next_ring()
                    sc.op("pe", mm_fm(pt, ch * 128), reads=b_win + [b_hT], writes=[bp])
                    sc.op("act", lambda e, pt=pt, ch=ch: e.activation(out=ubuf[:, ch, 16:528], in_=pt[:], func=AF.Copy),
                          reads=[bp], writes=[b_ubuf])
                qk_list = [(kind, p) for kind in range(2) for p in range(6)]
                pend = None

                def qk_stage2(kind, p, ri):
                    pt2, bp2 = next_ring()
                    sc.op("pe", lambda e, pt2=pt2, ri=ri: e.matmul(pt2[:], lhsT=swapm[:], rhs=qb[:, ri, :], start=True, stop=True),
                          reads=[b_swap, b_qb[ri]], writes=[bp2])
                    sc.op("dve", lambda e, pt2=pt2, ri=ri: e.tensor_tensor(out=t1[:, ri, :], in0=pt2[:], in1=tabS[:], op=ALU.mult),
                          reads=[bp2, b_tabS], writes=[b_t1[ri]])
                    sc.op("pool", lambda e, ri=ri: e.tensor_tensor(out=t2[:, ri, :], in0=qb[:, ri, :], in1=tabC[:], op=ALU.mult),
                          reads=[b_qb[ri], b_tabC], writes=[b_t2[ri]])
                    if kind == 0:
                        for hq in range(2):
                            prt = slice(hq * 64, hq * 64 + 64)
                            sc.op("dve", lambda e, ri=ri, prt=prt, hq=hq: e.tensor_tensor(out=QT[prt, hq, p, :], in0=t1[prt, ri, :],
                                                                                          in1=t2[prt, ri, :], op=ALU.add),
                                  reads=[b_t1[ri], b_t2[ri]], writes=[b_QT])
                        return
                    elif p < 4:
                        dst, bd = KT01[:, p, par * CH:(par + 1) * CH], [b_KT01[par]]
                    else:
                        dst, bd = KT2[:, p - 4, c * CH:(c + 1) * CH], [b_KT2[c]]
                    sc.op("dve", lambda e, ri=ri, dst=dst: e.tensor_tensor(out=dst, in0=t1[:, ri, :], in1=t2[:, ri, :], op=ALU.add),
                          reads=[b_t1[ri], b_t2[ri]], writes=bd)

                for qi, (kind, p) in enumerate(qk_list):
                    col0 = 256 + kind * 768 + p * 128
                    pt, bp = next_ring()
                    sc.op("pe", mm_fm(pt, col0), reads=b_win + [b_hT], writes=[bp])
                    ri = qi % 2
                    scl = 0.125 if kind == 0 else 1.0
                    sc.op("act", lambda e, pt=pt, ri=ri, scl=scl: e.activation(out=qb[:, ri, :], in_=pt[:], func=AF.Identity, scale=scl),
                          reads=[bp], writes=[b_qb[ri]])
                    if pend is not None:
                        qk_stage2(*pend)
                    pend = (kind, p, ri)
                qk_stage2(*pend)
                for g in range(2):
                    for bi in range(4):
                        pt, bp = next_ring()
                        tok = slice(bi * 128, (bi + 1) * 128) if g == 0 else slice(bi, CH, 4)
                        col0 = 1792 + g * 256
                        slot = g * 8 + par * 4 + bi

                        def fn(e, pt=pt, tok=tok, col0=col0):
                            ins = None
                            for kc in range(8):
                                ins = e.matmul(pt[:, 0:256], lhsT=hT[:, kc, tok], rhs=win[:, kc, col0:col0 + 256],
                                               start=(kc == 0), stop=(kc == 7))
                            return ins
                        sc.op("pe", fn, reads=b_win + [b_hT], writes=[bp])
                        eng = "act" if bi % 2 == 0 else "dve"
                        if eng == "act":
                            sc.op("act", lambda e, pt=pt, slot=slot: e.activation(out=V3[:, slot * 4:(slot + 1) * 4, 0:64],
                                                                                   in_=pt[:, 0:256].rearrange("p (h d) -> p h d", h=4),
                                                                                   func=AF.Copy), reads=[bp], writes=[b_V01[par]])
                        else:
                            sc.op("dve", lambda e, pt=pt, slot=slot: e.tensor_copy(out=V3[:, slot * 4:(slot + 1) * 4, 0:64],
                                                                                    in_=pt[:, 0:256].rearrange("p (h d) -> p h d", h=4)),
                                  reads=[bp], writes=[b_V01[par]])
                for r0 in range(0, 16, 2):
                    pt, bp = next_ring()

                    def fn(e, pt=pt, r0=r0):
                        ins = None
                        for dr in range(2):
                            for kc in range(8):
                                ins = e.matmul(pt[cc * 32:(cc + 1) * 32, dr * 256:(dr + 1) * 256], lhsT=hT[:, kc, r0 + dr:CH:16],
                                               rhs=win[:, kc, 2304:2560], start=(kc == 0), stop=(kc == 7),
                                               tile_position=(0, cc * 32))
                        return ins
                    sc.op("pe", fn, reads=b_win + [b_hT], writes=[bp])
                    slot = 16 + n16 * 16 + r0
                    eng = "act" if (r0 // 2) % 2 == 0 else "dve"
                    if eng == "act":
                        sc.op("act", lambda e, pt=pt, slot=slot: e.activation(out=V3[cc * 32:(cc + 1) * 32, slot * 4:(slot + 2) * 4, 0:64],
                                                                               in_=pt[cc * 32:(cc + 1) * 32, :].rearrange("p (h d) -> p h d", h=8), func=AF.Copy),
                              reads=[bp], writes=[b_V2[c]])
                    else:
                        sc.op("dve", lambda e, pt=pt, slot=slot: e.tensor_copy(out=V3[cc * 32:(cc + 1) * 32, slot * 4:(slot + 2) * 4, 0:64],
                                                                                in_=pt[cc * 32:(cc + 1) * 32, :].rearrange("p (h d) -> p h d", h=8)),
                              reads=[bp], writes=[b_V2[c]])
                sc.op("pool", lambda e: e.tensor_tensor(out=s2b[:, :, 1:528], in0=ubuf[:, :, 1:528], in1=ubuf[:, :, 0:527], op=ALU.add),
                      reads=[b_ubuf], writes=[b_s2])
                sc.op("pool", lambda e: e.tensor_tensor(out=s4b[:, :, 3:528], in0=s2b[:, :, 3:528], in1=s2b[:, :, 1:526], op=ALU.add),
                      reads=[b_s2], writes=[b_s4])
                sc.op("pool", lambda e: e.tensor_tensor(out=s8b[:, 7:528], in0=s4b[:, 1, 7:528], in1=s4b[:, 1, 3:524], op=ALU.add),
                      reads=[b_s4], writes=[b_s8])
                sc.op("pool", lambda e: e.tensor_tensor(out=s16b[64:128, 15:528], in0=s8b[64:128, 15:528], in1=s8b[64:128, 7:520], op=ALU.add),
                      reads=[b_s8], writes=[b_s16])
                srcs = [(s2b, 0, slice(0, 64), b_s2, 0), (s4b, 0, slice(64, 128), b_s4, 0),
                        (s8b, None, slice(0, 64), b_s8, 1), (s16b, None, slice(64, 128), b_s16, 1)]
                for (stile, sidx, prt, bs, ch) in srcs:
                    sap = (stile[prt, sidx, 16:528] if sidx is not None else stile[prt, 16:528])
                    if c == 0:
                        sap16 = (stile[prt, sidx, 16:32] if sidx is not None else stile[prt, 16:32])
                        sc.op("pool", lambda e, sap16=sap16, prt=prt, ch=ch: e.tensor_tensor(out=sap16, in0=sap16,
                                                                                               in1=poolc[prt, 2 + ch * 16:2 + ch * 16 + 16], op=ALU.mult),
                              reads=[bs, b_poolc], writes=[bs])
                    sc.op("dve", lambda e, sap=sap, prt=prt, ch=ch: e.scalar_tensor_tensor(
                        out=mixed[prt, ch, :], in0=sap, scalar=poolc[prt, ch:ch + 1], in1=ubuf[prt, ch, 16:528],
                        op0=ALU.mult, op1=ALU.subtract), reads=[bs, b_poolc, b_ubuf], writes=[b_mixed])
                sc.op("pool", lambda e: e.tensor_copy(out=ubuf[:, :, 0:16], in_=ubuf[:, :, 512:528]), reads=[b_ubuf, b_s2, b_mixed], writes=[b_ubuf])
            def pool_mm(c):
                for ch in range(2):
                    pt, bp = next_ring()
                    sc.op("pe", lambda e, pt=pt, ch=ch: e.matmul(pt[:], lhsT=wpbd[:, ch, :], rhs=mixed[:, ch, :], start=True, stop=True),
                          reads=[b_wpbd, b_mixed], writes=[bp])
                    sc.op("dve", lambda e, pt=pt, ch=ch: e.tensor_scalar(out=catP[:, ch, :], in0=pt[:], scalar1=bpool[:, ch:ch + 1],
                                                                          scalar2=pscale[:, ch:ch + 1], op0=ALU.add, op1=ALU.mult),
                          reads=[bp, b_bpool, b_pscale], writes=[b_catP])

            pt_i = {"i": 0}

            def attn(c):
                par = c % 2
                n16, cc = c // 4, c % 4
                norm_pending = []
                for sp_ in range(2):
                    first = [True, True]
                    items = []
                    for tt in range(4):
                        n = c * 4 + tt
                        tiles = []
                        if n > 0:
                            pn = n - 1
                            tiles.append((0, KT01[:, sp_, (pn % 8) * 128:(pn % 8) * 128 + 128], (pn % 8), 128, [b_KT01[(pn // 4) % 2]], [b_V01[(pn // 4) % 2]]))
                        tiles.append((1, KT01[:, sp_, (n % 8) * 128:(n % 8) * 128 + 128], (n % 8), 128, [b_KT01[par]], [b_V01[par]]))
                        items.append(dict(q=(sp_, slice(tt * 128, (tt + 1) * 128)), nq=128, tiles=tiles, mcol=0,
                                          outsl=slice(tt * 128, (tt + 1) * 128)))
                    for r in range(4):
                        tiles = []
                        if c > 0:
                            pp = 1 - par
                            tiles.append((0, KT01[:, 2 + sp_, pp * CH + r:(pp + 1) * CH:4], 8 + pp * 4 + r, 128, [b_KT01[pp]], [b_V01[pp]]))
                        tiles.append((1, KT01[:, 2 + sp_, par * CH + r:(par + 1) * CH:4], 8 + par * 4 + r, 128, [b_KT01[par]], [b_V01[par]]))
                        items.append(dict(q=(2 + sp_, slice(r, CH, 4)), nq=128, tiles=tiles, mcol=0, outsl=slice(r, CH, 4)))
                    for r in range(16):
                        tiles = []
                        if n16 > 0:
                            tiles.append((0, KT2[:, sp_, r:2048:16], 16 + r, 128, b_KT2[0:4], b_V2[0:4]))
                        kp = (cc + 1) * 32
                        tiles.append((1, KT2[:, sp_, n16 * 2048 + r:n16 * 2048 + kp * 16:16], 16 + n16 * 16 + r, kp,
                                      b_KT2[n16 * 4:c + 1], b_V2[n16 * 4:c + 1]))
                        items.append(dict(q=(4 + sp_, slice(r, CH, 16)), nq=32, tiles=tiles, mcol=cc * 32, outsl=slice(r, CH, 16)))
                    if ATT_LIMIT is not None:
                        items = items[:ATT_LIMIT]
                    batches, cur, sig = [], [], None
                    for it in items:
                        isig = (it["nq"], tuple((tl[0], tl[3]) for tl in it["tiles"]), it["mcol"])
                        cap = 512 // (2 * it["nq"])
                        if cur and (isig != sig or len(cur) >= cap):
                            batches.append(cur)
                            cur = []
                        sig = isig
                        cur.append(it)
                    if cur:
                        batches.append(cur)

                    def stage_pv(ctx):
                        batch, pi, colof, nq = ctx
                        fl = [first[0], first[1]]
                        first[0] = first[1] = False
                        vb = set()
                        for it in batch:
                            for tl in it["tiles"]:
                                vb.update(tl[5])

                        def fnV(e, batch=batch, pi=pi, fl=fl, colof=colof, nq=nq, sp_=sp_):
                            ins = None
                            fl = list(fl)
                            for it in batch:
                                for tl in it["tiles"]:
                                    kidx, _, slot, kp, _, _ = tl
                                    for hh in range(2):
                                        col = hh * 512 + colof(kidx, it["ii"])
                                        ins = e.matmul(acc[hh][0:65, it["outsl"]], lhsT=V3[0:kp, slot * 4 + sp_ * 2 + hh, :],
                                                       rhs=PT[0:kp, pi, col:col + nq], start=fl[hh], stop=False, skip_group_check=True)
                                        fl[hh] = False
                            return ins
                        sc.op("pe", fnV, reads=[b_PT[pi], b_Vones] + list(vb), writes=[b_acc[0], b_acc[1]])

                    pend_pv = []
                    for batch in batches:
                        pts = [next_ring(), next_ring()]
                        pi = pt_i["i"] % 3
                        pt_i["i"] += 1
                        ni = len(batch)
                        nq = batch[0]["nq"]
                        mc = batch[0]["mcol"]
                        tl0 = batch[0]["tiles"]
                        kidxs = [tl[0] for tl in tl0]
                        kps = {tl[0]: tl[3] for tl in tl0}
                        kb = set()
                        for ii, it in enumerate(batch):
                            it["ii"] = ii
                            for tl in it["tiles"]:
                                kb.update(tl[4])

                        def colof(kidx, ii, ni=ni, nq=nq):
                            return kidx * (ni * nq) + ii * nq

                        def fnS(e, batch=batch, pts=pts, colof=colof, nq=nq):
                            ins = None
                            for hh in range(2):
                                pt = pts[hh][0]
                                for it in batch:
                                    for tl in it["tiles"]:
                                        kidx, kap, _, kp, _, _ = tl
                                        col = colof(kidx, it["ii"])
                                        ins = e.matmul(pt[0:kp, col:col + nq], lhsT=kap, rhs=QT[:, hh, it["q"][0], it["q"][1]],
                                                       start=True, stop=True)
                            return ins
                        sc.op("pe", fnS, reads=[b_QT] + list(kb), writes=[pts[0][1], pts[1][1]])
                        for hh in range(2):
                            pt, bp = pts[hh]
                            hb = hh * 512
                            if ni == 1 and len(kidxs) == 2:
                                sc.op("act", lambda e, pt=pt, pi=pi, hb=hb: e.activation(out=PT[:, pi, hb:hb + 256], in_=pt[:, 0:256], func=AF.Exp),
                                      reads=[bp], writes=[b_PT[pi]])
                                sc.op("dve", lambda e, pi=pi, hb=hb: e.tensor_tensor(
                                    out=PT[:, pi, hb:hb + 256].rearrange("p (k q) -> p k q", k=2),
                                    in0=PT[:, pi, hb:hb + 256].rearrange("p (k q) -> p k q", k=2),
                                    in1=maskt[:, :, 0, :], op=ALU.mult), reads=[b_PT[pi], b_mask], writes=[b_PT[pi]])
                            else:
                                for kidx in kidxs:
                                    kp = kps[kidx]
                                    base = kidx * ni * nq
                                    wdt = ni * nq
                                    sc.op("act", lambda e, pt=pt, pi=pi, kp=kp, base=base, wdt=wdt, hb=hb: e.activation(
                                        out=PT[0:kp, pi, hb + base:hb + base + wdt], in_=pt[0:kp, base:base + wdt], func=AF.Exp),
                                        reads=[bp], writes=[b_PT[pi]])
                                    if ni == 1:
                                        sc.op("dve", lambda e, pi=pi, kp=kp, base=base, wdt=wdt, kidx=kidx, mc=mc, nq=nq, hb=hb: e.tensor_tensor(
                                            out=PT[0:kp, pi, hb + base:hb + base + wdt], in0=PT[0:kp, pi, hb + base:hb + base + wdt],
                                            in1=maskt[0:kp, kidx, 0, mc:mc + nq], op=ALU.mult),
                                            reads=[b_PT[pi], b_mask], writes=[b_PT[pi]])
                                    else:
                                        mk = bass.AP(maskt, kidx * 256 + mc, [[512, kp], [0, ni], [1, nq]])
                                        sc.op("dve", lambda e, pi=pi, kp=kp, base=base, wdt=wdt, mk=mk, ni=ni, hb=hb: e.tensor_tensor(
                                            out=PT[0:kp, pi, hb + base:hb + base + wdt].rearrange("p (i q) -> p i q", i=ni),
                                            in0=PT[0:kp, pi, hb + base:hb + base + wdt].rearrange("p (i q) -> p i q", i=ni),
                                            in1=mk, op=ALU.mult), reads=[b_PT[pi], b_mask], writes=[b_PT[pi]])
                        pend_pv.append((batch, pi, colof, nq))
                        if norm_pending and len(pend_pv) >= 2:
                            norm_pending.pop(0)()
                        if len(pend_pv) >= 2:
                            stage_pv(pend_pv.pop(0))
                    while pend_pv:
                        stage_pv(pend_pv.pop(0))
                    for hh in range(2):
                        sc.op("act", lambda e, hh=hh: e.activation(out=rden[64:65, hh, :], in_=acc[hh][64:65, :], func=AF.Ln),
                              reads=[b_acc[hh]], writes=[b_rdrow[hh]])
                        sc.op("act", lambda e, hh=hh: e.activation(out=rden[64:65, hh, :], in_=rden[64:65, hh, :], func=AF.Exp, scale=-1.0),
                              reads=[b_rdrow[hh]], writes=[b_rdrow[hh]])

                    def norm_b(sp_=sp_):
                        for hh in range(2):
                            pt, bp = next_ring()
                            sc.op("pe", lambda e, hh=hh, pt=pt: e.matmul(pt[0:64, :], lhsT=onesf[64:65, 0:64], rhs=rden[64:65, hh, :],
                                                                          start=True, stop=True),
                                  reads=[b_rdrow[hh], b_onesf], writes=[bp])
                            sc.op("act", lambda e, hh=hh, pt=pt: e.activation(out=rden[0:64, hh, :], in_=pt[0:64, :], func=AF.Copy),
                                  reads=[bp], writes=[b_rden[hh]])
                            sc.op("dve", lambda e, hh=hh, sp_=sp_: e.tensor_tensor(out=attT[0:64, sp_ * 2 + hh, :], in0=acc[hh][0:64, :],
                                                                                    in1=rden[0:64, hh, :], op=ALU.mult),
                                  reads=[b_acc[hh], b_rden[hh]], writes=[b_attT])
                    norm_pending.append(norm_b)
                return norm_pending

            def _rows(ap, hh):
                return ap[hh * 64:(hh + 1) * 64]

            def epilogue(t, r, gi, x_src_d, x_src_bufs, dst_d, dst_buf, b_xo_, xo_, b_tmp, tmp_, ssi, psYp):
                psY, psYb = psYp
                sc.op("sp", lambda e: e.dma_start(out=xo_[:, r, :], in_=x_src_d[t * 128:(t + 1) * 128, :]),
                      reads=list(x_src_bufs), writes=[b_xo_[r]], dma=True)
                sumsq(ssi, psY, psYb)
                rstd_ops(ssi)
                sc.op("dve", lambda e: e.scalar_tensor_tensor(out=tmp_[:], in0=psY, scalar=rs_t[:, ssi:ssi + 1], in1=Gt[:, gi, :],
                                                              op0=ALU.mult, op1=ALU.mult),
                      reads=psYb + [b_rs[ssi], b_Gt], writes=[b_tmp])
                sc.op("pool", lambda e: e.tensor_tensor(out=xo_[:, r, :], in0=xo_[:, r, :], in1=tmp_[:], op=ALU.add),
                      reads=[b_tmp, b_xo_[r]], writes=[b_xo_[r]])
                sc.op("sp", lambda e: e.dma_start(out=dst_d[t * 128:(t + 1) * 128, :], in_=xo_[:, r, :]),
                      reads=[b_xo_[r]], writes=[dst_buf], dma=True)

            def outproj_pool(c, tt):
                tok = slice(tt * 128, (tt + 1) * 128)
                psYp = psY_a if tt % 2 == 0 else psY_c
                psY = psYp[0]

                def fn(e, tok=tok, psY=psY):
                    ins = None
                    for nh in range(2):
                        ns = slice(nh * 512, (nh + 1) * 512)
                        for ch in range(2):
                            ins = e.matmul(psY[:, ns], lhsT=catP[:, ch, tok], rhs=wout_p[:, ch, ns], start=(ch == 0), stop=False)
                    return ins
                sc.op("pe", fn, reads=[b_catP, b_woutp], writes=psYp[1])

            def outproj_attn(c, tt):
                t = c * 4 + tt
                tok = slice(tt * 128, (tt + 1) * 128)
                psYp = psY_a if tt % 2 == 0 else psY_c
                psY = psYp[0]

                def fn(e, tok=tok, psY=psY):
                    ins = None
                    for nh in range(2):
                        ns = slice(nh * 512, (nh + 1) * 512)
                        for s_ in range(4):
                            ins = e.matmul(psY[:, ns], lhsT=attT[:, s_, tok], rhs=wout_a[:, s_, ns], start=False, stop=(s_ == 3))
                    return ins
                sc.op("pe", fn, reads=[b_attT, b_wouta], writes=psYp[1])
                epilogue(t, t % 2, 0, x_d, [], x1_d, b_x1d[t], b_xo, xo, b_tmpy, tmpy, 4 + (tt % 4), psYp)

            front(0)
            checkpoint(2, [(hT[:, 0, :], [b_hT], 512, 128), (hT[:, 7, :], [b_hT], 512, 128)])
            win_addr = nc.lookup_mloc(win).addr
            b_wup_pre = [Buf(f"wupq{q}") for q in range(10)]
            wup_v = wup_d.rearrange("(k p) n -> p k n", p=128)
            winflat = win[:, :, :].rearrange("p k n -> p (k n)")

            def wup_cols(q):
                c0 = (q % 2) * DFF + (q // 2) * 256
                return slice(c0, c0 + 256)

            for c in range(NCH):
                proj(c)
                if c == NCH - 1:
                    for q in range(10):
                        dst = winflat[:, q * 2048:(q + 1) * 2048].rearrange("p (k n) -> p k n", k=8)
                        sc.op("pool", lambda e, q=q, dst=dst: e.dma_start(out=dst, in_=wup_v[:, :, wup_cols(q)]),
                              writes=b_win + [b_wup_pre[q]], dma=True)
                if c == 0:
                    checkpoint(3, [(QT[:, 0, 0, :], [b_QT], 512, 128), (KT01[:, 0, 0:512], [b_KT01[0]], 512, 128), (Vall[:, 0:1024], [b_V01[0]], 1024, 128),
                                   (catP[:, 0, :], [b_catP], 512, 128), (QT[:, 1, 5, :], [b_QT], 512, 128), (KT2[:, 1, 0:512], [b_KT2[0]], 512, 128),
                                   (Vall[0:32, 16 * 256:20 * 256], [b_V2[0]], 1024, 32), (catP[:, 1, :], [b_catP], 512, 128), (tabC[:], [b_tabC], 512, 128), (tabS[:], [b_tabS], 512, 128)])
                normp = attn(c)
                if c == 0:
                    checkpoint(4, [(attT[0:64, 0, :], [b_attT], 512, 64), (attT[0:64, 1, :], [b_attT], 512, 64), (attT[0:64, 2, :], [b_attT], 512, 64), (attT[0:64, 3, :], [b_attT], 512, 64)])
                pool_mm(c)
                outproj_pool(c, 0)
                ring["banks"] = [0]
                while normp:
                    normp.pop(0)()
                outproj_pool(c, 1)
                if c + 1 < NCH:
                    front_load(c + 1, 0)
                    front_load(c + 1, 1)
                    front_norm(c + 1, 0)
                    front_load(c + 1, 2)
                for tt in range(4):
                    if c + 1 < NCH and tt + 1 < 4:
                        front_norm(c + 1, tt + 1)
                        if tt + 3 < 4:
                            front_load(c + 1, tt + 3)
                    if c + 1 < NCH:
                        front_trans(c + 1, tt)
                    if tt >= 2:
                        outproj_pool(c, tt)
                    outproj_attn(c, tt)
                ring["banks"] = [0, 1, 2, 3, 4]
        fence1 = sc.all_tokens()

        stF = contextlib.ExitStack()

        def sbF(name, shape, dt):
            return stF.enter_context(nc.sbuf_tensor("sf_" + name, list(shape), dt))

        with stF:
            def fb(name):
                return Buf(name, init=fence1)
            wup = sbF("wup", [128, 22, 8, 256], BF16)
            assert nc.lookup_mloc(wup).addr == win_addr, "wup must alias the w_in region for the early prefetch"
            wdn = sbF("wdn", [128, NJ, D], BF16)
            convw = sbF("convw", [128, NJ, 3], F32)
            convb = sbF("convb", [128, NJ], F32)
            xf = sbF("xf", [128, 2, D], F32)
            xnf = sbF("xnf", [128, 2, D], BF16)
            h2T = sbF("h2T", [128, 8, CH], BF16)
            aT = sbF("aT", [128, NJ, CH], BF16)
            gS = sbF("gS", [128, 2, 514], F32)
            tcv = sbF("tcv", [128, 2, CH], F32)
            ge = sbF("ge", [128, 2, CH], BF16)
            vS = sbF("vS", [128, 2, CH], BF16)
            halo = sbF("halo", [128, NJ, 2], F32)
            xo2 = sbF("xo2", [128, 2, D], F32)
            tmp2 = sbF("tmp2", [128, D], F32)
            b_wup = [b_wup_pre[q] if q < 10 else fb(f"wup{q}") for q in range(22)]
            b_wdn = [fb(f"wdn{i}") for i in range(11)]
            b_convw, b_convb = fb("convw"), fb("convb")
            b_xf = [fb("xf0"), fb("xf1")]
            b_xnf, b_h2T, b_aT = [fb("xnf0"), fb("xnf1")], fb("h2T"), fb("aT")
            b_gS = [fb("gS0"), fb("gS1")]
            b_tcv = [fb("tcv0"), fb("tcv1")]
            b_ge = [fb("ge0"), fb("ge1")]
            sink["ap"], sink["buf"] = ge[:, :, :].rearrange("p a b -> p (a b)"), b_ge
            b_vS = [fb("vS0"), fb("vS1")]
            b_halo = [fb(f"halo{j}") for j in range(NJ)]
            b_xo2 = [fb("xo2_0"), fb("xo2_1")]
            b_tmp2 = fb("tmp2")
            b_outd = [Buf(f"outd{t}") for t in range(32)]

            wdn_v = wdn_d.rearrange("(j p) n -> p j n", p=128)
            sc.op("sp", lambda e: e.dma_start(out=convw[:], in_=convw_d), writes=[b_convw], dma=True)
            sc.op("sp", lambda e: e.dma_start(out=convb[:], in_=convb_d), writes=[b_convb], dma=True)
            sc.op("dve", lambda e: e.memset(halo[:], 0.0), writes=b_halo)
            for i in range(11):
                for q in (2 * i, 2 * i + 1):
                    if q >= 10:
                        sc.op("pool", lambda e, q=q: e.dma_start(out=wup[:, q, :, :], in_=wup_v[:, :, wup_cols(q)]),
                              writes=[b_wup[q]], dma=True)
            for i in range(11):
                sc.op("pool", lambda e, i=i: e.dma_start(out=wdn[:, 2 * i:2 * i + 2, :], in_=wdn_v[:, 2 * i:2 * i + 2, :]),
                      writes=[b_wdn[i]], dma=True)

            def frontF(c):
                frontF_load(c, 0)
                frontF_load(c, 1)
                for tt in range(4):
                    frontF_norm(c, tt)
                    if tt + 2 < 4:
                        frontF_load(c, tt + 2)
                    frontF_trans(c, tt)

            def frontF_load(c, tt):
                t = c * 4 + tt
                r = t % 2
                sc.op("sp", lambda e, t=t, r=r: e.dma_start(out=xf[:, r, :], in_=x1_d[t * 128:(t + 1) * 128, :]),
                      reads=[b_x1d[t]], writes=[b_xf[r]], dma=True)

            def frontF_norm(c, tt):
                if True:
                    t = c * 4 + tt
                    r = t % 2
                    sumsq(tt, xf[:, r, :], [b_xf[r]])
                    rstd_ops(tt)
                    sc.op("dve", lambda e, r=r, tt=tt: e.tensor_scalar(out=xnf[:, r, :], in0=xf[:, r, :], scalar1=rs_t[:, tt:tt + 1],
                                                                        scalar2=None, op0=ALU.mult),
                          reads=[b_xf[r], b_rs[tt]], writes=[b_xnf[r]])

            def frontF_trans(c, tt):
                if True:
                    r = (c * 4 + tt) % 2

                    def fnT(e, r=r):
                        ins = None
                        for kc in range(8):
                            ins = e.transpose(psT[:, kc, :], xnf[:, r, kc * 128:(kc + 1) * 128], ident[:])
                        return ins
                    sc.op("pe", fnT, reads=[b_xnf[r], b_ident], writes=[b_psT[0], b_psT[1]])
                    for kc in range(8):
                        if kc % 2 == 0:
                            sc.op("dve", lambda e, kc=kc, tt=tt: e.tensor_scalar(
                                out=h2T[:, kc, tt * 128:(tt + 1) * 128], in0=psT[:, kc, :], scalar1=modv[:, 3, kc:kc + 1],
                                scalar2=modv[:, 2, kc:kc + 1], op0=ALU.mult, op1=ALU.add),
                                reads=[b_psT[kc // 4], b_modv], writes=[b_h2T])
                        else:
                            sc.op("act", lambda e, kc=kc, tt=tt: e.activation(
                                out=h2T[:, kc, tt * 128:(tt + 1) * 128], in_=psT[:, kc, :], func=AF.Identity,
                                bias=modv[:, 2, kc:kc + 1], scale=modv[:, 3, kc:kc + 1]),
                                reads=[b_psT[kc // 4], b_modv], writes=[b_h2T])

            def up(c):
                for j in range(NJ):
                    up_s1(j)
                    if j > 0:
                        up_s2(j - 1)
                up_s2(NJ - 1)

            def up_s1(j):
                if True:
                    ri = j % 2
                    ptg, bpg = next_ring()

                    def fng(e, ptg=ptg, j=j):
                        ins = None
                        for kc in range(8):
                            ins = e.matmul(ptg[:], lhsT=wup[:, 2 * (j // 2), kc, (j % 2) * 128:(j % 2) * 128 + 128], rhs=h2T[:, kc, :],
                                           start=(kc == 0), stop=(kc == 7))
                        return ins
                    sc.op("pe", fng, reads=[b_wup[2 * (j // 2)], b_h2T], writes=[bpg])
                    ptv, bpv = next_ring()

                    def fnv(e, ptv=ptv, j=j):
                        ins = None
                        for kc in range(8):
                            ins = e.matmul(ptv[:], lhsT=wup[:, 2 * (j // 2) + 1, kc, (j % 2) * 128:(j % 2) * 128 + 128], rhs=h2T[:, kc, :],
                                           start=(kc == 0), stop=(kc == 7))
                        return ins
                    sc.op("pe", fnv, reads=[b_wup[2 * (j // 2) + 1], b_h2T], writes=[bpv])
                    sc.op("pool", lambda e, ri=ri, j=j: e.tensor_copy(out=gS[:, ri, 0:2], in_=halo[:, j, :]),
                          reads=[b_halo[j]], writes=[b_gS[ri]])
                    sc.op("act", lambda e, ri=ri, ptg=ptg: e.activation(out=gS[:, ri, 2:514], in_=ptg[:], func=AF.Copy),
                          reads=[bpg], writes=[b_gS[ri]])
                    sc.op("act", lambda e, ri=ri, ptv=ptv: e.activation(out=vS[:, ri, :], in_=ptv[:], func=AF.Copy),
                          reads=[bpv], writes=[b_vS[ri]])
                    sc.op("pool", lambda e, ri=ri, j=j: e.tensor_copy(out=halo[:, j, :], in_=gS[:, ri, 512:514]),
                          reads=[b_gS[ri]], writes=[b_halo[j]])
                    sc.op("dve", lambda e, ri=ri, j=j: e.tensor_scalar(out=tcv[:, ri, :], in0=gS[:, ri, 2:514], scalar1=convw[:, j, 2:3],
                                                                        scalar2=convb[:, j:j + 1], op0=ALU.mult, op1=ALU.add),
                          reads=[b_gS[ri], b_convw, b_convb], writes=[b_tcv[ri]])
                    sc.op("dve", lambda e, ri=ri, j=j: e.scalar_tensor_tensor(out=tcv[:, ri, :], in0=gS[:, ri, 1:513], scalar=convw[:, j, 1:2],
                                                                                in1=tcv[:, ri, :], op0=ALU.mult, op1=ALU.add),
                          reads=[b_gS[ri], b_convw, b_tcv[ri]], writes=[b_tcv[ri]])
                    sc.op("dve", lambda e, ri=ri, j=j: e.scalar_tensor_tensor(out=tcv[:, ri, :], in0=gS[:, ri, 0:512], scalar=convw[:, j, 0:1],
                                                                               in1=tcv[:, ri, :], op0=ALU.mult, op1=ALU.add),
                          reads=[b_gS[ri], b_convw, b_tcv[ri]], writes=[b_tcv[ri]])
            def up_s2(j):
                if True:
                    ri = j % 2
                    sc.op("act", lambda e, ri=ri: e.activation(out=ge[:, ri, :], in_=tcv[:, ri, :], func=AF.Gelu_apprx_tanh),
                          reads=[b_tcv[ri]], writes=[b_ge[ri]])
                    sc.op("dve", lambda e, ri=ri, j=j: e.tensor_tensor(out=aT[:, j, :], in0=vS[:, ri, :], in1=ge[:, ri, :], op=ALU.mult),
                          reads=[b_vS[ri], b_ge[ri]], writes=[b_aT])

            def down_tile(c, tt):
                if True:
                    t = c * 4 + tt
                    tok = slice(tt * 128, (tt + 1) * 128)

                    psYp = psY_a if t % 2 == 0 else psY_b
                    psY = psYp[0]

                    def fn(e, tok=tok, psY=psY):
                        ins = None
                        for nh in range(2):
                            ns = slice(nh * 512, (nh + 1) * 512)
                            for j in range(NJ):
                                ins = e.matmul(psY[:, ns], lhsT=aT[:, j, tok], rhs=wdn[:, j, ns], start=(j == 0), stop=(j == NJ - 1))
                        return ins
                    sc.op("pe", fn, reads=[b_aT] + b_wdn, writes=psYp[1])
                    epilogue(t, t % 2, 1, x1_d, [b_x1d[t]], out_d, b_outd[t], b_xo2, xo2, b_tmp2, tmp2, 4 + (tt % 4), psYp)

            ring["banks"] = [0, 1, 2]
            frontF(0)
            for c in range(NCH):
                up(c)
                if c + 1 < NCH:
                    frontF_load(c + 1, 0)
                    frontF_load(c + 1, 1)
                    frontF_norm(c + 1, 0)
                    frontF_load(c + 1, 2)
                for tt in range(4):
                    if c + 1 < NCH and tt + 1 < 4:
                        frontF_norm(c + 1, tt + 1)
                        if tt + 3 < 4:
                            frontF_load(c + 1, tt + 3)
                    if c + 1 < NCH:
                        frontF_trans(c + 1, tt)
                    down_tile(c, tt)
            sc.op("sp", None, reads=b_outd)

            emit(nc, sc, sem_eng, sem_dma)
    return nc


def emit(nc, sc, sem_eng, sem_dma):
    class _FirstWait:
        def __init__(self, eng, pending):
            self._eng = eng
            self._pending = pending

        def __getattr__(self, name):
            f = getattr(self._eng, name)

            def g(*a, **k):
                ins = f(*a, **k)
                if self._pending:
                    sem, val = self._pending.pop()
                    ins._wait_ge(sem, val)
                return ins
            return g

    def run_engine(ename, eng):
        seen = {}
        for (fn, waits, tok, dma) in sc.ops[ename]:
            todo = []
            for w in sorted(waits):
                key = (w[0], w[1], w[2])
                if seen.get(key, 0) >= w[3]:
                    continue
                seen[key] = w[3]
                sem = sem_eng[w[1]] if w[0] == "eng" else sem_dma[(w[1], w[2])]
                todo.append((sem, w[3]))
            embed = EMBED_WAIT and fn is not None and not dma and len(todo) > 0
            for (sem, val) in (todo[:-1] if embed else todo):
                eng.wait_ge(sem, val)
            if fn is None:
                continue
            if embed:
                pend = [todo[-1]]
                ins = fn(_FirstWait(eng, pend))
                assert not pend
            else:
                ins = fn(eng)
            if dma:
                ins.then_inc(sem_dma[(tok[1], tok[2])], 16)
            else:
                ins.then_inc(sem_eng[ename], 1)

    with nc.Block() as block:
        @block.sync
        def _(e):
            run_engine("sp", e)

        @block.tensor
        def _(e):
            run_engine("pe", e)

        @block.scalar
        def _(e):
            run_engine("act", e)

        @block.vector
        def _(e):
            run_engine("dve", e)

        @block.gpsimd
        def _(e):
            run_engine("pool", e)


def _consts():
    ident = np.eye(128, dtype=np.float32).astype(bf)
    sw = np.zeros((128, 128), np.float32)
    for hb in (0, 64):
        for d in range(8):
            sw[hb + d + 8, hb + d] = -1.0
            sw[hb + d, hb + d + 8] = 1.0
    inv_freq = (500000.0 ** (-np.arange(0, 16, 2, dtype=np.float32) / 16.0)).astype(np.float32)
    invf = np.zeros((128, 1), np.float32)
    for p in range(128):
        invf[p, 0] = inv_freq[p // 16]
    jj = np.arange(128)[:, None]
    ii = np.arange(128)[None, :]
    m_prev = (jj >= ii).astype(np.float32)
    m_cur = (jj <= ii).astype(np.float32)
    mask = np.zeros((128, 2, 2, 128), np.float32)
    for h in range(2):
        mask[:, 0, h, :] = m_prev
        mask[:, 1, h, :] = m_cur
    poolc = np.zeros((128, 34), np.float32)
    wins = {(0, 0): 2, (0, 1): 4, (1, 0): 8, (1, 1): 16}
    for ch in range(2):
        for half in range(2):
            w = wins[(ch, half)]
            prt = slice(half * 64, half * 64 + 64)
            poolc[prt, ch] = 1.0 / w
            for t in range(16):
                poolc[prt, 2 + ch * 16 + t] = w / min(t + 1, w)
    return dict(ident=ident, swapm=sw.astype(bf), invf=invf, mask=mask.astype(bf), poolc=poolc)


_NC_CACHE = {}


def _prep(x, c, positions, w_ada, b_ada, g_pre_mix, g_post_mix, g_pre_ffn, g_post_ffn,
          w_in, w_pool, b_pool, pool_scale, w_out, w_up, conv_w, conv_b, w_down):
    f32 = np.float32
    x = np.asarray(x, f32)
    c = np.asarray(c, f32)
    positions = np.asarray(positions, np.int32)
    w_ada = np.ascontiguousarray(np.asarray(w_ada, f32)[0])
    b_ada = np.asarray(b_ada, f32)[0]
    B = x.shape[0]
    consts = _consts()

    def pp(v):
        return np.ascontiguousarray(v.reshape(8, 128).T)
    bA = np.stack([pp(b_ada[0:D]), pp(b_ada[D:2 * D]), pp(b_ada[3 * D:4 * D]), pp(b_ada[4 * D:5 * D])], axis=1)
    bG = np.stack([np.broadcast_to(b_ada[2 * D:3 * D], (128, D)), np.broadcast_to(b_ada[5 * D:6 * D], (128, D))], axis=1)
    gpre = np.stack([pp(np.asarray(g_pre_mix, f32)[0]), pp(np.asarray(g_pre_ffn, f32)[0])], axis=1)
    gpost = np.stack([np.broadcast_to(np.asarray(g_post_mix, f32)[0], (128, D)),
                      np.broadcast_to(np.asarray(g_post_ffn, f32)[0], (128, D))], axis=1)
    bpool = np.ascontiguousarray(np.asarray(b_pool, f32)[0].reshape(2, 128).T)
    pscale = np.ascontiguousarray(np.asarray(pool_scale, f32)[0].reshape(2, 128).T)
    convw = np.ascontiguousarray(np.asarray(conv_w, f32)[0].T.reshape(NJ, 128, 3).transpose(1, 0, 2))
    convb = np.ascontiguousarray(np.asarray(conv_b, f32)[0].reshape(NJ, 128).T)
    shared = {
        "w_ada": w_ada, "bA": np.ascontiguousarray(bA), "bG": np.ascontiguousarray(bG), "gpre": np.ascontiguousarray(gpre),
        "gpost": np.ascontiguousarray(gpost), "w_in": np.ascontiguousarray(np.asarray(w_in, f32)[0]),
        "w_out": np.ascontiguousarray(np.asarray(w_out, f32)[0]), "w_up": np.ascontiguousarray(np.asarray(w_up, f32)[0]),
        "w_down": np.ascontiguousarray(np.asarray(w_down, f32)[0]), "w_pool": np.ascontiguousarray(np.asarray(w_pool, f32)[0]),
        "bpool": bpool, "pscale": pscale, "convw": convw, "convb": convb,
    }
    shared.update(consts)
    in_maps = []
    for b in range(B):
        m = dict(shared)
        m["x"] = np.ascontiguousarray(x[b])
        m["cT"] = np.ascontiguousarray(c[b].reshape(8, 128).T)
        m["pos"] = np.ascontiguousarray(np.tile(positions[b].reshape(16, 256), (8, 1)))
        in_maps.append(m)
    return in_maps


def kernel(x, c, positions, w_ada, b_ada, g_pre_mix, g_post_mix, g_pre_ffn, g_post_ffn,
           w_in, w_pool, b_pool, pool_scale, w_out, w_up, conv_w, conv_b, w_down):
    f32 = np.float32
    in_maps = _prep(x, c, positions, w_ada, b_ada, g_pre_mix, g_post_mix, g_pre_ffn, g_post_ffn,
                    w_in, w_pool, b_pool, pool_scale, w_out, w_up, conv_w, conv_b, w_down)
    B = len(in_maps)
    if "nc" not in _NC_CACHE:
        _NC_CACHE["nc"] = build()
    nc = _NC_CACHE["nc"]
    res = run_bass_kernel_spmd(nc, in_maps, core_ids=list(range(B)))
    return np.stack([np.asarray(r["out"], f32) for r in res.results], axis=0)
```

```python
import contextlib
import numpy as np
import ml_dtypes
import concourse.bass as bass
import concourse.mybir as mybir
from concourse.bass_utils import run_bass_kernel_spmd

F32 = mybir.dt.float32
BF16 = mybir.dt.bfloat16
I32 = mybir.dt.int32
AF = mybir.ActivationFunctionType
ALU = mybir.AluOpType
bf = ml_dtypes.bfloat16

S = 4096
D = 1024
NCH = 8
CH = 512
DFF = 2816
NJ = 22
EPS = 1e-6
ENG = ["pe", "act", "dve", "pool", "sp"]
C1 = 6.28125
C2 = float(2 * np.pi - 6.28125)
PI_LO = float(np.nextafter(np.float32(np.pi), np.float32(0)))
TWO_PI = float(2 * np.pi)
ATT_LIMIT = None
EMBED_WAIT = True
ATT_SUB = 9
ATT_HH = (0, 1)


class Buf:
    def __init__(self, name, init=()):
        self.name = name
        self.w = None
        self.r = list(init)


class Sched:
    def __init__(self):
        self.ops = {e: [] for e in ENG}
        self.cnt = {e: 0 for e in ENG}
        self.dma_i = {"sp": 0, "pool": 0}
        self.ndma = {"sp": 24, "pool": 12}
        self.dma_uses = {}

    def op(self, eng, fn, reads=(), writes=(), dma=False):
        if getattr(self, 'stopped', False):
            return None
        waits = set()
        for b in reads:
            if b.w is not None:
                waits.add(b.w)
        for b in writes:
            if b.w is not None:
                waits.add(b.w)
            waits.update(b.r)
        if dma:
            idx = self.dma_i[eng] % self.ndma[eng]
            self.dma_i[eng] += 1
            key = (eng, idx)
            prev = self.dma_uses.get(key, 0)
            self.dma_uses[key] = prev + 1
            tok = ("dma", eng, idx, 16 * (prev + 1))
            if prev > 0:
                waits.add(("dma", eng, idx, 16 * prev))
        else:
            self.cnt[eng] += 1
            tok = ("eng", eng, 0, self.cnt[eng])
        if eng == "pe":
            waits = {w for w in waits if not (w[0] == "eng" and w[1] == "pe")}
        self.ops[eng].append((fn, waits, tok, dma))
        for b in reads:
            b.r.append(tok)
        for b in writes:
            b.w = tok
            b.r = []
        return tok

    def all_tokens(self):
        toks = []
        for e in ENG:
            if self.cnt[e] > 0:
                toks.append(("eng", e, 0, self.cnt[e]))
        for (q, idx), n in self.dma_uses.items():
            toks.append(("dma", q, idx, 16 * n))
        return toks


class _Stop(Exception):
    pass


def build(stage=99):
    nc = bass.Bass("TRN2", target_bir_lowering=False)
    sc = Sched()
    dbg_bufs = []

    def dump(ap, bufs, row0, ncols, nparts=128):
        b = Buf("dbg")
        dbg_bufs.append(b)
        sc.op("pool", lambda e: e.dma_start(out=out_d[row0:row0 + nparts, 0:ncols], in_=ap), reads=list(bufs), writes=[b], dma=True)

    def checkpoint(k, dumps=()):
        if stage == k:
            for i, (ap, bufs, ncols, nparts) in enumerate(dumps):
                dump(ap, bufs, i * 128, ncols, nparts)
            sc.op("sp", None, reads=dbg_bufs)
            sc.stopped = True

    def din(name, shape, dt):
        return nc.dram_tensor(name, list(shape), dt, kind="ExternalInput").ap()

    x_d = din("x", [S, D], F32)
    cT_d = din("cT", [128, 8], F32)
    pos_d = din("pos", [128, 256], I32)
    wada_d = din("w_ada", [D, 6 * D], F32)
    bA_d = din("bA", [128, 4, 8], F32)
    bG_d = din("bG", [128, 2, D], F32)
    gpre_d = din("gpre", [128, 2, 8], F32)
    gpost_d = din("gpost", [128, 2, D], F32)
    win_d = din("w_in", [D, 2560], F32)
    wout_d = din("w_out", [512, D], F32)
    wup_d = din("w_up", [D, 2 * DFF], F32)
    wdn_d = din("w_down", [DFF, D], F32)
    wpool_d = din("w_pool", [4, 64, 64], F32)
    bpool_d = din("bpool", [128, 2], F32)
    pscale_d = din("pscale", [128, 2], F32)
    convw_d = din("convw", [128, NJ, 3], F32)
    convb_d = din("convb", [128, NJ], F32)
    ident_d = din("ident", [128, 128], BF16)
    swap_d = din("swapm", [128, 128], BF16)
    invf_d = din("invf", [128, 1], F32)
    mask_d = din("mask", [128, 2, 2, 128], BF16)
    poolc_d = din("poolc", [128, 2 + 32], F32)
    out_d = nc.dram_tensor("out", [S, D], F32, kind="ExternalOutput").ap()
    x1_d = nc.dram_tensor("x1s", [S, D], F32, kind="Internal").ap()
    tab_d = nc.dram_tensor("tabs", [2, 8, S], F32, kind="Internal").ap()

    stack = contextlib.ExitStack()

    def sb(name, shape, dt):
        return stack.enter_context(nc.sbuf_tensor("sb_" + name, list(shape), dt))

    def ps(name, shape, dt):
        return stack.enter_context(nc.psum_tensor(name, list(shape), dt))

    with stack:
        sem_eng = {e: stack.enter_context(nc.semaphore("s_" + e)) for e in ["pe", "act", "dve", "pool"]}
        sem_dma = {}
        for q in ["sp", "pool"]:
            for i in range(sc.ndma[q]):
                sem_dma[(q, i)] = stack.enter_context(nc.semaphore(f"d_{q}{i}"))

        psT = ps("psT", [128, 8, 128], BF16)
        psBig = ps("psBig", [128, 5, 512], F32)
        accbig = ps("accbig", [128, 2, 512], F32)
        psR = [psBig[:, i, :] for i in range(5)]
        acc = [accbig[:, 0, :], accbig[:, 1, :]]
        b_psT = [Buf("psT0"), Buf("psT1")]
        b_psR = [Buf(f"psR{i}") for i in range(5)]
        b_acc = [Buf("acc0"), Buf("acc1")]
        psY_a = (psBig[:, 3:5, :].rearrange("p a b -> p (a b)"), [b_psR[3], b_psR[4]])
        psY_b = (accbig[:, :, :].rearrange("p a b -> p (a b)"), [b_acc[0], b_acc[1]])
        psY_c = (psBig[:, 1:3, :].rearrange("p a b -> p (a b)"), [b_psR[1], b_psR[2]])
        ring = {"i": 0, "banks": [0, 1, 2, 3, 4]}

        def next_ring():
            bk = ring["banks"]
            i = bk[ring["i"] % len(bk)]
            ring["i"] += 1
            return psR[i], b_psR[i]

        ident = sb("ident", [128, 128], BF16)
        modv = sb("modv", [128, 4, 8], F32)
        Gt = sb("Gt", [128, 2, D], F32)
        ss_t = sb("ss_t", [128, 8], F32)
        rs_t = sb("rs_t", [128, 8], F32)
        ln_t = sb("ln_t", [128, 8], F32)
        b_ident, b_modv, b_Gt = Buf("ident"), Buf("modv"), Buf("Gt")
        sink = {}
        b_ss = [Buf(f"ss{i}") for i in range(8)]
        b_rs = [Buf(f"rs{i}") for i in range(8)]
        b_ln = [Buf(f"ln{i}") for i in range(8)]
        sc.op("sp", lambda e: e.dma_start(out=ident[:], in_=ident_d), writes=[b_ident], dma=True)

        epsb = sb("epsb", [128, 1], F32)
        b_eps = Buf("eps")
        sc.op("dve", lambda e: e.memset(epsb[:], EPS), writes=[b_eps])

        def rstd_ops(i):
            sc.op("act", lambda e: e.activation(out=ln_t[:, i:i + 1], in_=ss_t[:, i:i + 1], func=AF.Ln,
                                                bias=epsb[:, 0:1], scale=1.0 / D),
                  reads=[b_ss[i], b_eps], writes=[b_ln[i]])
            sc.op("act", lambda e: e.activation(out=rs_t[:, i:i + 1], in_=ln_t[:, i:i + 1], func=AF.Exp, scale=-0.5),
                  reads=[b_ln[i]], writes=[b_rs[i]])

        def sumsq(i, src_ap, src_bufs):
            sc.op("dve", lambda e: e.memset(ss_t[:, i:i + 1], 0.0), writes=[b_ss[i]])
            jk, bjk = sink["ap"], sink["buf"]
            sc.op("act", lambda e: e.activation(out=jk, in_=src_ap, func=AF.Square, accum_out=ss_t[:, i:i + 1]),
                  reads=list(src_bufs), writes=list(bjk) + [b_ss[i]])

        st2 = contextlib.ExitStack()

        def sb2(name, shape, dt):
            return st2.enter_context(nc.sbuf_tensor("s2_" + name, list(shape), dt))

        with st2:
            cTt = sb2("cTt", [128, 8], F32)
            cactf = sb2("cactf", [128, 8], F32)
            cact = sb2("cact", [128, 8], BF16)
            ones_t = sb2("ones_t", [128, 128], BF16)
            crep = sb2("crep", [128, 8, 128], BF16)
            bA = sb2("bA", [128, 4, 8], F32)
            gpre = sb2("gpre", [128, 2, 8], F32)
            bG = sb2("bG", [128, 2, D], F32)
            gpost = sb2("gpost", [128, 2, D], F32)
            wst = [sb2(f"wst{i}", [128, 8, D], BF16) for i in range(2)]
            invf = sb2("invf", [128, 1], F32)
            posi = sb2("posi", [128, 256], I32)
            ang = sb2("ang", [128, 256], F32)
            a2 = sb2("a2", [128, 256], F32)
            ki = sb2("ki", [128, 256], I32)
            kf = sb2("kf", [128, 256], F32)
            rr = sb2("rr", [128, 256], F32)
            tb = sb2("tb", [128, 256], F32)
            b_cT, b_cactf, b_cact, b_ones, b_crep = Buf("cT"), Buf("cactf"), Buf("cact"), Buf("ones"), Buf("crep")
            b_bA, b_gpre, b_bG, b_gpost, b_invf = Buf("bA"), Buf("gpre"), Buf("bG"), Buf("gpost"), Buf("invf")
            b_wst = [Buf("wst0"), Buf("wst1")]
            b_posi, b_ang, b_a2, b_ki, b_kf, b_rr, b_tb = (Buf(n) for n in ["posi", "ang", "a2", "ki", "kf", "rr", "tb"])
            b_tabd = [Buf("tabd0"), Buf("tabd1")]

            sc.op("sp", lambda e: e.dma_start(out=cTt[:], in_=cT_d), writes=[b_cT], dma=True)
            sc.op("sp", lambda e: e.dma_start(out=bA[:], in_=bA_d), writes=[b_bA], dma=True)
            sc.op("sp", lambda e: e.dma_start(out=gpre[:], in_=gpre_d), writes=[b_gpre], dma=True)
            sc.op("sp", lambda e: e.dma_start(out=invf[:], in_=invf_d), writes=[b_invf], dma=True)
            wada_v = wada_d.rearrange("(k p) n -> p k n", p=128)

            def load_seg(seg):
                t, bt = wst[seg % 2], b_wst[seg % 2]
                sc.op("pool", lambda e: e.dma_start(out=t[:], in_=wada_v[:, :, seg * D:(seg + 1) * D]),
                      writes=[bt], dma=True)

            load_seg(0)
            load_seg(1)
            sc.op("sp", lambda e: e.dma_start(out=bG[:], in_=bG_d), writes=[b_bG], dma=True)
            sc.op("sp", lambda e: e.dma_start(out=gpost[:], in_=gpost_d), writes=[b_gpost], dma=True)

            sc.op("act", lambda e: e.activation(out=cactf[:], in_=cTt[:], func=AF.Silu), reads=[b_cT], writes=[b_cactf])
            sc.op("dve", lambda e: e.tensor_copy(out=cact[:], in_=cactf[:]), reads=[b_cactf], writes=[b_cact])
            sc.op("dve", lambda e: e.memset(ones_t[:], 1.0), writes=[b_ones])
            for kc in range(8):
                sc.op("dve", lambda e, kc=kc: e.tensor_scalar(out=crep[:, kc, :], in0=ones_t[:], scalar1=cactf[:, kc:kc + 1],
                                                              scalar2=None, op0=ALU.mult),
                      reads=[b_ones, b_cactf], writes=[b_crep])

            psm = psR[0]
            segs_pp = {0: 0, 1: 1, 3: 2, 4: 3}

            def seg_pp(seg):
                t, bt = wst[seg % 2], b_wst[seg % 2]
                g = segs_pp[seg]

                def fn(e):
                    ins = None
                    for mg in range(8):
                        for kc in range(8):
                            ins = e.matmul(psm[:, g * 8 + mg:g * 8 + mg + 1], lhsT=t[:, kc, mg * 128:(mg + 1) * 128],
                                           rhs=cact[:, kc:kc + 1], start=(kc == 0), stop=(kc == 7))
                    return ins
                sc.op("pe", fn, reads=[bt, b_cact], writes=[b_psR[0]])

            def seg_gate(seg):
                t, bt = wst[seg % 2], b_wst[seg % 2]
                gi = 0 if seg == 2 else 1

                def fn(e):
                    ins = None
                    for nh in range(2):
                        for kc in range(8):
                            ins = e.matmul(psY_a[0][:, nh * 512:(nh + 1) * 512], lhsT=crep[:, kc, :],
                                           rhs=t[:, kc, nh * 512:(nh + 1) * 512], start=(kc == 0), stop=(kc == 7))
                    return ins
                sc.op("pe", fn, reads=[bt, b_crep], writes=psY_a[1])
                sc.op("dve", lambda e: e.tensor_tensor(out=Gt[:, gi, :], in0=psY_a[0], in1=bG[:, gi, :], op=ALU.add),
                      reads=psY_a[1] + [b_bG], writes=[b_Gt])
                sc.op("dve", lambda e: e.tensor_tensor(out=Gt[:, gi, :], in0=Gt[:, gi, :], in1=gpost[:, gi, :], op=ALU.mult),
                      reads=[b_gpost, b_Gt], writes=[b_Gt])

            seg_pp(0)
            load_seg(2)
            seg_pp(1)
            load_seg(3)
            seg_gate(2)
            load_seg(4)
            seg_pp(3)
            load_seg(5)
            seg_pp(4)
            seg_gate(5)
            sc.op("dve", lambda e: e.tensor_tensor(out=modv[:], in0=psm[:, 0:32].rearrange("p (a b) -> p a b", a=4),
                                                   in1=bA[:], op=ALU.add),
                  reads=[b_psR[0], b_bA], writes=[b_modv])
            for (a, gi) in ((1, 0), (3, 1)):
                sc.op("dve", lambda e, a=a, gi=gi: e.scalar_tensor_tensor(out=modv[:, a, :], in0=modv[:, a, :], scalar=1.0,
                                                                          in1=gpre[:, gi, :], op0=ALU.add, op1=ALU.mult),
                      reads=[b_modv, b_gpre], writes=[b_modv])

            for half in range(1):
                sc.op("sp", lambda e: e.dma_start(out=posi[:], in_=pos_d), writes=[b_posi], dma=True)
                sc.op("dve", lambda e: e.tensor_copy(out=ang[:], in_=posi[:]), reads=[b_posi], writes=[b_ang])
                sc.op("dve", lambda e: e.tensor_scalar(out=ang[:], in0=ang[:], scalar1=invf[:, 0:1], scalar2=None, op0=ALU.mult),
                      reads=[b_ang, b_invf], writes=[b_ang])
                for v in range(2):
                    off = float(np.pi / 2) if v == 0 else 0.0
                    sc.op("dve", lambda e, off=off: e.tensor_scalar(out=a2[:], in0=ang[:], scalar1=off, scalar2=None, op0=ALU.add),
                          reads=[b_ang], writes=[b_a2])
                    sc.op("dve", lambda e: e.tensor_scalar(out=ki[:], in0=a2[:], scalar1=float(1.0 / (2 * np.pi)), scalar2=None,
                                                           op0=ALU.mult), reads=[b_a2], writes=[b_ki])
                    sc.op("dve", lambda e: e.tensor_copy(out=kf[:], in_=ki[:]), reads=[b_ki], writes=[b_kf])
                    sc.op("dve", lambda e: e.scalar_tensor_tensor(out=rr[:], in0=kf[:], scalar=-C1, in1=a2[:], op0=ALU.mult,
                                                                  op1=ALU.add), reads=[b_kf, b_a2], writes=[b_rr])
                    sc.op("dve", lambda e: e.scalar_tensor_tensor(out=rr[:], in0=kf[:], scalar=-C2, in1=rr[:], op0=ALU.mult,
                                                                  op1=ALU.add), reads=[b_kf, b_rr], writes=[b_rr])
                    for (cmpop, thr, corr) in ((ALU.is_gt, PI_LO, -TWO_PI), (ALU.is_lt, -PI_LO, TWO_PI)):
                        sc.op("dve", lambda e, cmpop=cmpop, thr=thr, corr=corr: e.tensor_scalar(out=kf[:], in0=rr[:], scalar1=thr, scalar2=corr,
                                                                                              op0=cmpop, op1=ALU.mult),
                              reads=[b_rr], writes=[b_kf])
                        sc.op("dve", lambda e: e.tensor_tensor(out=rr[:], in0=rr[:], in1=kf[:], op=ALU.add), reads=[b_rr, b_kf], writes=[b_rr])
                    sc.op("dve", lambda e: e.tensor_scalar(out=rr[:], in0=rr[:], scalar1=-PI_LO, scalar2=PI_LO, op0=ALU.max, op1=ALU.min),
                          reads=[b_rr], writes=[b_rr])
                    sc.op("act", lambda e: e.activation(out=tb[:], in_=rr[:], func=AF.Sin), reads=[b_rr], writes=[b_tb])
                    sc.op("sp", lambda e, v=v: e.dma_start(out=tab_d[v].rearrange("f (t i) -> (f t) i", i=256), in_=tb[:]), reads=[b_tb],
                          writes=[b_tabd[v]], dma=True)
        checkpoint(1, [(modv[:].rearrange('p a b -> p (a b)'), [b_modv], 32, 128), (Gt[:, 0, :], [b_Gt], 1024, 128), (Gt[:, 1, :], [b_Gt], 1024, 128)])
        fence0 = sc.all_tokens()

        stM = contextlib.ExitStack()

        def sbM(name, shape, dt):
            return stM.enter_context(nc.sbuf_tensor("sm_" + name, list(shape), dt))

        with stM:
            def mb(name):
                return Buf(name, init=fence0)
            win = sbM("win", [128, 8, 2560], BF16)
            junk = sbM("junk", [128, D], BF16)
            sink["ap"], sink["buf"] = junk[:], [mb("junk")]
            wout_p = sbM("wout_p", [128, 2, D], BF16)
            wout_a = sbM("wout_a", [128, 4, D], BF16)
            wpbd = sbM("wpbd", [128, 2, 128], BF16)
            swapm = sbM("swapm", [128, 128], BF16)
            maskt = sbM("maskt", [128, 2, 2, 128], BF16)
            poolc = sbM("poolc", [128, 34], F32)
            bpool = sbM("bpool", [128, 2], F32)
            pscale = sbM("pscale", [128, 2], F32)
            KT01 = sbM("KT01", [128, 4, 1024], BF16)
            KT2 = sbM("KT2", [128, 2, S], BF16)
            NV = 48 * 256
            Vall = sbM("Vall", [128, 192 * 65], BF16)
            V3 = Vall[:, :].rearrange("p (s d) -> p s d", d=65)
            onesf = sbM("onesf", [65, 64], F32)
            QT = sbM("QT", [128, 2, 6, CH], BF16)
            xt = sbM("xt", [128, 2, D], F32)
            xn = sbM("xn", [128, 2, D], BF16)
            hT = sbM("hT", [128, 8, CH], BF16)
            tabC = sbM("tabC", [128, CH], F32)
            tabS = sbM("tabS", [128, CH], F32)
            ubuf = sbM("ubuf", [128, 2, 528], F32)
            s2b = sbM("s2b", [128, 2, 528], F32)
            s4b = sbM("s4b", [128, 2, 528], F32)
            s8b = sbM("s8b", [128, 528], F32)
            s16b = sbM("s16b", [128, 528], F32)
            mixed = sbM("mixed", [128, 2, CH], BF16)
            catP = sbM("catP", [128, 2, CH], BF16)
            qb = sbM("qb", [128, 2, CH], BF16)
            t1 = sbM("t1", [128, 2, CH], F32)
            t2 = sbM("t2", [128, 2, CH], F32)
            PT = sbM("PT", [128, 3, 1024], BF16)
            rden = sbM("rden", [65, 2, CH], F32)
            attT = sbM("attT", [128, 4, CH], BF16)
            xo = sbM("xo", [128, 2, D], F32)
            tmpy = sbM("tmpy", [128, D], F32)

            b_win = [mb(f"win{k}") for k in range(8)]
            b_woutp, b_wouta, b_wpbd, b_swap, b_mask = mb("woutp"), mb("wouta"), mb("wpbd"), mb("swap"), mb("mask")
            b_poolc, b_bpool, b_pscale = mb("poolc"), mb("bpool"), mb("pscale")
            b_KT01 = [mb("KT01_0"), mb("KT01_1")]
            b_KT2 = [mb(f"KT2_{c}") for c in range(8)]
            b_V01 = [mb("V01_0"), mb("V01_1")]
            b_V2 = [mb(f"V2_{c}") for c in range(8)]
            b_Vones = mb("Vones")
            b_QT = mb("QT")
            b_xt = [mb("xt0"), mb("xt1")]
            b_xn = [mb("xn0"), mb("xn1")]
            b_hT = mb("hT")
            b_tabC, b_tabS = mb("tabC"), mb("tabS")
            b_ubuf, b_s2, b_s4, b_s8, b_s16, b_mixed, b_catP = (mb(n) for n in ["ubuf", "s2", "s4", "s8", "s16", "mixed", "catP"])
            b_qb = [mb("qb0"), mb("qb1")]
            b_t1 = [mb("t1_0"), mb("t1_1")]
            b_t2 = [mb("t2_0"), mb("t2_1")]
            b_PT = [mb(f"PT{i}") for i in range(3)]
            b_rden = [mb("rden0"), mb("rden1")]
            b_rdrow = [mb("rdrow0"), mb("rdrow1")]
            b_attT = mb("attT")
            b_xo = [mb("xo0"), mb("xo1")]
            b_tmpy = mb("tmpy")
            b_x1d = [Buf(f"x1d{t}") for t in range(32)]

            win_v = win_d.rearrange("(k p) n -> p k n", p=128)
            for kc in range(8):
                sc.op("pool", lambda e, kc=kc: e.dma_start(out=win[:, kc, :], in_=win_v[:, kc, :]), writes=[b_win[kc]], dma=True)
            sc.op("sp", lambda e: e.dma_start(out=swapm[:], in_=swap_d), writes=[b_swap], dma=True)
            sc.op("sp", lambda e: e.dma_start(out=maskt[:], in_=mask_d), writes=[b_mask], dma=True)
            sc.op("sp", lambda e: e.dma_start(out=poolc[:], in_=poolc_d), writes=[b_poolc], dma=True)
            sc.op("sp", lambda e: e.dma_start(out=bpool[:], in_=bpool_d), writes=[b_bpool], dma=True)
            sc.op("sp", lambda e: e.dma_start(out=pscale[:], in_=pscale_d), writes=[b_pscale], dma=True)
            sc.op("dve", lambda e: e.memset(wpbd[:], 0.0), writes=[b_wpbd])
            for g in range(4):
                sc.op("pool", lambda e, g=g: e.dma_start(out=wpbd[(g % 2) * 64:(g % 2) * 64 + 64, g // 2, (g % 2) * 64:(g % 2) * 64 + 64],
                                                          in_=wpool_d[g]), writes=[b_wpbd], dma=True)
            sc.op("pool", lambda e: e.dma_start(out=wout_p[:], in_=wout_d[0:256, :].rearrange("(k p) n -> p k n", p=128)),
                  writes=[b_woutp], dma=True)
            sc.op("dve", lambda e: e.memset(wout_a[64:128, :, :], 0.0), writes=[b_wouta])
            sc.op("pool", lambda e: e.memset(attT[64:128, :, :], 0.0), writes=[b_attT])
            sc.op("pool", lambda e: e.dma_start(out=wout_a[0:64, :, :], in_=wout_d[256:512, :].rearrange("(k p) n -> p k n", p=64)),
                  writes=[b_wouta], dma=True)
            sc.op("dve", lambda e: e.memset(V3[:, :, 64:65], 1.0), writes=[b_Vones])
            b_onesf = mb("onesf")
            sc.op("dve", lambda e: e.memset(onesf[:], 1.0), writes=[b_onesf])
            sc.op("dve", lambda e: e.memset(ubuf[:, :, 0:16], 0.0), writes=[b_ubuf])
            sc.op("dve", lambda e: e.memset(tabC[:], 1.0), writes=[b_tabC])
            sc.op("dve", lambda e: e.memset(tabS[:], 0.0), writes=[b_tabS])
            sc.op("pool", lambda e: e.memset(QT[64:128, 0, :, :], 0.0), writes=[b_QT])
            sc.op("pool", lambda e: e.memset(QT[0:64, 1, :, :], 0.0), writes=[b_QT])

            def vslot_ap(slot, head, kparts):
                o = (slot * 4 + head) * 64
                return bass.AP(Vall.tensor if hasattr(Vall, "tensor") else Vall, o, [[NV + 64, kparts], [NV - o, 2], [1, 64]])

            def front(c):
                front_load(c, 0)
                front_load(c, 1)
                for tt in range(4):
                    front_norm(c, tt)
                    if tt + 2 < 4:
                        front_load(c, tt + 2)
                    front_trans(c, tt)

            def front_load(c, tt):
                t = c * 4 + tt
                r = t % 2
                sc.op("sp", lambda e, t=t, r=r: e.dma_start(out=xt[:, r, :], in_=x_d[t * 128:(t + 1) * 128, :]),
                      writes=[b_xt[r]], dma=True)

            def front_norm(c, tt):
                if True:
                    t = c * 4 + tt
                    r = t % 2
                    sumsq(tt, xt[:, r, :], [b_xt[r]])
                    rstd_ops(tt)
                    sc.op("dve", lambda e, r=r, tt=tt: e.tensor_scalar(out=xn[:, r, :], in0=xt[:, r, :], scalar1=rs_t[:, tt:tt + 1],
                                                                        scalar2=None, op0=ALU.mult),
                          reads=[b_xt[r], b_rs[tt]], writes=[b_xn[r]])

            def front_trans(c, tt):
                if True:
                    t = c * 4 + tt
                    r = t % 2

                    def fnT(e, r=r):
                        ins = None
                        for kc in range(8):
                            ins = e.transpose(psT[:, kc, :], xn[:, r, kc * 128:(kc + 1) * 128], ident[:])
                        return ins
                    sc.op("pe", fnT, reads=[b_xn[r], b_ident], writes=[b_psT[0], b_psT[1]])
                    for kc in range(8):
                        eng = "dve" if kc % 2 == 0 else "act"
                        if eng == "dve":
                            sc.op("dve", lambda e, kc=kc, tt=tt: e.tensor_scalar(
                                out=hT[:, kc, tt * 128:(tt + 1) * 128], in0=psT[:, kc, :], scalar1=modv[:, 1, kc:kc + 1],
                                scalar2=modv[:, 0, kc:kc + 1], op0=ALU.mult, op1=ALU.add),
                                reads=[b_psT[kc // 4], b_modv], writes=[b_hT])
                        else:
                            sc.op("act", lambda e, kc=kc, tt=tt: e.activation(
                                out=hT[:, kc, tt * 128:(tt + 1) * 128], in_=psT[:, kc, :], func=AF.Identity,
                                bias=modv[:, 0, kc:kc + 1], scale=modv[:, 1, kc:kc + 1]),
                                reads=[b_psT[kc // 4], b_modv], writes=[b_hT])

            def proj(c):
                par = c % 2
                n16, cc = c // 4, c % 4
                for (v, tt_, bt_) in ((0, tabC, b_tabC), (1, tabS, b_tabS)):
                    for p0 in (0, 8, 64, 72):
                        sc.op("sp", lambda e, v=v, tt_=tt_, p0=p0: e.dma_start(out=tt_[p0:p0 + 8, :], in_=tab_d[v, :, c * CH:(c + 1) * CH]),
                              reads=[b_tabd[v]], writes=[bt_], dma=True)

                def mm_fm(pt, col0):
                    def fn(e):
                        ins = None
                        for kc in range(8):
                            ins = e.matmul(pt[:], lhsT=win[:, kc, col0:col0 + 128], rhs=hT[:, kc, :], start=(kc == 0), stop=(kc == 7))
                        return ins
                    return fn
                for g in range(2):
                    for bi in range(4):
                        pt, bp = next_ring()
                        tok = slice(bi * 128, (bi + 1) * 128) if g == 0 else slice(bi, CH, 4)
                        col0 = 1792 + g * 256
                        slot = g * 8 + par * 4 + bi

                        def fn(e, pt=pt, tok=tok, col0=col0):
                            ins = None
                            for kc in range(8):
                                ins = e.matmul(pt[:, 0:256], lhsT=hT[:, kc, tok], rhs=win[:, kc, col0:col0 + 256],
                                               start=(kc == 0), stop=(kc == 7))
                            return ins
                        sc.op("pe", fn, reads=b_win + [b_hT], writes=[bp])
                        eng = "act" if bi % 2 == 0 else "dve"
                        if eng == "act":
                            sc.op("act", lambda e, pt=pt, slot=slot: e.activation(out=V3[:, slot * 4:(slot + 1) * 4, 0:64],
                                                                                   in_=pt[:, 0:256].rearrange("p (h d) -> p h d", h=4),
                                                                                   func=AF.Copy), reads=[bp], writes=[b_V01[par]])
                        else:
                            sc.op("dve", lambda e, pt=pt, slot=slot: e.tensor_copy(out=V3[:, slot * 4:(slot + 1) * 4, 0:64],
                                                                                    in_=pt[:, 0:256].rearrange("p (h d) -> p h d", h=4)),
                                  reads=[bp], writes=[b_V01[par]])
                for r0 in range(0, 16, 2):
                    pt, bp = next_ring()

                    def fn(e, pt=pt, r0=r0):
                        ins = None
                        for dr in range(2):
                            for kc in range(8):
                                ins = e.matmul(pt[cc * 32:(cc + 1) * 32, dr * 256:(dr + 1) * 256], lhsT=hT[:, kc, r0 + dr:CH:16],
                                               rhs=win[:, kc, 2304:2560], start=(kc == 0), stop=(kc == 7),
                                               tile_position=(0, cc * 32))
                        return ins
                    sc.op("pe", fn, reads=b_win + [b_hT], writes=[bp])
                    slot = 16 + n16 * 16 + r0
                    eng = "act" if (r0 // 2) % 2 == 0 else "dve"
                    if eng == "act":
                        sc.op("act", lambda e, pt=pt, slot=slot: e.activation(out=V3[cc * 32:(cc + 1) * 32, slot * 4:(slot + 2) * 4, 0:64],
                                                                               in_=pt[cc * 32:(cc + 1) * 32, :].rearrange("p (h d) -> p h d", h=8), func=AF.Copy),
                              reads=[bp], writes=[b_V2[c]])
                    else:
                        sc.op("dve", lambda e, pt=pt, slot=slot: e.tensor_copy(out=V3[cc * 32:(cc + 1) * 32, slot * 4:(slot + 2) * 4, 0:64],
                                                                                in_=pt[cc * 32:(cc + 1) * 32, :].rearrange("p (h d) -> p h d", h=8)),
                              reads=[bp], writes=[b_V2[c]])
                for ch in range(2):
                    pt, bp = next_ring()
                    sc.op("pe", mm_fm(pt, ch * 128), reads=b_win + [b_hT], writes=[bp])
                    sc.op("act", lambda e, pt=pt, ch=ch: e.activation(out=ubuf[:, ch, 16:528], in_=pt[:], func=AF.Copy),
                          reads=[bp], writes=[b_ubuf])
                qk_list = [(kind, p) for kind in range(2) for p in range(6)]
                pend = None

                def qk_stage2(kind, p, ri):
                    pt2, bp2 = next_ring()
                    sc.op("pe", lambda e, pt2=pt2, ri=ri: e.matmul(pt2[:], lhsT=swapm[:], rhs=qb[:, ri, :], start=True, stop=True),
                          reads=[b_swap, b_qb[ri]], writes=[bp2])
                    sc.op("dve", lambda e, pt2=pt2, ri=ri: e.tensor_tensor(out=t1[:, ri, :], in0=pt2[:], in1=tabS[:], op=ALU.mult),
                          reads=[bp2, b_tabS], writes=[b_t1[ri]])
                    sc.op("pool", lambda e, ri=ri: e.tensor_tensor(out=t2[:, ri, :], in0=qb[:, ri, :], in1=tabC[:], op=ALU.mult),
                          reads=[b_qb[ri], b_tabC], writes=[b_t2[ri]])
                    if kind == 0:
                        for hq in range(2):
                            prt = slice(hq * 64, hq * 64 + 64)
                            sc.op("dve", lambda e, ri=ri, prt=prt, hq=hq: e.tensor_tensor(out=QT[prt, hq, p, :], in0=t1[prt, ri, :],
                                                                                          in1=t2[prt, ri, :], op=ALU.add),
                                  reads=[b_t1[ri], b_t2[ri]], writes=[b_QT])
                        return
                    elif p < 4:
                        dst, bd = KT01[:, p, par * CH:(par + 1) * CH], [b_KT01[par]]
                    else:
                        dst, bd = KT2[:, p - 4, c * CH:(c + 1) * CH], [b_KT2[c]]
                    sc.op("dve", lambda e, ri=ri, dst=dst: e.tensor_tensor(out=dst, in0=t1[:, ri, :], in1=t2[:, ri, :], op=ALU.add),
                          reads=[b_t1[ri], b_t2[ri]], writes=bd)

                for qi, (kind, p) in enumerate(qk_list):
                    col0 = 256 + kind * 768 + p * 128
                    pt, bp = next_ring()
                    sc.op("pe", mm_fm(pt, col0), reads=b_win + [b_hT], writes=[bp])
                    ri = qi % 2
                    scl = 0.125 if kind == 0 else 1.0
                    sc.op("act", lambda e, pt=pt, ri=ri, scl=scl: e.activation(out=qb[:, ri, :], in_=pt[:], func=AF.Identity, scale=scl),
                          reads=[bp], writes=[b_qb[ri]])
                    if pend is not None:
                        qk_stage2(*pend)
                    pend = (kind, p, ri)
                qk_stage2(*pend)
                sc.op("pool", lambda e: e.tensor_tensor(out=s2b[:, :, 1:528], in0=ubuf[:, :, 1:528], in1=ubuf[:, :, 0:527], op=ALU.add),
                      reads=[b_ubuf], writes=[b_s2])
                sc.op("pool", lambda e: e.tensor_tensor(out=s4b[:, :, 3:528], in0=s2b[:, :, 3:528], in1=s2b[:, :, 1:526], op=ALU.add),
                      reads=[b_s2], writes=[b_s4])
                sc.op("pool", lambda e: e.tensor_tensor(out=s8b[:, 7:528], in0=s4b[:, 1, 7:528], in1=s4b[:, 1, 3:524], op=ALU.add),
                      reads=[b_s4], writes=[b_s8])
                sc.op("pool", lambda e: e.tensor_tensor(out=s16b[64:128, 15:528], in0=s8b[64:128, 15:528], in1=s8b[64:128, 7:520], op=ALU.add),
                      reads=[b_s8], writes=[b_s16])
                srcs = [(s2b, 0, slice(0, 64), b_s2, 0), (s4b, 0, slice(64, 128), b_s4, 0),
                        (s8b, None, slice(0, 64), b_s8, 1), (s16b, None, slice(64, 128), b_s16, 1)]
                for (stile, sidx, prt, bs, ch) in srcs:
                    sap = (stile[prt, sidx, 16:528] if sidx is not None else stile[prt, 16:528])
                    if c == 0:
                        sap16 = (stile[prt, sidx, 16:32] if sidx is not None else stile[prt, 16:32])
                        sc.op("pool", lambda e, sap16=sap16, prt=prt, ch=ch: e.tensor_tensor(out=sap16, in0=sap16,
                                                                                               in1=poolc[prt, 2 + ch * 16:2 + ch * 16 + 16], op=ALU.mult),
                              reads=[bs, b_poolc], writes=[bs])
                    sc.op("dve", lambda e, sap=sap, prt=prt, ch=ch: e.scalar_tensor_tensor(
                        out=mixed[prt, ch, :], in0=sap, scalar=poolc[prt, ch:ch + 1], in1=ubuf[prt, ch, 16:528],
                        op0=ALU.mult, op1=ALU.subtract), reads=[bs, b_poolc, b_ubuf], writes=[b_mixed])
                sc.op("pool", lambda e: e.tensor_copy(out=ubuf[:, :, 0:16], in_=ubuf[:, :, 512:528]), reads=[b_ubuf, b_s2, b_mixed], writes=[b_ubuf])
            def pool_mm(c):
                for ch in range(2):
                    pt, bp = next_ring()
                    sc.op("pe", lambda e, pt=pt, ch=ch: e.matmul(pt[:], lhsT=wpbd[:, ch, :], rhs=mixed[:, ch, :], start=True, stop=True),
                          reads=[b_wpbd, b_mixed], writes=[bp])
                    sc.op("dve", lambda e, pt=pt, ch=ch: e.tensor_scalar(out=catP[:, ch, :], in0=pt[:], scalar1=bpool[:, ch:ch + 1],
                                                                          scalar2=pscale[:, ch:ch + 1], op0=ALU.add, op1=ALU.mult),
                          reads=[bp, b_bpool, b_pscale], writes=[b_catP])

            pt_i = {"i": 0}

            def attn(c):
                par = c % 2
                n16, cc = c // 4, c % 4
                norm_pending = []
                for sp_ in range(2):
                    first = [True, True]
                    items = []
                    for tt in range(4):
                        n = c * 4 + tt
                        tiles = []
                        if n > 0:
                            pn = n - 1
                            tiles.append((0, KT01[:, sp_, (pn % 8) * 128:(pn % 8) * 128 + 128], (pn % 8), 128, [b_KT01[(pn // 4) % 2]], [b_V01[(pn // 4) % 2]]))
                        tiles.append((1, KT01[:, sp_, (n % 8) * 128:(n % 8) * 128 + 128], (n % 8), 128, [b_KT01[par]], [b_V01[par]]))
                        items.append(dict(q=(sp_, slice(tt * 128, (tt + 1) * 128)), nq=128, tiles=tiles, mcol=0,
                                          outsl=slice(tt * 128, (tt + 1) * 128)))
                    for r in range(4):
                        tiles = []
                        if c > 0:
                            pp = 1 - par
                            tiles.append((0, KT01[:, 2 + sp_, pp * CH + r:(pp + 1) * CH:4], 8 + pp * 4 + r, 128, [b_KT01[pp]], [b_V01[pp]]))
                        tiles.append((1, KT01[:, 2 + sp_, par * CH + r:(par + 1) * CH:4], 8 + par * 4 + r, 128, [b_KT01[par]], [b_V01[par]]))
                        items.append(dict(q=(2 + sp_, slice(r, CH, 4)), nq=128, tiles=tiles, mcol=0, outsl=slice(r, CH, 4)))
                    for r in range(16):
                        tiles = []
                        if n16 > 0:
                            tiles.append((0, KT2[:, sp_, r:2048:16], 16 + r, 128, b_KT2[0:4], b_V2[0:4]))
                        kp = (cc + 1) * 32
                        tiles.append((1, KT2[:, sp_, n16 * 2048 + r:n16 * 2048 + kp * 16:16], 16 + n16 * 16 + r, kp,
                                      b_KT2[n16 * 4:c + 1], b_V2[n16 * 4:c + 1]))
                        items.append(dict(q=(4 + sp_, slice(r, CH, 16)), nq=32, tiles=tiles, mcol=cc * 32, outsl=slice(r, CH, 16)))
                    if ATT_LIMIT is not None:
                        items = items[:ATT_LIMIT]
                    batches, cur, sig = [], [], None
                    for it in items:
                        isig = (it["nq"], tuple((tl[0], tl[3]) for tl in it["tiles"]), it["mcol"])
                        cap = 512 // (2 * it["nq"])
                        if cur and (isig != sig or len(cur) >= cap):
                            batches.append(cur)
                            cur = []
                        sig = isig
                        cur.append(it)
                    if cur:
                        batches.append(cur)

                    def stage_pv(ctx):
                        batch, pi, colof, nq = ctx
                        fl = [first[0], first[1]]
                        first[0] = first[1] = False
                        vb = set()
                        for it in batch:
                            for tl in it["tiles"]:
                                vb.update(tl[5])

                        def fnV(e, batch=batch, pi=pi, fl=fl, colof=colof, nq=nq, sp_=sp_):
                            ins = None
                            fl = list(fl)
                            for it in batch:
                                for tl in it["tiles"]:
                                    kidx, _, slot, kp, _, _ = tl
                                    for hh in range(2):
                                        col = hh * 512 + colof(kidx, it["ii"])
                                        ins = e.matmul(acc[hh][0:65, it["outsl"]], lhsT=V3[0:kp, slot * 4 + sp_ * 2 + hh, :],
                                                       rhs=PT[0:kp, pi, col:col + nq], start=fl[hh], stop=False, skip_group_check=True)
                                        fl[hh] = False
                            return ins
                        sc.op("pe", fnV, reads=[b_PT[pi], b_Vones] + list(vb), writes=[b_acc[0], b_acc[1]])

                    pend_pv = []
                    for batch in batches:
                        pts = [next_ring(), next_ring()]
                        pi = pt_i["i"] % 3
                        pt_i["i"] += 1
                        ni = len(batch)
                        nq = batch[0]["nq"]
                        mc = batch[0]["mcol"]
                        tl0 = batch[0]["tiles"]
                        kidxs = [tl[0] for tl in tl0]
                        kps = {tl[0]: tl[3] for tl in tl0}
                        kb = set()
                        for ii, it in enumerate(batch):
                            it["ii"] = ii
                            for tl in it["tiles"]:
                                kb.update(tl[4])

                        def colof(kidx, ii, ni=ni, nq=nq):
                            return kidx * (ni * nq) + ii * nq

                        def fnS(e, batch=batch, pts=pts, colof=colof, nq=nq):
                            ins = None
                            for hh in range(2):
                                pt = pts[hh][0]
                                for it in batch:
                                    for tl in it["tiles"]:
                                        kidx, kap, _, kp, _, _ = tl
                                        col = colof(kidx, it["ii"])
                                        ins = e.matmul(pt[0:kp, col:col + nq], lhsT=kap, rhs=QT[:, hh, it["q"][0], it["q"][1]],
                                                       start=True, stop=True)
                            return ins
                        sc.op("pe", fnS, reads=[b_QT] + list(kb), writes=[pts[0][1], pts[1][1]])
                        for hh in range(2):
                            pt, bp = pts[hh]
                            hb = hh * 512
                            if ni == 1 and len(kidxs) == 2:
                                sc.op("act", lambda e, pt=pt, pi=pi, hb=hb: e.activation(out=PT[:, pi, hb:hb + 256], in_=pt[:, 0:256], func=AF.Exp),
                                      reads=[bp], writes=[b_PT[pi]])
                                sc.op("dve", lambda e, pi=pi, hb=hb: e.tensor_tensor(
                                    out=PT[:, pi, hb:hb + 256].rearrange("p (k q) -> p k q", k=2),
                                    in0=PT[:, pi, hb:hb + 256].rearrange("p (k q) -> p k q", k=2),
                                    in1=maskt[:, :, 0, :], op=ALU.mult), reads=[b_PT[pi], b_mask], writes=[b_PT[pi]])
                            else:
                                for kidx in kidxs:
                                    kp = kps[kidx]
                                    base = kidx * ni * nq
                                    wdt = ni * nq
                                    sc.op("act", lambda e, pt=pt, pi=pi, kp=kp, base=base, wdt=wdt, hb=hb: e.activation(
                                        out=PT[0:kp, pi, hb + base:hb + base + wdt], in_=pt[0:kp, base:base + wdt], func=AF.Exp),
                                        reads=[bp], writes=[b_PT[pi]])
                                    if ni == 1:
                                        sc.op("dve", lambda e, pi=pi, kp=kp, base=base, wdt=wdt, kidx=kidx, mc=mc, nq=nq, hb=hb: e.tensor_tensor(
                                            out=PT[0:kp, pi, hb + base:hb + base + wdt], in0=PT[0:kp, pi, hb + base:hb + base + wdt],
                                            in1=maskt[0:kp, kidx, 0, mc:mc + nq], op=ALU.mult),
                                            reads=[b_PT[pi], b_mask], writes=[b_PT[pi]])
                                    else:
                                        mk = bass.AP(maskt, kidx * 256 + mc, [[512, kp], [0, ni], [1, nq]])
                                        sc.op("dve", lambda e, pi=pi, kp=kp, base=base, wdt=wdt, mk=mk, ni=ni, hb=hb: e.tensor_tensor(
                                            out=PT[0:kp, pi, hb + base:hb + base + wdt].rearrange("p (i q) -> p i q", i=ni),
                                            in0=PT[0:kp, pi, hb + base:hb + base + wdt].rearrange("p (i q) -> p i q", i=ni),
                                            in1=mk, op=ALU.mult), reads=[b_PT[pi], b_mask], writes=[b_PT[pi]])
                        pend_pv.append((batch, pi, colof, nq))
                        if norm_pending and len(pend_pv) >= 1:
                            norm_pending.pop(0)()
                        if len(pend_pv) >= 2:
                            stage_pv(pend_pv.pop(0))
                    while pend_pv:
                        stage_pv(pend_pv.pop(0))
                    for hh in range(2):
                        sc.op("act", lambda e, hh=hh: e.activation(out=rden[64:65, hh, :], in_=acc[hh][64:65, :], func=AF.Ln),
                              reads=[b_acc[hh]], writes=[b_rdrow[hh]])
                        sc.op("act", lambda e, hh=hh: e.activation(out=rden[64:65, hh, :], in_=rden[64:65, hh, :], func=AF.Exp, scale=-1.0),
                              reads=[b_rdrow[hh]], writes=[b_rdrow[hh]])

                    def norm_b(sp_=sp_):
                        for hh in range(2):
                            pt, bp = next_ring()
                            sc.op("pe", lambda e, hh=hh, pt=pt: e.matmul(pt[0:64, :], lhsT=onesf[64:65, 0:64], rhs=rden[64:65, hh, :],
                                                                          start=True, stop=True),
                                  reads=[b_rdrow[hh], b_onesf], writes=[bp])
                            sc.op("act", lambda e, hh=hh, pt=pt: e.activation(out=rden[0:64, hh, :], in_=pt[0:64, :], func=AF.Copy),
                                  reads=[bp], writes=[b_rden[hh]])
                            sc.op("dve", lambda e, hh=hh, sp_=sp_: e.tensor_tensor(out=attT[0:64, sp_ * 2 + hh, :], in0=acc[hh][0:64, :],
                                                                                    in1=rden[0:64, hh, :], op=ALU.mult),
                                  reads=[b_acc[hh], b_rden[hh]], writes=[b_attT])
                    norm_pending.append(norm_b)
                return norm_pending

            def _rows(ap, hh):
                return ap[hh * 64:(hh + 1) * 64]

            def epilogue(t, r, gi, x_src_d, x_src_bufs, dst_d, dst_buf, b_xo_, xo_, b_tmp, tmp_, ssi, psYp):
                psY, psYb = psYp
                sc.op("sp", lambda e: e.dma_start(out=xo_[:, r, :], in_=x_src_d[t * 128:(t + 1) * 128, :]),
                      reads=list(x_src_bufs), writes=[b_xo_[r]], dma=True)
                sumsq(ssi, psY, psYb)
                rstd_ops(ssi)
                sc.op("dve", lambda e: e.scalar_tensor_tensor(out=tmp_[:], in0=psY, scalar=rs_t[:, ssi:ssi + 1], in1=Gt[:, gi, :],
                                                              op0=ALU.mult, op1=ALU.mult),
                      reads=psYb + [b_rs[ssi], b_Gt], writes=[b_tmp])
                sc.op("pool", lambda e: e.tensor_tensor(out=xo_[:, r, :], in0=xo_[:, r, :], in1=tmp_[:], op=ALU.add),
                      reads=[b_tmp, b_xo_[r]], writes=[b_xo_[r]])
                sc.op("sp", lambda e: e.dma_start(out=dst_d[t * 128:(t + 1) * 128, :], in_=xo_[:, r, :]),
                      reads=[b_xo_[r]], writes=[dst_buf], dma=True)

            def outproj_pool(c, tt):
                tok = slice(tt * 128, (tt + 1) * 128)
                psYp = psY_a if tt % 2 == 0 else psY_c
                psY = psYp[0]

                def fn(e, tok=tok, psY=psY):
                    ins = None
                    for nh in range(2):
                        ns = slice(nh * 512, (nh + 1) * 512)
                        for ch in range(2):
                            ins = e.matmul(psY[:, ns], lhsT=catP[:, ch, tok], rhs=wout_p[:, ch, ns], start=(ch == 0), stop=False)
                    return ins
                sc.op("pe", fn, reads=[b_catP, b_woutp], writes=psYp[1])

            def outproj_attn(c, tt):
                t = c * 4 + tt
                tok = slice(tt * 128, (tt + 1) * 128)
                psYp = psY_a if tt % 2 == 0 else psY_c
                psY = psYp[0]

                def fn(e, tok=tok, psY=psY):
                    ins = None
                    for nh in range(2):
                        ns = slice(nh * 512, (nh + 1) * 512)
                        for s_ in range(4):
                            ins = e.matmul(psY[:, ns], lhsT=attT[:, s_, tok], rhs=wout_a[:, s_, ns], start=False, stop=(s_ == 3))
                    return ins
                sc.op("pe", fn, reads=[b_attT, b_wouta], writes=psYp[1])
                epilogue(t, t % 2, 0, x_d, [], x1_d, b_x1d[t], b_xo, xo, b_tmpy, tmpy, 4 + (tt % 4), psYp)

            front(0)
            checkpoint(2, [(hT[:, 0, :], [b_hT], 512, 128), (hT[:, 7, :], [b_hT], 512, 128)])
            win_addr = nc.lookup_mloc(win).addr
            b_wup_pre = [Buf(f"wupq{q}") for q in range(10)]
            wup_v = wup_d.rearrange("(k p) n -> p k n", p=128)
            winflat = win[:, :, :].rearrange("p k n -> p (k n)")

            def wup_cols(q):
                c0 = (q % 2) * DFF + (q // 2) * 256
                return slice(c0, c0 + 256)

            for c in range(NCH):
                proj(c)
                if c == NCH - 1:
                    for q in range(10):
                        dst = winflat[:, q * 2048:(q + 1) * 2048].rearrange("p (k n) -> p k n", k=8)
                        sc.op("pool", lambda e, q=q, dst=dst: e.dma_start(out=dst, in_=wup_v[:, :, wup_cols(q)]),
                              writes=b_win + [b_wup_pre[q]], dma=True)
                if c == 0:
                    checkpoint(3, [(QT[:, 0, 0, :], [b_QT], 512, 128), (KT01[:, 0, 0:512], [b_KT01[0]], 512, 128), (Vall[:, 0:1024], [b_V01[0]], 1024, 128),
                                   (catP[:, 0, :], [b_catP], 512, 128), (QT[:, 1, 5, :], [b_QT], 512, 128), (KT2[:, 1, 0:512], [b_KT2[0]], 512, 128),
                                   (Vall[0:32, 16 * 256:20 * 256], [b_V2[0]], 1024, 32), (catP[:, 1, :], [b_catP], 512, 128), (tabC[:], [b_tabC], 512, 128), (tabS[:], [b_tabS], 512, 128)])
                normp = attn(c)
                if c == 0:
                    checkpoint(4, [(attT[0:64, 0, :], [b_attT], 512, 64), (attT[0:64, 1, :], [b_attT], 512, 64), (attT[0:64, 2, :], [b_attT], 512, 64), (attT[0:64, 3, :], [b_attT], 512, 64)])
                pool_mm(c)
                outproj_pool(c, 0)
                ring["banks"] = [0]
                while normp:
                    normp.pop(0)()
                outproj_pool(c, 1)
                if c + 1 < NCH:
                    front_load(c + 1, 0)
                    front_load(c + 1, 1)
                    front_norm(c + 1, 0)
                    front_load(c + 1, 2)
                for tt in range(4):
                    if c + 1 < NCH and tt + 1 < 4:
                        front_norm(c + 1, tt + 1)
                        if tt + 3 < 4:
                            front_load(c + 1, tt + 3)
                    if c + 1 < NCH:
                        front_trans(c + 1, tt)
                    if tt >= 2:
                        outproj_pool(c, tt)
                    outproj_attn(c, tt)
                ring["banks"] = [0, 1, 2, 3, 4]
        fence1 = sc.all_tokens()

        stF = contextlib.ExitStack()

        def sbF(name, shape, dt):
            return stF.enter_context(nc.sbuf_tensor("sf_" + name, list(shape), dt))

        with stF:
            def fb(name):
                return Buf(name, init=fence1)
            wup = sbF("wup", [128, 22, 8, 256], BF16)
            assert nc.lookup_mloc(wup).addr == win_addr, "wup must alias the w_in region for the early prefetch"
            wdn = sbF("wdn", [128, NJ, D], BF16)
            convw = sbF("convw", [128, NJ, 3], F32)
            convb = sbF("convb", [128, NJ], F32)
            xf = sbF("xf", [128, 2, D], F32)
            xnf = sbF("xnf", [128, 2, D], BF16)
            h2T = sbF("h2T", [128, 8, CH], BF16)
            aT = sbF("aT", [128, NJ, CH], BF16)
            gS = sbF("gS", [128, 2, 514], F32)
            tcv = sbF("tcv", [128, 2, CH], F32)
            ge = sbF("ge", [128, 2, CH], BF16)
            vS = sbF("vS", [128, 2, CH], BF16)
            halo = sbF("halo", [128, NJ, 2], F32)
            xo2 = sbF("xo2", [128, 2, D], F32)
            tmp2 = sbF("tmp2", [128, D], F32)
            b_wup = [b_wup_pre[q] if q < 10 else fb(f"wup{q}") for q in range(22)]
            b_wdn = [fb(f"wdn{i}") for i in range(11)]
            b_convw, b_convb = fb("convw"), fb("convb")
            b_xf = [fb("xf0"), fb("xf1")]
            b_xnf, b_h2T, b_aT = [fb("xnf0"), fb("xnf1")], fb("h2T"), fb("aT")
            b_gS = [fb("gS0"), fb("gS1")]
            b_tcv = [fb("tcv0"), fb("tcv1")]
            b_ge = [fb("ge0"), fb("ge1")]
            sink["ap"], sink["buf"] = ge[:, :, :].rearrange("p a b -> p (a b)"), b_ge
            b_vS = [fb("vS0"), fb("vS1")]
            b_halo = [fb(f"halo{j}") for j in range(NJ)]
            b_xo2 = [fb("xo2_0"), fb("xo2_1")]
            b_tmp2 = fb("tmp2")
            b_outd = [Buf(f"outd{t}") for t in range(32)]

            wdn_v = wdn_d.rearrange("(j p) n -> p j n", p=128)
            sc.op("sp", lambda e: e.dma_start(out=convw[:], in_=convw_d), writes=[b_convw], dma=True)
            sc.op("sp", lambda e: e.dma_start(out=convb[:], in_=convb_d), writes=[b_convb], dma=True)
            sc.op("dve", lambda e: e.memset(halo[:], 0.0), writes=b_halo)
            for i in range(11):
                for q in (2 * i, 2 * i + 1):
                    if q >= 10:
                        sc.op("pool", lambda e, q=q: e.dma_start(out=wup[:, q, :, :], in_=wup_v[:, :, wup_cols(q)]),
                              writes=[b_wup[q]], dma=True)
            for i in range(11):
                sc.op("pool", lambda e, i=i: e.dma_start(out=wdn[:, 2 * i:2 * i + 2, :], in_=wdn_v[:, 2 * i:2 * i + 2, :]),
                      writes=[b_wdn[i]], dma=True)

            def frontF(c):
                frontF_load(c, 0)
                frontF_load(c, 1)
                for tt in range(4):
                    frontF_norm(c, tt)
                    if tt + 2 < 4:
                        frontF_load(c, tt + 2)
                    frontF_trans(c, tt)

            def frontF_load(c, tt):
                t = c * 4 + tt
                r = t % 2
                sc.op("sp", lambda e, t=t, r=r: e.dma_start(out=xf[:, r, :], in_=x1_d[t * 128:(t + 1) * 128, :]),
                      reads=[b_x1d[t]], writes=[b_xf[r]], dma=True)

            def frontF_norm(c, tt):
                if True:
                    t = c * 4 + tt
                    r = t % 2
                    sumsq(tt, xf[:, r, :], [b_xf[r]])
                    rstd_ops(tt)
                    sc.op("dve", lambda e, r=r, tt=tt: e.tensor_scalar(out=xnf[:, r, :], in0=xf[:, r, :], scalar1=rs_t[:, tt:tt + 1],
                                                                        scalar2=None, op0=ALU.mult),
                          reads=[b_xf[r], b_rs[tt]], writes=[b_xnf[r]])

            def frontF_trans(c, tt):
                if True:
                    r = (c * 4 + tt) % 2

                    def fnT(e, r=r):
                        ins = None
                        for kc in range(8):
                            ins = e.transpose(psT[:, kc, :], xnf[:, r, kc * 128:(kc + 1) * 128], ident[:])
                        return ins
                    sc.op("pe", fnT, reads=[b_xnf[r], b_ident], writes=[b_psT[0], b_psT[1]])
                    for kc in range(8):
                        if kc % 2 == 0:
                            sc.op("dve", lambda e, kc=kc, tt=tt: e.tensor_scalar(
                                out=h2T[:, kc, tt * 128:(tt + 1) * 128], in0=psT[:, kc, :], scalar1=modv[:, 3, kc:kc + 1],
                                scalar2=modv[:, 2, kc:kc + 1], op0=ALU.mult, op1=ALU.add),
                                reads=[b_psT[kc // 4], b_modv], writes=[b_h2T])
                        else:
                            sc.op("act", lambda e, kc=kc, tt=tt: e.activation(
                                out=h2T[:, kc, tt * 128:(tt + 1) * 128], in_=psT[:, kc, :], func=AF.Identity,
                                bias=modv[:, 2, kc:kc + 1], scale=modv[:, 3, kc:kc + 1]),
                                reads=[b_psT[kc // 4], b_modv], writes=[b_h2T])

            def up(c):
                for j in range(NJ):
                    up_s1(j)
                    if j > 0:
                        up_s2(j - 1)
                up_s2(NJ - 1)

            def up_s1(j):
                if True:
                    ri = j % 2
                    ptg, bpg = next_ring()

                    def fng(e, ptg=ptg, j=j):
                        ins = None
                        for kc in range(8):
                            ins = e.matmul(ptg[:], lhsT=wup[:, 2 * (j // 2), kc, (j % 2) * 128:(j % 2) * 128 + 128], rhs=h2T[:, kc, :],
                                           start=(kc == 0), stop=(kc == 7))
                        return ins
                    sc.op("pe", fng, reads=[b_wup[2 * (j // 2)], b_h2T], writes=[bpg])
                    ptv, bpv = next_ring()

                    def fnv(e, ptv=ptv, j=j):
                        ins = None
                        for kc in range(8):
                            ins = e.matmul(ptv[:], lhsT=wup[:, 2 * (j // 2) + 1, kc, (j % 2) * 128:(j % 2) * 128 + 128], rhs=h2T[:, kc, :],
                                           start=(kc == 0), stop=(kc == 7))
                        return ins
                    sc.op("pe", fnv, reads=[b_wup[2 * (j // 2) + 1], b_h2T], writes=[bpv])
                    sc.op("pool", lambda e, ri=ri, j=j: e.tensor_copy(out=gS[:, ri, 0:2], in_=halo[:, j, :]),
                          reads=[b_halo[j]], writes=[b_gS[ri]])
                    sc.op("act", lambda e, ri=ri, ptg=ptg: e.activation(out=gS[:, ri, 2:514], in_=ptg[:], func=AF.Copy),
                          reads=[bpg], writes=[b_gS[ri]])
                    sc.op("act", lambda e, ri=ri, ptv=ptv: e.activation(out=vS[:, ri, :], in_=ptv[:], func=AF.Copy),
                          reads=[bpv], writes=[b_vS[ri]])
                    sc.op("pool", lambda e, ri=ri, j=j: e.tensor_copy(out=halo[:, j, :], in_=gS[:, ri, 512:514]),
                          reads=[b_gS[ri]], writes=[b_halo[j]])
                    sc.op("dve", lambda e, ri=ri, j=j: e.tensor_scalar(out=tcv[:, ri, :], in0=gS[:, ri, 2:514], scalar1=convw[:, j, 2:3],
                                                                        scalar2=convb[:, j:j + 1], op0=ALU.mult, op1=ALU.add),
                          reads=[b_gS[ri], b_convw, b_convb], writes=[b_tcv[ri]])
                    sc.op("dve", lambda e, ri=ri, j=j: e.scalar_tensor_tensor(out=tcv[:, ri, :], in0=gS[:, ri, 1:513], scalar=convw[:, j, 1:2],
                                                                                in1=tcv[:, ri, :], op0=ALU.mult, op1=ALU.add),
                          reads=[b_gS[ri], b_convw, b_tcv[ri]], writes=[b_tcv[ri]])
                    sc.op("dve", lambda e, ri=ri, j=j: e.scalar_tensor_tensor(out=tcv[:, ri, :], in0=gS[:, ri, 0:512], scalar=convw[:, j, 0:1],
                                                                               in1=tcv[:, ri, :], op0=ALU.mult, op1=ALU.add),
                          reads=[b_gS[ri], b_convw, b_tcv[ri]], writes=[b_tcv[ri]])
            def up_s2(j):
                if True:
                    ri = j % 2
                    sc.op("act", lambda e, ri=ri: e.activation(out=ge[:, ri, :], in_=tcv[:, ri, :], func=AF.Gelu_apprx_tanh),
                          reads=[b_tcv[ri]], writes=[b_ge[ri]])
                    sc.op("dve", lambda e, ri=ri, j=j: e.tensor_tensor(out=aT[:, j, :], in0=vS[:, ri, :], in1=ge[:, ri, :], op=ALU.mult),
                          reads=[b_vS[ri], b_ge[ri]], writes=[b_aT])

            def down_tile(c, tt):
                if True:
                    t = c * 4 + tt
                    tok = slice(tt * 128, (tt + 1) * 128)

                    psYp = psY_a if t % 2 == 0 else psY_b
                    psY = psYp[0]

                    def fn(e, tok=tok, psY=psY):
                        ins = None
                        for nh in range(2):
                            ns = slice(nh * 512, (nh + 1) * 512)
                            for j in range(NJ):
                                ins = e.matmul(psY[:, ns], lhsT=aT[:, j, tok], rhs=wdn[:, j, ns], start=(j == 0), stop=(j == NJ - 1))
                        return ins
                    sc.op("pe", fn, reads=[b_aT] + b_wdn, writes=psYp[1])
                    epilogue(t, t % 2, 1, x1_d, [b_x1d[t]], out_d, b_outd[t], b_xo2, xo2, b_tmp2, tmp2, 4 + (tt % 4), psYp)

            ring["banks"] = [0, 1, 2]
            frontF(0)
            for c in range(NCH):
                up(c)
                if c + 1 < NCH:
                    frontF_load(c + 1, 0)
                    frontF_load(c + 1, 1)
                    frontF_norm(c + 1, 0)
                    frontF_load(c + 1, 2)
                for tt in range(4):
                    if c + 1 < NCH and tt + 1 < 4:
                        frontF_norm(c + 1, tt + 1)
                        if tt + 3 < 4:
                            frontF_load(c + 1, tt + 3)
                    if c + 1 < NCH:
                        frontF_trans(c + 1, tt)
                    down_tile(c, tt)
            sc.op("sp", None, reads=b_outd)

            emit(nc, sc, sem_eng, sem_dma)
    return nc


def emit(nc, sc, sem_eng, sem_dma):
    class _FirstWait:
        def __init__(self, eng, pending):
            self._eng = eng
            self._pending = pending

        def __getattr__(self, name):
            f = getattr(self._eng, name)

            def g(*a, **k):
                ins = f(*a, **k)
                if self._pending:
                    sem, val = self._pending.pop()
                    ins._wait_ge(sem, val)
                return ins
            return g

    def run_engine(ename, eng):
        seen = {}
        for (fn, waits, tok, dma) in sc.ops[ename]:
            todo = []
            for w in sorted(waits):
                key = (w[0], w[1], w[2])
                if seen.get(key, 0) >= w[3]:
                    continue
                seen[key] = w[3]
                sem = sem_eng[w[1]] if w[0] == "eng" else sem_dma[(w[1], w[2])]
                todo.append((sem, w[3]))
            embed = EMBED_WAIT and fn is not None and not dma and len(todo) > 0
            for (sem, val) in (todo[:-1] if embed else todo):
                eng.wait_ge(sem, val)
            if fn is None:
                continue
            if embed:
                pend = [todo[-1]]
                ins = fn(_FirstWait(eng, pend))
                assert not pend
            else:
                ins = fn(eng)
            if dma:
                ins.then_inc(sem_dma[(tok[1], tok[2])], 16)
            else:
                ins.then_inc(sem_eng[ename], 1)

    with nc.Block() as block:
        @block.sync
        def _(e):
            run_engine("sp", e)

        @block.tensor
        def _(e):
            run_engine("pe", e)

        @block.scalar
        def _(e):
            run_engine("act", e)

        @block.vector
        def _(e):
            run_engine("dve", e)

        @block.gpsimd
        def _(e):
            run_engine("pool", e)


def _consts():
    ident = np.eye(128, dtype=np.float32).astype(bf)
    sw = np.zeros((128, 128), np.float32)
    for hb in (0, 64):
        for d in range(8):
            sw[hb + d + 8, hb + d] = -1.0
            sw[hb + d, hb + d + 8] = 1.0
    inv_freq = (500000.0 ** (-np.arange(0, 16, 2, dtype=np.float32) / 16.0)).astype(np.float32)
    invf = np.zeros((128, 1), np.float32)
    for p in range(128):
        invf[p, 0] = inv_freq[p // 16]
    jj = np.arange(128)[:, None]
    ii = np.arange(128)[None, :]
    m_prev = (jj >= ii).astype(np.float32)
    m_cur = (jj <= ii).astype(np.float32)
    mask = np.zeros((128, 2, 2, 128), np.float32)
    for h in range(2):
        mask[:, 0, h, :] = m_prev
        mask[:, 1, h, :] = m_cur
    poolc = np.zeros((128, 34), np.float32)
    wins = {(0, 0): 2, (0, 1): 4, (1, 0): 8, (1, 1): 16}
    for ch in range(2):
        for half in range(2):
            w = wins[(ch, half)]
            prt = slice(half * 64, half * 64 + 64)
            poolc[prt, ch] = 1.0 / w
            for t in range(16):
                poolc[prt, 2 + ch * 16 + t] = w / min(t + 1, w)
    return dict(ident=ident, swapm=sw.astype(bf), invf=invf, mask=mask.astype(bf), poolc=poolc)


_NC_CACHE = {}


def _prep(x, c, positions, w_ada, b_ada, g_pre_mix, g_post_mix, g_pre_ffn, g_post_ffn,
          w_in, w_pool, b_pool, pool_scale, w_out, w_up, conv_w, conv_b, w_down):
    f32 = np.float32
    x = np.asarray(x, f32)
    c = np.asarray(c, f32)
    positions = np.asarray(positions, np.int32)
    w_ada = np.ascontiguousarray(np.asarray(w_ada, f32)[0])
    b_ada = np.asarray(b_ada, f32)[0]
    B = x.shape[0]
    consts = _consts()

    def pp(v):
        return np.ascontiguousarray(v.reshape(8, 128).T)
    bA = np.stack([pp(b_ada[0:D]), pp(b_ada[D:2 * D]), pp(b_ada[3 * D:4 * D]), pp(b_ada[4 * D:5 * D])], axis=1)
    bG = np.stack([np.broadcast_to(b_ada[2 * D:3 * D], (128, D)), np.broadcast_to(b_ada[5 * D:6 * D], (128, D))], axis=1)
    gpre = np.stack([pp(np.asarray(g_pre_mix, f32)[0]), pp(np.asarray(g_pre_ffn, f32)[0])], axis=1)
    gpost = np.stack([np.broadcast_to(np.asarray(g_post_mix, f32)[0], (128, D)),
                      np.broadcast_to(np.asarray(g_post_ffn, f32)[0], (128, D))], axis=1)
    bpool = np.ascontiguousarray(np.asarray(b_pool, f32)[0].reshape(2, 128).T)
    pscale = np.ascontiguousarray(np.asarray(pool_scale, f32)[0].reshape(2, 128).T)
    convw = np.ascontiguousarray(np.asarray(conv_w, f32)[0].T.reshape(NJ, 128, 3).transpose(1, 0, 2))
    convb = np.ascontiguousarray(np.asarray(conv_b, f32)[0].reshape(NJ, 128).T)
    shared = {
        "w_ada": w_ada, "bA": np.ascontiguousarray(bA), "bG": np.ascontiguousarray(bG), "gpre": np.ascontiguousarray(gpre),
        "gpost": np.ascontiguousarray(gpost), "w_in": np.ascontiguousarray(np.asarray(w_in, f32)[0]),
        "w_out": np.ascontiguousarray(np.asarray(w_out, f32)[0]), "w_up": np.ascontiguousarray(np.asarray(w_up, f32)[0]),
        "w_down": np.ascontiguousarray(np.asarray(w_down, f32)[0]), "w_pool": np.ascontiguousarray(np.asarray(w_pool, f32)[0]),
        "bpool": bpool, "pscale": pscale, "convw": convw, "convb": convb,
    }
    shared.update(consts)
    in_maps = []
    for b in range(B):
        m = dict(shared)
        m["x"] = np.ascontiguousarray(x[b])
        m["cT"] = np.ascontiguousarray(c[b].reshape(8, 128).T)
        m["pos"] = np.ascontiguousarray(np.tile(positions[b].reshape(16, 256), (8, 1)))
        in_maps.append(m)
    return in_maps


def kernel(x, c, positions, w_ada, b_ada, g_pre_mix, g_post_mix, g_pre_ffn, g_post_ffn,
           w_in, w_pool, b_pool, pool_scale, w_out, w_up, conv_w, conv_b, w_down):
    f32 = np.float32
    in_maps = _prep(x, c, positions, w_ada, b_ada, g_pre_mix, g_post_mix, g_pre_ffn, g_post_ffn,
                    w_in, w_pool, b_pool, pool_scale, w_out, w_up, conv_w, conv_b, w_down)
    B = len(in_maps)
    if "nc" not in _NC_CACHE:
        _NC_CACHE["nc"] = build()
    nc = _NC_CACHE["nc"]
    res = run_bass_kernel_spmd(nc, in_maps, core_ids=list(range(B)))
    return np.stack([np.asarray(r["out"], f32) for r in res.results], axis=0)
```
